# Optimizing a Trainium2 kernel written in Bass

```python
import jax, jax.numpy as jnp
from jax import lax
import numpy as np

D_MODEL = 1024
BATCH = 32
SEQ = 256
DEPTH = 2
DEC_BATCH = 8
DEC_SEQ = 1024
PAST_LEN = 512

GRID_W = 64
HEAD_DIM = 64
N_HEADS = D_MODEL // 128
N_KV_HEADS = N_HEADS // 4
ATT_Q = N_HEADS * HEAD_DIM
ATT_KV = N_KV_HEADS * HEAD_DIM
RET_HEADS = 4
RET_DK = D_MODEL // 16
RET_DV = 2 * RET_DK
RET_QK = RET_HEADS * RET_DK
RET_V = RET_HEADS * RET_DV
FOURIER_GROUPS = 4
FOURIER_GROUP_DIM = D_MODEL // 8
FOURIER_W = FOURIER_GROUPS * FOURIER_GROUP_DIM
N_BRANCH = 3
D_IN = ATT_Q + 2 * ATT_KV + 2 * RET_QK + 2 * RET_V + FOURIER_W + N_BRANCH * D_MODEL
D_FF = ((8 * D_MODEL // 3 + 127) // 128) * 128
CONV_W = 3
CHUNK = 128
Q_BLOCK = 128
ROPE_THETA = 10000.0
EPS = 1e-6

kernel_name = 'hybrid_prefix_diffusion_step'


def rmsnorm(x, g):
    xf = x.astype(jnp.float32)
    y = xf * lax.rsqrt(jnp.mean(xf * xf, axis=-1, keepdims=True) + EPS)
    return y.astype(x.dtype) * g


def axial_rope_tables(n_tok):
    rows = n_tok // GRID_W
    row_id = jnp.repeat(jnp.arange(rows), GRID_W).astype(jnp.float32)
    col_id = jnp.tile(jnp.arange(GRID_W), rows).astype(jnp.float32)
    n_freq = HEAD_DIM // 4
    inv = ROPE_THETA ** (-jnp.arange(n_freq, dtype=jnp.float32) / n_freq)
    ang = jnp.stack([row_id[:, None] * inv, col_id[:, None] * inv], axis=1)
    return jnp.cos(ang), jnp.sin(ang)


def apply_rope(x, cos, sin):
    b, n, h, hd = x.shape
    xr = x.reshape(b, n, h, 2, 2, hd // 4)
    x1, x2 = xr[..., 0, :], xr[..., 1, :]
    c = cos[None, :, None].astype(x.dtype)
    s = sin[None, :, None].astype(x.dtype)
    out = jnp.stack([x1 * c - x2 * s, x2 * c + x1 * s], axis=-2)
    return out.reshape(b, n, h, hd)


def blocked_gqa(q, k, v):
    b, lq, h, hd = q.shape
    g = h // N_KV_HEADS
    nb = lq // Q_BLOCK
    qb = q.reshape(b, nb, Q_BLOCK, N_KV_HEADS, g, hd).transpose(1, 0, 2, 3, 4, 5)
    scale = HEAD_DIM ** -0.5

    def attend(qblk):
        s = jnp.einsum('bqkgd,bskd->bkgqs', qblk, k).astype(jnp.float32) * scale
        p = jax.nn.softmax(s, axis=-1).astype(v.dtype)
        return jnp.einsum('bkgqs,bskd->bqkgd', p, v)

    o = lax.map(attend, qb)
    return o.transpose(1, 0, 2, 3, 4, 5).reshape(b, lq, h * hd)


def retention_chunked(q, k, v, log_gamma, s0):
    b, n_tok, h, _ = q.shape
    dv = v.shape[-1]
    n = n_tok // CHUNK

    def chunks(t):
        return t.reshape(b, n, CHUNK, h, t.shape[-1]).transpose(1, 0, 2, 3, 4)

    idx = jnp.arange(CHUNK, dtype=jnp.float32)
    rel = idx[:, None] - idx[None, :]
    intra = jnp.where(rel >= 0, jnp.exp(jnp.maximum(rel, 0.0)[None] * log_gamma[:, None, None]), 0.0)
    q_dec = jnp.exp((idx[:, None] + 1.0) * log_gamma[None, :])
    k_dec = jnp.exp((CHUNK - 1.0 - idx)[:, None] * log_gamma[None, :])
    c_dec = jnp.exp(CHUNK * log_gamma)

    def step(state, inp):
        qc, kc, vc = inp
        att = jnp.einsum('bihd,bjhd->bhij', qc, kc) * intra
        o = (jnp.einsum('bhij,bjhe->bihe', att, vc)
             + jnp.einsum('bihd,bhde->bihe', qc, state) * q_dec[None, :, :, None])
        state = (state * c_dec[None, :, None, None]
                 + jnp.einsum('bjhd,bjhe->bhde', kc * k_dec[None, :, :, None], vc))
        return state, o

    s_fin, o = lax.scan(step, s0, (chunks(q), chunks(k), chunks(v)))
    return o.transpose(1, 0, 2, 3, 4).reshape(b, n_tok, h, dv), s_fin


def bidir_retention(q, k, v, lg_f, lg_b, s0_f, s0_b):
    o_f, s_f = retention_chunked(q, k, v, lg_f, s0_f)
    o_b, s_b = retention_chunked(jnp.flip(q, 1), jnp.flip(k, 1), jnp.flip(v, 1), lg_b, s0_b)
    return o_f + jnp.flip(o_b, 1), s_f, s_b


def dwconv3(a, w, bias):
    n_tok = a.shape[1]
    ap = jnp.pad(a, ((0, 0), (1, 1), (0, 0)))
    return ap[:, :n_tok] * w[0] + ap[:, 1:n_tok + 1] * w[1] + ap[:, 2:n_tok + 2] * w[2] + bias


def trunk_layer(x, cond, rope, ctx, w_ada, b_ada, norm1, w_in, q_norm, k_norm,
                ret_decay_f, ret_decay_b, ret_norm, w_br_att, w_br_ret, w_br_four,
                w_out, norm2, w_up, conv_w, conv_b, w_down):
    bsz, n_tok, _ = x.shape
    mod = (jax.nn.silu(cond) @ w_ada + b_ada)[:, None, :]
    sh1, sc1, g1, sh2, sc2, g2 = jnp.split(mod, 6, axis=-1)

    h = rmsnorm(x, norm1) * (1 + sc1) + sh1
    splits = np.cumsum([ATT_Q, ATT_KV, ATT_KV, RET_QK, RET_QK, RET_V, RET_V, FOURIER_W]).tolist()
    q_a, k_a, v_a, q_r, k_r, v_r, g_r, u_f, gate_logits = jnp.split(h @ w_in, splits, axis=-1)

    q_a = rmsnorm(q_a.reshape(bsz, n_tok, N_HEADS, HEAD_DIM), q_norm)
    k_a = rmsnorm(k_a.reshape(bsz, n_tok, N_KV_HEADS, HEAD_DIM), k_norm)
    v_a = v_a.reshape(bsz, n_tok, N_KV_HEADS, HEAD_DIM)
    if ctx is None:
        keys, vals = k_a, v_a
        s0_f = jnp.zeros((bsz, RET_HEADS, RET_DK, RET_DV), jnp.float32)
        s0_b = s0_f
    else:
        ctx_k, ctx_v, ctx_sf, ctx_sb = ctx
        cos, sin = rope
        q_a = apply_rope(q_a, cos, sin)
        keys = jnp.concatenate([apply_rope(k_a, cos, sin), ctx_k.astype(k_a.dtype)], axis=1)
        vals = jnp.concatenate([v_a, ctx_v.astype(v_a.dtype)], axis=1)
        s0_f = ctx_sf.astype(jnp.float32)
        s0_b = ctx_sb.astype(jnp.float32)
    o_att = blocked_gqa(q_a, keys, vals)

    qr = q_r.reshape(bsz, n_tok, RET_HEADS, RET_DK).astype(jnp.float32)
    kr = k_r.reshape(bsz, n_tok, RET_HEADS, RET_DK).astype(jnp.float32) * (RET_DK ** -0.5)
    vr = v_r.reshape(bsz, n_tok, RET_HEADS, RET_DV).astype(jnp.float32)
    lg_f = jax.nn.log_sigmoid(ret_decay_f.astype(jnp.float32))
    lg_b = jax.nn.log_sigmoid(ret_decay_b.astype(jnp.float32))
    o_r, s_f, s_b = bidir_retention(qr, kr, vr, lg_f, lg_b, s0_f, s0_b)
    o_r = rmsnorm(o_r, ret_norm).reshape(bsz, n_tok, RET_V).astype(x.dtype) * jax.nn.silu(g_r)

    u = u_f.reshape(bsz, n_tok, FOURIER_GROUPS, FOURIER_GROUP_DIM).astype(jnp.float32)
    o_f = jnp.fft.fft2(u, axes=(1, 3), norm='ortho').real.astype(x.dtype).reshape(bsz, n_tok, FOURIER_W)

    ga, gr, gf = jnp.split(jax.nn.sigmoid(gate_logits), N_BRANCH, axis=-1)
    merged = ga * (o_att @ w_br_att) + gr * (o_r @ w_br_ret) + gf * (o_f @ w_br_four)
    x = x + g1 * (merged @ w_out)

    h2 = rmsnorm(x, norm2) * (1 + sc2) + sh2
    a, val = jnp.split(h2 @ w_up, 2, axis=-1)
    a = dwconv3(a, conv_w, conv_b)
    x = x + g2 * ((jax.nn.gelu(a) * val) @ w_down)

    new_ctx = (k_a, v_a, s_f, s_b) if ctx is None else None
    return x, new_ctx


def setup_inputs(seed: int = 0) -> dict:
    key = jax.random.key(seed)
    ks = jax.random.split(key, 32)

    def nrm(k, shape, scale):
        return jax.random.normal(k, shape, jnp.float32) * scale

    ls_f = jnp.linspace(jnp.log(1.0 / 32), jnp.log(1.0 / 512), RET_HEADS)
    ls_b = ls_f[::-1]
    logit_f = jnp.log1p(-jnp.exp(ls_f)) - ls_f
    logit_b = jnp.log1p(-jnp.exp(ls_b)) - ls_b
    return {
        'x_prompt': nrm(ks[0], (BATCH, SEQ, D_MODEL), 1.0),
        'x_sample': nrm(ks[1], (DEC_BATCH, DEC_SEQ, D_MODEL), 1.0),
        'cache_k': nrm(ks[2], (DEC_BATCH, DEPTH, PAST_LEN, N_KV_HEADS, HEAD_DIM), 1.0),
        'cache_v': nrm(ks[3], (DEC_BATCH, DEPTH, PAST_LEN, N_KV_HEADS, HEAD_DIM), 1.0),
        'state_ret_fwd': nrm(ks[4], (DEC_BATCH, DEPTH, RET_HEADS, RET_DK, RET_DV), 0.5),
        'state_ret_bwd': nrm(ks[5], (DEC_BATCH, DEPTH, RET_HEADS, RET_DK, RET_DV), 0.5),
        'c': nrm(ks[6], (DEC_BATCH, D_MODEL), 1.0),
        'c_ctx': nrm(ks[7], (D_MODEL,), 1.0),
        'w_ada': nrm(ks[8], (DEPTH, D_MODEL, 6 * D_MODEL), 0.5 * D_MODEL ** -0.5),
        'b_ada': nrm(ks[9], (DEPTH, 6 * D_MODEL), 0.02),
        'norm1': 1.0 + nrm(ks[10], (DEPTH, D_MODEL), 0.02),
        'w_in': nrm(ks[11], (DEPTH, D_MODEL, D_IN), D_MODEL ** -0.5),
        'q_norm': 1.0 + nrm(ks[12], (DEPTH, HEAD_DIM), 0.02),
        'k_norm': 1.0 + nrm(ks[13], (DEPTH, HEAD_DIM), 0.02),
        'ret_decay_f': logit_f[None, :] + nrm(ks[14], (DEPTH, RET_HEADS), 0.1),
        'ret_decay_b': logit_b[None, :] + nrm(ks[15], (DEPTH, RET_HEADS), 0.1),
        'ret_norm': 1.0 + nrm(ks[16], (DEPTH, RET_DV), 0.02),
        'w_br_att': nrm(ks[17], (DEPTH, ATT_Q, D_MODEL), ATT_Q ** -0.5),
        'w_br_ret': nrm(ks[18], (DEPTH, RET_V, D_MODEL), RET_V ** -0.5),
        'w_br_four': nrm(ks[19], (DEPTH, FOURIER_W, D_MODEL), FOURIER_W ** -0.5),
        'w_out': nrm(ks[20], (DEPTH, D_MODEL, D_MODEL), D_MODEL ** -0.5),
        'norm2': 1.0 + nrm(ks[21], (DEPTH, D_MODEL), 0.02),
        'w_up': nrm(ks[22], (DEPTH, D_MODEL, 2 * D_FF), D_MODEL ** -0.5),
        'conv_w': nrm(ks[23], (DEPTH, CONV_W, D_FF), CONV_W ** -0.5),
        'conv_b': nrm(ks[24], (DEPTH, D_FF), 0.02),
        'w_down': nrm(ks[25], (DEPTH, D_FF, D_MODEL), D_FF ** -0.5),
    }


def reference(x_prompt, x_sample, cache_k, cache_v, state_ret_fwd, state_ret_bwd, c, c_ctx,
              w_ada, b_ada, norm1, w_in, q_norm, k_norm, ret_decay_f, ret_decay_b, ret_norm,
              w_br_att, w_br_ret, w_br_four, w_out, norm2, w_up, conv_w, conv_b, w_down):
    rope = axial_rope_tables(x_sample.shape[1])
    cond_ctx = c_ctx[None, :]
    xp, xs = x_prompt, x_sample
    new_k, new_v, new_sf, new_sb = [], [], [], []
    for l in range(DEPTH):
        lw = (w_ada[l], b_ada[l], norm1[l], w_in[l], q_norm[l], k_norm[l], ret_decay_f[l],
              ret_decay_b[l], ret_norm[l], w_br_att[l], w_br_ret[l], w_br_four[l], w_out[l],
              norm2[l], w_up[l], conv_w[l], conv_b[l], w_down[l])
        xp, (k_l, v_l, sf_l, sb_l) = trunk_layer(xp, cond_ctx, None, None, *lw)
        new_k.append(k_l)
        new_v.append(v_l)
        new_sf.append(sf_l)
        new_sb.append(sb_l)
        ctx_l = (cache_k[:, l], cache_v[:, l], state_ret_fwd[:, l], state_ret_bwd[:, l])
        xs, _ = trunk_layer(xs, c, rope, ctx_l, *lw)
    return (xp, xs, jnp.stack(new_k, axis=1), jnp.stack(new_v, axis=1),
            jnp.stack(new_sf, axis=1), jnp.stack(new_sb, axis=1))
```

```python
import contextlib
import math
import numpy as np
import ml_dtypes
import concourse.bass as bass
import concourse.mybir as mybir
from concourse.bass_utils import run_bass_kernel_spmd

F32 = mybir.dt.float32
BF16 = mybir.dt.bfloat16
AF = mybir.ActivationFunctionType
ALU = mybir.AluOpType

D = 1024
DEPTH = 2
NT = 1024
D_IN = 5888
D_FF = 2816
NFF = 22
EPS = 1e-6
TW = 1920
TOFF = 896
PPL = 155
SLAB = 4608
NWB = 4
NSLAB = 35
SERIALIZE = False


class Buf:
    __slots__ = ("name", "w", "r")

    def __init__(self, name):
        self.name = name
        self.w = None
        self.r = []


class TB:
    def __init__(self, name, nch, ntb=4):
        self.b = [[Buf("%s_%d_%d" % (name, c, t)) for t in range(ntb)] for c in range(nch)]

    def s(self, chunks, t0=0, t1=NT):
        if isinstance(chunks, int):
            chunks = [chunks]
        return [self.b[c][t] for c in chunks for t in range(t0 // 256, (t1 + 255) // 256)]


class Sched:
    ENGS = ("pe", "act", "dve", "pool", "sp")

    def __init__(self, nc, stack, serialize=False, ring=8):
        self.nc = nc
        self.sem = {e: stack.enter_context(nc.semaphore("s_" + e)) for e in self.ENGS}
        self.cnt = {e: 0 for e in self.ENGS}
        self.waited = {e: {} for e in self.ENGS}
        self.prog = {e: [] for e in self.ENGS}
        self.rings = {}
        for q in ("sp", "pool"):
            self.rings[q] = [[stack.enter_context(nc.semaphore("r_%s%d" % (q, i))), 0] for i in range(ring)]
        self.ring_pos = {q: 0 for q in self.rings}
        self.serialize = serialize
        self.last_tok = None
        self.pool_bar = None
        self.skip_self = False

    def _wait(self, eng, tok):
        if tok is None:
            return
        sem, val = tok
        if self.skip_self and sem is self.sem[eng]:
            return
        k = id(sem)
        if self.waited[eng].get(k, 0) >= val:
            return
        self.waited[eng][k] = val
        self.prog[eng].append(("wait", sem, val))

    def _deps(self, eng, reads, writes):
        for b in reads:
            self._wait(eng, b.w)
        for b in writes:
            self._wait(eng, b.w)
            for t in b.r:
                self._wait(eng, t)
        if self.serialize:
            self._wait(eng, self.last_tok)

    def _commit(self, tok, reads, writes):
        for b in reads:
            b.r.append(tok)
            if len(b.r) > 48:
                best = {}
                for s, v in b.r:
                    if best.get(id(s), (None, 0))[1] < v:
                        best[id(s)] = (s, v)
                b.r = list(best.values())
        for b in writes:
            b.w = tok
            b.r = []
        self.last_tok = tok

    def op(self, eng, fn, reads=(), writes=()):
        self._deps(eng, reads, writes)
        self.cnt[eng] += 1
        tok = (self.sem[eng], self.cnt[eng])
        self.prog[eng].append(("op", fn, self.sem[eng]))
        self._commit(tok, reads, writes)
        return tok

    def dma(self, q, out, in_, reads=(), writes=(), arena=False):
        if arena and q == "pool" and self.pool_bar:
            for t in self.pool_bar:
                self._wait(q, t)
            self.pool_bar = None
        self._deps(q, reads, writes)
        ring = self.rings[q]
        i = self.ring_pos[q]
        self.ring_pos[q] = (i + 1) % len(ring)
        sem, n = ring[i]
        if n > 0:
            self._wait(q, (sem, 16 * n))
        ring[i][1] = n + 1
        tok = (sem, 16 * (n + 1))
        self.prog[q].append(("dma", out, in_, sem))
        self._commit(tok, reads, writes)
        return tok

    def mm(self, out, pairs, reads, writes):
        pairs = list(pairs)

        def fn(e):
            n = len(pairs)
            ins = None
            for i, (l, r) in enumerate(pairs):
                ins = e.matmul(out, lhsT=l, rhs=r, start=(i == 0), stop=(i == n - 1))
            return ins
        return self.op("pe", fn, reads, writes)

    def mm_split(self, out, pairs, reads_list, common_reads, writes):
        pairs = list(pairs)
        n = len(pairs)
        for i, (l_, r_) in enumerate(pairs):
            self.skip_self = (i > 0)
            self.op("pe", lambda e, i=i, l_=l_, r_=r_: e.matmul(out, lhsT=l_, rhs=r_, start=(i == 0), stop=(i == n - 1)),
                    list(reads_list[i]) + (list(common_reads) if i == 0 else []), writes)
        self.skip_self = False

    def act(self, out, in_, func, reads, writes, bias=None, scale=None):
        kw = {}
        if bias is not None:
            kw["bias"] = bias
        if scale is not None:
            kw["scale"] = scale
        return self.op("act", lambda e: e.activation(out=out, in_=in_, func=func, **kw), reads, writes)

    def tt(self, out, in0, in1, op, reads, writes, eng="dve"):
        return self.op(eng, lambda e: e.tensor_tensor(out=out, in0=in0, in1=in1, op=op), reads, writes)

    def ts(self, out, in0, s1, s2, op0, op1, reads, writes, eng="dve"):
        if op1 is None:
            return self.op(eng, lambda e: e.tensor_scalar(out=out, in0=in0, scalar1=s1, scalar2=None, op0=op0),
                           reads, writes)
        return self.op(eng, lambda e: e.tensor_scalar(out=out, in0=in0, scalar1=s1, scalar2=s2, op0=op0, op1=op1),
                       reads, writes)

    def stt(self, out, in0, scalar, in1, op0, op1, reads, writes):
        return self.op("dve", lambda e: e.scalar_tensor_tensor(out=out, in0=in0, scalar=scalar, in1=in1,
                                                                op0=op0, op1=op1), reads, writes)

    def copy(self, out, in_, reads, writes, eng="dve"):
        if eng == "act":
            return self.op(eng, lambda e: e.activation(out=out, in_=in_, func=AF.Copy), reads, writes)
        return self.op(eng, lambda e: e.tensor_copy(out=out, in_=in_), reads, writes)

    def barrier(self):
        toks = [(self.sem[e], self.cnt[e]) for e in self.ENGS if self.cnt[e] > 0]
        for q in self.rings:
            for sem, n in self.rings[q]:
                if n > 0:
                    toks.append((sem, 16 * n))
        for e in ("act", "dve", "sp"):
            for t in toks:
                self._wait(e, t)
        self.pool_bar = toks

    def memset(self, ap, val, writes, eng="dve"):
        return self.op(eng, lambda e: e.memset(ap, val), (), writes)

    def wait_all(self, eng, toks):
        for t in toks:
            self._wait(eng, t)

    def run(self, block):
        def mk(eng):
            def body(e):
                for it in self.prog[eng]:
                    if it[0] == "wait":
                        e.wait_ge(it[1], it[2])
                    elif it[0] == "op":
                        it[1](e).then_inc(it[2], 1)
                    else:
                        e.dma_start(out=it[1], in_=it[2]).then_inc(it[3], 16)
            return body
        block.tensor(mk("pe"))
        block.scalar(mk("act"))
        block.vector(mk("dve"))
        block.gpsimd(mk("pool"))
        block.sync(mk("sp"))


class Rot:
    def __init__(self, items):
        self.items = list(items)
        self.i = 0

    def __call__(self):
        v = self.items[self.i % len(self.items)]
        self.i += 1
        return v


class Arena:
    def __init__(self, t, nelem16, sched=None):
        self.sched = sched
        self.t = t
        self.n = nelem16
        self.off = 0

    def reset(self, off=0):
        self.off = off
        if self.sched is not None:
            self.sched.barrier()

    def alloc(self, free, dt):
        n = int(np.prod(free))
        w = n * 2 if dt == F32 else n
        off = (self.off + 31) // 32 * 32
        assert off + w <= self.n, ("arena overflow", off, w, self.n)
        ap = self.t[:, off:off + w]
        if dt == F32:
            ap = ap.bitcast(F32)
        if len(free) == 2:
            ap = ap.rearrange("p (a b) -> p a b", a=free[0])
        elif len(free) == 3:
            ap = ap.rearrange("p (a b c) -> p a b c", a=free[0], b=free[1])
        elif len(free) == 4:
            ap = ap.rearrange("p (a b c d) -> p a b c d", a=free[0], b=free[1], c=free[2])
        self.off = off + w
        return ap


def build_program():
    nc = bass.Bass("TRN2", target_bir_lowering=False)

    def din(name, shape, dt=F32):
        return nc.dram_tensor(name, list(shape), dt, kind="ExternalInput").ap()

    def dout(name, shape, dt=F32):
        return nc.dram_tensor(name, list(shape), dt, kind="ExternalOutput").ap()

    xin = din("xin", [2, 128, 8, NT])
    condT = din("condT", [128, 8, 2])
    ppd = din("pp", [128, 2 * PPL + 24])
    wpk = din("wpk", [2, NSLAB, 128, SLAB])
    wadapk = din("wadapk", [2, 12, 128, 4096])
    ckT = din("ckT", [2, 2, 2, 128, 512])
    cvd = din("cv", [2, 512, 128])
    s0d = din("s0", [2, 128, 4, 128])
    cb16d = din("cb16", [128, 768], BF16)
    roped = din("rope", [128, 2, NT])
    dft256d = din("dft256", [128, 2, 2, 256], BF16)
    dft1024d = din("dft1024", [4, 128, 8, 512], BF16)
    dtabd = din("dtab", [3, 128, TW])
    n1tabd = din("n1tab", [2, 128, NT])
    eppd = din("epp", [128, 4])

    yout = dout("yout", [2, 128, 8, NT])
    ck_out = dout("ck_out", [2, 128, NT])
    cv_out = dout("cv_out", [2, NT, 128])
    sf_out = dout("sf_out", [2, 4, 4, 64, 128])
    sb_out = dout("sb_out", [2, 4, 4, 64, 128])

    tmask_d = nc.dram_tensor("tmask_d", [2, 4, 128, TW], BF16, kind="Internal").ap()
    decq_d = nc.dram_tensor("decq_d", [2, 128, 2, 2, NT], BF16, kind="Internal").ap()

    out_toks = []
    with contextlib.ExitStack() as st:
        S = Sched(nc, st, serialize=SERIALIZE)

        def sb(name, shape, dt):
            return st.enter_context(nc.sbuf_tensor(name, list(shape), dt))

        xT = sb("xT", [128, 8, NT], F32)
        hT = sb("hT", [128, 8, NT], BF16)
        obr = [sb("oatt", [128, 4, NT], BF16), sb("oret", [128, 4, NT], BF16), sb("ofou", [128, 4, NT], BF16)]
        wbuf = [sb("wb%d" % i, [128, SLAB], BF16) for i in range(NWB)]
        rope = sb("rope_t", [128, 2, NT], F32)
        cb16 = sb("cb16_t", [128, 768], BF16)
        pp = sb("pp_t", [128, 2 * PPL + 24], F32)
        modT = sb("modT", [128, 2, 2, 48], F32)
        abT = sb("abT", [128, 2, 8], F32)
        lgb_t = sb("lgb_t", [128, 24], F32)
        kdec = sb("kdec", [128, 2, 2, 8], F32)
        epp = sb("epp_t", [128, 4], F32)
        scond = sb("scond", [128, 8, 2], BF16)
        condt = sb("condt", [128, 8, 2], F32)
        tblf = sb("tblf", [128, 3, 480], F32)
        tble = sb("tble", [128, 4, 512], BF16)
        tbln = sb("tbln", [128, NT], F32)
        lgq = sb("lgq", [128, 8], F32)
        ARN = 37888
        arena_t = sb("arena", [128, ARN], BF16)
        AR = Arena(arena_t, ARN, S)
        ps = st.enter_context(nc.psum_tensor("ps", [128, 8, 512], F32))

        ones1024 = cb16[:, 0:128]
        ones128 = cb16[:, 128:256]
        bd64 = cb16[:, 256:384]
        pswap = cb16[:, 384:512]
        cs128 = cb16[:, 512:768]

        XT = TB("xT", 8)
        HT = TB("hT", 8)
        OBR = [TB("oatt", 4), TB("oret", 4), TB("ofou", 4)]
        WB = [Buf("wb%d" % i) for i in range(NWB)]
        PB = [Buf("ps%d" % i) for i in range(8)]
        CONST = Buf("const")
        ABB = Buf("ab")
        LG = Buf("lg")
        KDEC = Buf("kdec")
        wrot = Rot(range(NWB))

        def load_slab(pieces):
            i = wrot()
            for (dst, src) in pieces(wbuf[i]):
                S.dma("pool", dst, src, reads=[], writes=[WB[i]])
            return wbuf[i], WB[i]

        def load_w(l, idx, n):
            return load_slab(lambda wbt: [(wbt[:, 0:n], wpk[l, idx, :, 0:n])])

        def wcols(w_l, c0, nc_, kc=8):
            return w_l.rearrange("(kc p) n -> p kc n", p=128)[:, :, c0:c0 + nc_]

        def slabview(wb, off, kc, ncols):
            return wb[:, off:off + kc * ncols].rearrange("p (k n) -> p k n", k=kc)

        S.dma("sp", cb16[:], cb16d[:, :], writes=[CONST])
        S.dma("sp", pp[:], ppd[:, :], writes=[CONST])
        S.dma("sp", rope[:], roped[:, :, :], writes=[CONST])
        S.dma("sp", epp[:], eppd[:, :], writes=[CONST])
        S.dma("sp", condt[:], condT[:, :, :], writes=[CONST])

        SC = Buf("scond")
        MODL = [Buf("mod0"), Buf("mod1")]
        S.act(scond[:], condt[:], AF.Silu, [CONST], [SC])
        brot0 = Rot(range(8))

        mod_ext = [None]
        drot = Rot([6, 7])

        def mod_slab(l, sidx, brot):
            if mod_ext[0] is None:
                wb, WBb = load_slab(lambda wbt: [(wbt[:, 0:4096], wadapk[l, sidx])])
            else:
                bufs_, BUFS_, rot_ = mod_ext[0]
                i_ = rot_()
                wb, WBb = bufs_[i_], BUFS_[i_]
                S.dma("pool", wb[:, 0:4096], wadapk[l, sidx], writes=[WBb], arena=True)
                brot = drot
            wv = slabview(wb, 0, 8, 512)
            b = brot()
            for j in range(4):
                S.mm(ps[:, b, 2 * j:2 * j + 2],
                     [(wv[:, kc, j * 128:(j + 1) * 128], scond[:, kc, :]) for kc in range(8)],
                     [WBb, SC], [PB[b]])
            for cnd in range(2):
                S.tt(modT[:, l, cnd, sidx * 4:sidx * 4 + 4],
                     ps[:, b, 0:8].rearrange("p (j c) -> p j c", c=2)[:, :, cnd],
                     pp[:, l * PPL + 16 + sidx * 4: l * PPL + 16 + sidx * 4 + 4], ALU.add,
                     [PB[b], CONST], [MODL[l]])
        RDO = 2 * PPL
        S.act(lgb_t[:], pp[:, RDO:RDO + 24], AF.Exp, [CONST], [LG], scale=-1.0)
        S.act(lgb_t[:], lgb_t[:], AF.Ln, [LG], [LG], bias=1.0, scale=1.0)
        S.ts(lgb_t[:], lgb_t[:], -1.0, None, ALU.mult, None, [LG], [LG])

        def lg_b(l, dr, h):
            return lgb_t[:, l * 8 + dr * 4 + h: l * 8 + dr * 4 + h + 1]

        def lg_p(l, dr, j):
            c = 16 + l * 4 + dr * 2 + j
            return lgb_t[:, c:c + 1]

        import collections
        DCH = Buf("dch")
        DCN = Buf("dcn")
        EB = [Buf("te0"), Buf("te1"), Buf("te2"), Buf("te3")]
        TMDQ = [[[Buf("tmd%d%d%d" % (l, h, q)) for q in range(4)] for h in range(4)] for l in range(2)]
        DQDH = [[Buf("dqd%d_%d" % (l, k_)) for k_ in range(8)] for l in range(2)]
        ering = Rot(range(3))

        def tmask_q(l, h, q, load):
            if load:
                S.dma("sp", tblf[:, :, :], dtabd[:, :, q * 480:(q + 1) * 480].rearrange("a p n -> p a n"),
                      writes=[DCH])
            i = ering()
            e1, e2 = tble[:, i, 0:480], tble[:, 3, 0:480]
            S.act(e1, tblf[:, 0, 0:480], AF.Exp, [DCH, LG], [EB[i]], scale=lg_b(l, 0, h))
            S.act(e2, tblf[:, 1, 0:480], AF.Exp, [DCH, LG], [EB[3]], scale=lg_b(l, 1, h))
            S.tt(e1, e1, e2, ALU.mult, [EB[3]], [EB[i]])
            S.tt(e1, e1, tblf[:, 2, 0:480], ALU.add, [DCH], [EB[i]])
            S.dma("sp", tmask_d[l, h][:, q * 480:(q + 1) * 480], e1, reads=[EB[i]], writes=[TMDQ[l][h][q]])

        S.dma("sp", tbln[:, :], n1tabd[0], writes=[DCN])
        for l_ in range(2):
            for j_ in range(2):
                k_ = l_ * 2 + j_
                S.ts(lgq[:, k_:k_ + 1], lg_p(l_, 1, j_), -1.0, None, ALU.mult, None, [LG], [LG])
                S.ts(lgq[:, 4 + k_:5 + k_], lg_p(l_, 1, j_), 1025.0, None, ALU.mult, None, [LG], [LG])

        def decq_h(l, j, dr, hf):
            i = ering()
            src = tbln[:, hf * 512:(hf + 1) * 512]
            if dr == 0:
                S.act(tble[:, i, :], src, AF.Exp, [DCN, LG], [EB[i]], scale=lg_p(l, 0, j))
            else:
                k_ = l * 2 + j
                S.act(tble[:, i, :], src, AF.Exp, [DCN, LG], [EB[i]], scale=lgq[:, k_:k_ + 1],
                      bias=lgq[:, 4 + k_:5 + k_])
            S.dma("sp", decq_d[l, :, j, dr, hf * 512:(hf + 1) * 512], tble[:, i, :], reads=[EB[i]],
                  writes=[DQDH[l][j * 4 + dr * 2 + hf]])

        def tmask_steps(l):
            return [(lambda l=l, h=h, q=q: tmask_q(l, h, q, h == 0)) for q in range(4) for h in range(4)]

        def decq_steps(l):
            return [(lambda l=l, j=j, dr=dr, hf=hf: decq_h(l, j, dr, hf))
                    for j in range(2) for dr in range(2) for hf in range(2)]
        for l in range(2):
            for i in range(2):
                for dr in range(2):
                    S.act(kdec[:, l, i, dr * 4:dr * 4 + 4], lgb_t[:, l * 8 + dr * 4:l * 8 + dr * 4 + 4], AF.Exp,
                          [LG, CONST], [KDEC], scale=epp[:, i * 2 + dr:i * 2 + dr + 1])
        for sidx in range(4):
            mod_slab(0, sidx, brot0)
        Q = collections.defaultdict(collections.deque)
        cur_brot = [brot0]
        cur_pass = [0]
        for sidx in range(4, 12):
            Q[(0, "att")].append(lambda sidx=sidx: mod_slab(0, sidx, cur_brot[0]))
        t0s = tmask_steps(0)
        Q[(0, "four")].extend(t0s[:8])
        Q[(0, "ret")].extend(t0s[8:])
        ml1 = [(lambda sidx=sidx: mod_slab(1, sidx, cur_brot[0])) for sidx in range(6)]
        t1s = tmask_steps(1)
        while ml1 or t1s:
            for lst in (ml1, t1s, t1s):
                if lst:
                    Q[(0, "merge")].append(lst.pop(0))
        for sidx in range(6, 12):
            Q[(1, "att")].append(lambda sidx=sidx: mod_slab(1, sidx, cur_brot[0]))
        Q[(1, "merge")].extend(decq_steps(0) + decq_steps(1))

        def drain(name, n=1):
            q_ = Q[(cur_pass[0], name)]
            while q_ and n != 0:
                q_.popleft()()
                n -= 1

        def norm_mod(A, Bv, brot_, MOD):
            AR.reset()
            sq = [AR.alloc([8, 512], BF16) for _ in range(2)]
            tmp = [AR.alloc([512], F32) for _ in range(4)]
            SQ = [Buf("sq0"), Buf("sq1")]
            TMP = [Buf("tmp%d" % i) for i in range(4)]
            bks = []
            for tg in range(2):
                t0, t1 = tg * 512, tg * 512 + 512
                S.act(sq[tg], xT[:, :, t0:t1], AF.Square, XT.s(range(8), t0, t1), [SQ[tg]])
            for tg in range(2):
                b = brot_()
                bks.append(b)
                S.mm(ps[:, b, :], [(ones1024, sq[tg][:, c, :]) for c in range(8)], [SQ[tg], CONST], [PB[b]])
                S.act(ps[:, b, :], ps[:, b, :], AF.Ln, [], [PB[b]], bias=EPS, scale=1.0)
                S.act(ps[:, b, :], ps[:, b, :], AF.Exp, [], [PB[b]], scale=-0.5)
            k = 0
            for tg in range(2):
                t0, t1 = tg * 512, tg * 512 + 512
                b = bks[tg]
                for c in range(8):
                    i = k % 4
                    k += 1
                    S.stt(tmp[i], xT[:, c, t0:t1], A[:, c:c + 1], ps[:, b, :], ALU.mult, ALU.mult,
                          XT.s(c, t0, t1) + [PB[b], ABB, MOD], [TMP[i]])
                    S.act(hT[:, c, t0:t1], tmp[i], AF.Identity, [TMP[i], MOD], HT.s(c, t0, t1),
                          bias=Bv[:, c:c + 1], scale=1.0)

        def rstd_from(psb, PBb, out, OUTB):
            S.act(out, psb, AF.Ln, [PBb], [OUTB], bias=EPS, scale=1.0)
            S.act(out, out, AF.Exp, [OUTB], [OUTB], scale=-0.5)

        def layer_pass(half, l):
            smp = (half == 1)
            nseq, L = (1, 1024) if smp else (4, 256)
            ntl = L // 128
            NQ = 512 if smp else 256
            P0 = l * PPL
            md = modT[:, l, half, :]
            MOD = MODL[l]
            defer_mod = (half == 0 and l == 0)
            cur_pass[0] = half * 2 + l
            brot_ = Rot(range(8))
            cur_brot[0] = brot_

            S.stt(abT[:, 0, :], md[:, 8:16], 1.0, pp[:, P0:P0 + 8], ALU.add, ALU.mult, [MOD, CONST], [ABB])

            norm_mod(abT[:, 0, :], md[:, 0:8], brot_, MOD)

            def ph_att():
                AR.reset()
                qT = AR.alloc([4, NT], BF16)
                nkeys = 1536 if smp else 1024
                KT = AR.alloc([2, 2, nkeys], BF16)
                nvt = 12 if smp else 8
                VA = AR.alloc([nvt, 2, 128], BF16)
                etb = [AR.alloc([512], BF16) for _ in range(4)]
                sqb = [AR.alloc([512], BF16) for _ in range(4)]
                rsb = [AR.alloc([512], F32) for _ in range(4)]
                if smp:
                    qnb = [AR.alloc([512], BF16) for _ in range(4)]
                    t1b = [AR.alloc([512], F32) for _ in range(3)]
                recb = [AR.alloc([512], F32) for _ in range(2)]
                if not smp:
                    kst = AR.alloc([2, NT], F32)
                    vst = AR.alloc([8, 128], F32)
                    KST, VST = Buf("kst"), Buf("vst")
                if Q[(cur_pass[0], "att")]:
                    mod_ext[0] = ([AR.alloc([4096], BF16) for _ in range(2)], [Buf("mx%d" % k_) for k_ in range(2)], Rot(range(2)))
                QT = TB("qT", 4)
                KTB = TB("KT", 2, 6)
                VAB = [Buf("va%d" % i) for i in range(nvt)]
                ETB = [Buf("et%d" % i) for i in range(4)]
                SQB = [Buf("sqb%d" % i) for i in range(4)]
                RSB = [Buf("rsb%d" % i) for i in range(4)]
                QNB = [Buf("qnb%d" % i) for i in range(4)]
                T1B = [Buf("t1b%d" % i) for i in range(4)]
                T2B = [Buf("t2b%d" % i) for i in range(4)]
                RECB = [Buf("rec%d" % i) for i in range(2)]
                rr = Rot(range(2))

                S.memset(VA[:, :, :, 64:128], 1.0, VAB)
                S.memset(KT[:, :, :, 0:1024], 0.0, KTB.s([0, 1], 0, 1024))

                wbq, WBq = load_w(l, 0, 4096)
                wq = slabview(wbq, 0, 8, 512)

                wbk, WBk = load_w(l, 1, 3072)
                wk = slabview(wbk, 0, 8, 384)
                if smp:
                    for kv in range(2):
                        for z in range(2):
                            S.dma("pool", KT[:, kv, z, 1024:1536], ckT[l, kv, z], writes=KTB.s(kv, 1024, 1536), arena=True)
                    for kv in range(2):
                        S.dma("pool", VA[:, 8:12, kv, 0:64],
                              cvd[l].rearrange("(tt p) f -> p tt f", p=128)[:, :, kv * 64:(kv + 1) * 64],
                              writes=VAB[8:12], arena=True)

                units = []
                for tg in range(2):
                    t0 = tg * 512
                    for j in range(4):
                        units.append(dict(w=wq, WBw=WBq, c0=j * 128, g=P0 + 64, out=qT[:, j, t0:t0 + 512],
                                          OUTB=QT.s(j, t0, t0 + 512), t0=t0, kst=None))
                    for kv in range(2):
                        units.append(dict(w=wk, WBw=WBk, c0=kv * 128, g=P0 + 65, out=None, kv=kv,
                                          OUTB=KTB.s(kv, t0, t0 + 512), t0=t0,
                                          kst=(None if smp else kst[:, kv, t0:t0 + 512])))
                for ui, u in enumerate(units):
                    u["i"] = ui % 4
                    u["i3"] = ui % 3
                    u["split"] = ui in (0, 1, 6, 7)

                rotA, rotB, rotC = Rot([0, 1, 2, 3]), Rot([4, 5]), Rot([6, 7])

                def stA(u):
                    t0, c0, w = u["t0"], u["c0"], u["w"]
                    b = rotA()
                    u["b"] = b
                    if u["split"]:
                        S.mm_split(ps[:, b, :], [(w[:, kc, c0:c0 + 128], hT[:, kc, t0:t0 + 512]) for kc in range(8)],
                                   [HT.s(kc, t0, t0 + 512) for kc in range(8)], [u["WBw"]], [PB[b]])
                    else:
                        S.mm(ps[:, b, :], [(w[:, kc, c0:c0 + 128], hT[:, kc, t0:t0 + 512]) for kc in range(8)],
                             [u["WBw"]] + HT.s(range(8), t0, t0 + 512), [PB[b]])
                    S.act(sqb[u["i"]], ps[:, b, :], AF.Square, [PB[b]], [SQB[u["i"]]])

                def stB(u):
                    i, b, g = u["i"], u["b"], u["g"]
                    b2 = rotB()
                    S.mm(ps[:, b2, :], [(bd64, sqb[i])], [SQB[i], CONST], [PB[b2]])
                    rstd_from(ps[:, b2, :], PB[b2], rsb[i], RSB[i])
                    if smp:
                        S.stt(ps[:, b, :], ps[:, b, :], pp[:, g:g + 1], rsb[i], ALU.mult, ALU.mult,
                              [RSB[i], CONST], [PB[b]])
                        S.copy(qnb[i], ps[:, b, :], [PB[b]], [QNB[i]], eng="act")
                    elif u["kst"] is not None:
                        S.stt(u["kst"], ps[:, b, :], pp[:, g:g + 1], rsb[i], ALU.mult, ALU.mult,
                              [PB[b], RSB[i], CONST], [KST])
                        for z in range(2):
                            S.copy(KT[z * 64:(z + 1) * 64, u["kv"], z, u["t0"]:u["t0"] + 512], u["kst"][z * 64:(z + 1) * 64, :],
                                   [KST], u["OUTB"], eng=("act" if z == 0 else "dve"))
                    else:
                        S.stt(u["out"], ps[:, b, :], pp[:, g:g + 1], rsb[i], ALU.mult, ALU.mult,
                              [PB[b], RSB[i], CONST], u["OUTB"])

                def stC(u):
                    if not smp:
                        return
                    i, t0 = u["i"], u["t0"]
                    i3 = u["i3"]
                    b = u["b"]
                    S.tt(t1b[i3], ps[:, b, :], rope[:, 0, t0:t0 + 512], ALU.mult, [PB[b], CONST], [T1B[i3]])
                    b3 = rotC()
                    S.mm(ps[:, b3, :], [(pswap, qnb[i])], [QNB[i], CONST], [PB[b3]])
                    S.tt(ps[:, b3, :], ps[:, b3, :], rope[:, 1, t0:t0 + 512], ALU.mult, [CONST], [PB[b3]])
                    if u["out"] is not None:
                        S.tt(u["out"], t1b[i3], ps[:, b3, :], ALU.add, [T1B[i3], PB[b3]], u["OUTB"])
                    else:
                        for z in range(2):
                            S.tt(KT[z * 64:(z + 1) * 64, u["kv"], z, t0:t0 + 512], t1b[i3][z * 64:(z + 1) * 64, :],
                                 ps[z * 64:(z + 1) * 64, b3, :], ALU.add, [T1B[i3], PB[b3]], u["OUTB"])
                stages = [stA, stB, stC]
                for step in range(len(units) + len(stages) - 1):
                    for si, stg in enumerate(stages):
                        ui = step - si
                        if 0 <= ui < len(units):
                            stg(units[ui])
                    if step >= 4:
                        drain("att", 1)
                if not smp:
                    for kv in range(2):
                        out_toks.append(S.dma("sp", ck_out[l, kv * 64:(kv + 1) * 64, :], kst[0:64, kv, :], reads=[KST]))
                for tt in range(8):
                    b = brot_()
                    S.mm(ps[:, b, 0:128], [(hT[:, kc, tt * 128:(tt + 1) * 128], wk[:, kc, 256:384]) for kc in range(8)],
                         [WBk] + HT.s(range(8), tt * 128, tt * 128 + 128), [PB[b]])
                    S.copy(VA[:, tt, :, 0:64], ps[:, b, 0:128].rearrange("p (k d) -> p k d", k=2), [PB[b]], [VAB[tt]])
                    if not smp:
                        S.copy(vst[:, tt, :], ps[:, b, 0:128], [PB[b]], [VST], eng="act")
                if not smp:
                    out_toks.append(S.dma("sp", cv_out[l].rearrange("(tt p) f -> p tt f", p=128), vst, reads=[VST]))

                orot = Rot([6, 7])
                srot = Rot([0, 1, 2, 3, 4, 5])
                erot = Rot(range(4))
                scale = 64 ** -0.5
                if smp:
                    sbanks = [Rot([0, 1]), Rot([2, 3])]
                    obanks = Rot([4, 6])
                    e4 = Rot(range(4))
                    for h in range(8):
                        kv, z, ch = h // 4, h % 2, h // 2
                        base = z * 64
                        ob0 = obanks()
                        obs = [ob0, ob0 + 1]

                        def s_exp(st, kt, kv=kv, z=z, ch=ch):
                            sbk = sbanks[st]()
                            q0 = st * 512
                            S.mm(ps[:, sbk, :], [(KT[:, kv, z, kt * 128:(kt + 1) * 128], qT[:, ch, q0:q0 + 512])],
                                 KTB.s(kv, kt * 128, kt * 128 + 128) + QT.s(ch, q0, q0 + 512), [PB[sbk]])
                            ei = e4()
                            S.act(etb[ei], ps[:, sbk, :], AF.Exp, [PB[sbk]], [ETB[ei]], scale=scale)
                            return ei

                        def pv(st, kt, ei, kv=kv, obs=obs):
                            ob = obs[st]
                            S.op("pe", lambda e: e.matmul(ps[:, ob, :], lhsT=VA[:, kt, kv, :], rhs=etb[ei],
                                                          start=(kt == 0), stop=(kt == 11)),
                                 [VAB[kt], ETB[ei]], [PB[ob]])
                        cur = [s_exp(0, 0), s_exp(1, 0)]
                        for kt in range(12):
                            for st in range(2):
                                nxt = s_exp(st, kt + 1) if kt + 1 < 12 else None
                                pv(st, kt, cur[st])
                                cur[st] = nxt
                        for st in range(2):
                            ob, q0 = obs[st], st * 512
                            ri = rr()
                            S.op("dve", lambda e, ri=ri, ob=ob: e.reciprocal(out=recb[ri][64:128, :], in_=ps[64:128, ob, :]),
                                 [PB[ob]], [RECB[ri]])
                            S.tt(obr[0][base:base + 64, ch, q0:q0 + 512], ps[0:64, ob, :], recb[ri][64:128, :],
                                 ALU.mult, [PB[ob], RECB[ri]], OBR[0].s(ch, q0, q0 + 512))
                else:
                    aunits = [(sq_, p_) for sq_ in range(4) for p_ in range(4)]

                    def atA(u):
                        sq_, p_ = u
                        kv, ch, q0 = p_ // 2, p_, sq_ * 256
                        eis = []
                        for kt in range(2):
                            k0 = sq_ * 256 + kt * 128
                            sbk = srot()

                            def smm(e, sbk=sbk, kv=kv, ch=ch, q0=q0, k0=k0):
                                e.matmul(ps[:, sbk, 0:256], lhsT=KT[:, kv, 0, k0:k0 + 128], rhs=qT[:, ch, q0:q0 + 256],
                                         start=True, stop=True)
                                return e.matmul(ps[:, sbk, 256:512], lhsT=KT[:, kv, 1, k0:k0 + 128],
                                                rhs=qT[:, ch, q0:q0 + 256], start=True, stop=True)
                            S.op("pe", smm, KTB.s(kv, k0, k0 + 128) + QT.s(ch, q0, q0 + 256), [PB[sbk]])
                            ei = erot()
                            S.act(etb[ei], ps[:, sbk, :], AF.Exp, [PB[sbk]], [ETB[ei]], scale=scale)
                            eis.append(ei)
                        return eis

                    def atB(u, eis):
                        sq_, p_ = u
                        kv, ch, q0 = p_ // 2, p_, sq_ * 256
                        ob = orot()

                        def pvm(e, ob=ob, kv=kv, sq_=sq_, eis=eis):
                            ins = None
                            for z in range(2):
                                for kt in range(2):
                                    ins = e.matmul(ps[:, ob, z * 256:(z + 1) * 256], lhsT=VA[:, 2 * sq_ + kt, kv, :],
                                                   rhs=etb[eis[kt]][:, z * 256:(z + 1) * 256], start=(kt == 0), stop=(kt == 1))
                            return ins
                        S.op("pe", pvm, [VAB[2 * sq_], VAB[2 * sq_ + 1], ETB[eis[0]], ETB[eis[1]]], [PB[ob]])
                        ri = rr()
                        S.act(recb[ri][64:128, :], ps[64:128, ob, :], AF.Ln, [PB[ob]], [RECB[ri]])
                        S.act(recb[ri][64:128, :], recb[ri][64:128, :], AF.Exp, [RECB[ri]], [RECB[ri]], scale=-1.0)
                        for z in range(2):
                            S.tt(obr[0][z * 64:(z + 1) * 64, ch, q0:q0 + 256], ps[0:64, ob, z * 256:(z + 1) * 256],
                                 recb[ri][64:128, z * 256:(z + 1) * 256], ALU.mult, [PB[ob], RECB[ri]],
                                 OBR[0].s(ch, q0, q0 + 256))
                    prev = None
                    for u in aunits:
                        drain("att", 1)
                        eis = atA(u)
                        if prev is not None:
                            atB(*prev)
                        prev = (u, eis)
                    atB(*prev)
                drain("att", -1)

            def ph_ret():
                AR.reset()
                qr = AR.alloc([2, NT], BF16)
                kr = AR.alloc([2, 2, NT], BF16)
                vr = AR.alloc([8, 512], BF16)
                sg = AR.alloc([4, NT], BF16)
                atb = [AR.alloc([512], BF16) for _ in range(4)]
                tmk = [AR.alloc([TW], BF16) for _ in range(2)]
                sqb = [AR.alloc([512], BF16) for _ in range(2)]
                rsb = [AR.alloc([512], F32) for _ in range(2)]
                t1b = [AR.alloc([512], F32) for _ in range(2)]
                QR, KR = TB("qr", 2), TB("kr", 2)
                VRB = [Buf("vr%d" % i) for i in range(8)]
                SG = TB("sg", 4)
                ATB = [Buf("at%d" % i) for i in range(4)]
                TMK = [Buf("tmk%d" % i) for i in range(2)]
                wb1, WB1 = load_w(l, 2, 4096)
                wb2, WB2 = load_w(l, 3, 4096)
                wb3, WB3 = load_w(l, 4, 4096)
                if smp:
                    qd = AR.alloc([4, NT], BF16)
                    dqt = AR.alloc([2, 2, NT], BF16)
                    s0t = AR.alloc([4, 128], BF16)
                    QD, DQT, S0B = TB("qd", 4), Buf("dqt"), Buf("s0")
                    S.dma("sp", dqt, decq_d[l], reads=DQDH[l], writes=[DQT])
                    S.dma("pool", s0t, s0d[l], writes=[S0B], arena=True)
                else:
                    ktm = AR.alloc([8, 4, 128], BF16)
                    sst = [AR.alloc([512], F32) for _ in range(2)]
                    KTM = [Buf("ktm%d" % i) for i in range(8)]
                    SST = [Buf("sst%d" % i) for i in range(2)]
                S.memset(kr, 0.0, KR.s([0, 1]))

                w1 = slabview(wb1, 0, 8, 512)
                for tg in range(2):
                    for j in range(2):
                        t0 = tg * 512
                        b = brot_()
                        S.mm(ps[:, b, :], [(w1[:, kc, j * 128:(j + 1) * 128], hT[:, kc, t0:t0 + 512]) for kc in range(8)],
                             [WB1] + HT.s(range(8), t0, t0 + 512), [PB[b]])
                        S.copy(qr[:, j, t0:t0 + 512], ps[:, b, :], [PB[b]], QR.s(j, t0, t0 + 512), eng="act")
                        b = brot_()
                        S.mm(ps[:, b, :], [(w1[:, kc, 256 + j * 128:256 + (j + 1) * 128], hT[:, kc, t0:t0 + 512])
                                           for kc in range(8)],
                             [WB1] + HT.s(range(8), t0, t0 + 512), [PB[b]])
                        for z in range(2):
                            S.ts(kr[z * 64:(z + 1) * 64, j, z, t0:t0 + 512], ps[z * 64:(z + 1) * 64, b, :], 0.125, None,
                                 ALU.mult, None, [PB[b]], KR.s(j, t0, t0 + 512))
                if not smp:
                    for tt in range(8):
                        b = brot_()
                        S.mm(ps[:, b, 0:256], [(hT[:, kc, tt * 128:(tt + 1) * 128], w1[:, kc, 256:512]) for kc in range(8)],
                             [WB1] + HT.s(range(8), tt * 128, tt * 128 + 128), [PB[b]])
                        for dr in range(2):
                            for h in range(4):
                                S.ts(ktm[:, tt, h, dr * 64:(dr + 1) * 64], ps[:, b, h * 64:(h + 1) * 64],
                                     kdec[:, l, tt % 2, dr * 4 + h:dr * 4 + h + 1], 0.125, ALU.mult, ALU.mult,
                                     [PB[b], KDEC], [KTM[tt]])
                w2 = slabview(wb2, 0, 8, 512)
                for tt in range(8):
                    b = brot_()
                    S.mm(ps[:, b, :], [(hT[:, kc, tt * 128:(tt + 1) * 128], w2[:, kc, :]) for kc in range(8)],
                         [WB2] + HT.s(range(8), tt * 128, tt * 128 + 128), [PB[b]])
                    S.copy(vr[:, tt, :], ps[:, b, :], [PB[b]], [VRB[tt]], eng=("act" if tt % 2 else "dve"))
                    drain("ret", 1)
                w3 = slabview(wb3, 0, 8, 512)
                for h in range(4):
                    for tg in range(2):
                        t0 = tg * 512
                        b = brot_()
                        S.mm(ps[:, b, :], [(w3[:, kc, h * 128:(h + 1) * 128], hT[:, kc, t0:t0 + 512]) for kc in range(8)],
                             [WB3] + HT.s(range(8), t0, t0 + 512), [PB[b]])
                        S.act(sg[:, h, t0:t0 + 512], ps[:, b, :], AF.Silu, [PB[b]], SG.s(h, t0, t0 + 512))
                        drain("ret", 1)
                if smp:
                    for h in range(4):
                        j, bs = h // 2, (h % 2) * 64
                        for dr in range(2):
                            S.tt(qd[dr * 64:(dr + 1) * 64, h, :], qr[bs:bs + 64, j, :], dqt[bs:bs + 64, j, dr, :], ALU.mult,
                                 QR.s(j) + [DQT], QD.s(h))

                drain("ret", -1)
                if smp:
                    obanks = Rot([4, 6])
                    rsb_ = [Rot([0, 1]), Rot([2, 3])]
                    mrot = Rot([0, 2, 1, 3])
                else:
                    orot = Rot([6, 7])
                    srot = Rot([0, 1, 2, 3])
                    mrot = Rot([4, 5])
                arot = Rot(range(4))
                trot = Rot(range(2))
                pend_ret = [None]
                rr = Rot(range(2))

                def post_norm(ob, h, n0, W, b2=None):
                    i = rr()
                    S.act(sqb[i][:, 0:W], ps[:, ob, 0:W], AF.Square, [PB[ob]], [SQB[i]])
                    if b2 is None:
                        b2 = mrot()
                    S.mm(ps[:, b2, 0:W], [(ones128, sqb[i][:, 0:W])], [SQB[i], CONST], [PB[b2]])
                    rstd_from(ps[:, b2, 0:W], PB[b2], rsb[i][:, 0:W], RSB[i])
                    S.stt(t1b[i][:, 0:W], ps[:, ob, 0:W], pp[:, P0 + 66:P0 + 67], rsb[i][:, 0:W],
                          ALU.mult, ALU.mult, [PB[ob], RSB[i], CONST], [T1B[i]])
                    S.tt(obr[1][:, h, n0:n0 + W], t1b[i][:, 0:W], sg[:, h, n0:n0 + W], ALU.mult,
                         [T1B[i]] + SG.s(h, n0, n0 + W), OBR[1].s(h, n0, n0 + W), eng="pool")

                for h in range(4):
                    z, j = h % 2, h // 2
                    ti = trot()
                    S.dma("sp", tmk[ti], tmask_d[l, h], reads=TMDQ[l][h], writes=[TMK[ti]])
                    if smp:
                        ob0 = obanks()
                        obs = [ob0, ob0 + 1]
                        for st in range(2):
                            S.op("pe", lambda e, ob=obs[st], h=h, n0=st * 512: e.matmul(
                                ps[:, ob, :], lhsT=s0t[:, h, :], rhs=qd[:, h, n0:n0 + 512], start=True, stop=False),
                                 [S0B] + QD.s(h, st * 512, st * 512 + 512), [PB[obs[st]]])

                        def sc_mask(st, mt, j=j, z=z, ti=ti):
                            n0, m0 = st * 512, mt * 128
                            sbk = rsb_[st]()
                            S.mm(ps[:, sbk, :], [(kr[:, j, z, m0:m0 + 128], qr[:, j, n0:n0 + 512])],
                                 KR.s(j, m0, m0 + 128) + QR.s(j, n0, n0 + 512), [PB[sbk]])
                            ai = arot()
                            off = TOFF + n0 - m0
                            S.tt(atb[ai], ps[:, sbk, :], tmk[ti][:, off:off + 512], ALU.mult,
                                 [PB[sbk], TMK[ti]], [ATB[ai]])
                            return ai
                        cur = [sc_mask(0, 0), sc_mask(1, 0)]
                        if pend_ret[0] is not None:
                            pend_ret[0]()
                            pend_ret[0] = None
                        for mt in range(8):
                            for st in range(2):
                                ai = cur[st]
                                S.op("pe", lambda e, ob=obs[st], mt=mt, ai=ai, h=h: e.matmul(
                                    ps[:, ob, :], lhsT=vr[:, mt, h * 128:(h + 1) * 128], rhs=atb[ai],
                                    start=False, stop=(mt == 7)), [VRB[mt], ATB[ai]], [PB[obs[st]]])
                                cur[st] = sc_mask(st, mt + 1) if mt + 1 < 8 else None
                        pend_ret[0] = (lambda obs=obs, h=h: [post_norm(obs[st_], h, st_ * 512, 512, b2=rsb_[st_]())
                                                            for st_ in range(2)])
                    else:
                        def rtA(sp, j=j, z=z, ti=ti):
                            ais = []
                            for mt in range(2):
                                sbk = srot()

                                def smm(e, sbk=sbk, mt=mt, sp=sp, j=j, z=z):
                                    ins = None
                                    for a_ in range(2):
                                        q0 = sp * 512 + a_ * 256
                                        ins = e.matmul(ps[:, sbk, a_ * 256:(a_ + 1) * 256],
                                                       lhsT=kr[:, j, z, q0 + mt * 128:q0 + mt * 128 + 128],
                                                       rhs=qr[:, j, q0:q0 + 256], start=True, stop=True)
                                    return ins
                                S.op("pe", smm, KR.s(j, sp * 512, sp * 512 + 512) + QR.s(j, sp * 512, sp * 512 + 512), [PB[sbk]])
                                ai = arot()
                                off = TOFF - mt * 128
                                S.tt(atb[ai].rearrange("p (a n) -> p a n", a=2), ps[:, sbk, :].rearrange("p (a n) -> p a n", a=2),
                                     tmk[ti][:, off:off + 256].unsqueeze(1).to_broadcast([128, 2, 256]), ALU.mult,
                                     [PB[sbk], TMK[ti]], [ATB[ai]])
                                ais.append(ai)
                            return ais

                        def rtB(sp, ais, h=h):
                            ob = orot()

                            def pvm(e, ob=ob, sp=sp, ais=ais, h=h):
                                ins = None
                                for a_ in range(2):
                                    for mt in range(2):
                                        ins = e.matmul(ps[:, ob, a_ * 256:(a_ + 1) * 256],
                                                       lhsT=vr[:, sp * 4 + a_ * 2 + mt, h * 128:(h + 1) * 128],
                                                       rhs=atb[ais[mt]][:, a_ * 256:(a_ + 1) * 256],
                                                       start=(mt == 0), stop=(mt == 1))
                                return ins
                            S.op("pe", pvm, [VRB[sp * 4 + k_] for k_ in range(4)] + [ATB[ais[0]], ATB[ais[1]]], [PB[ob]])
                            post_norm(ob, h, sp * 512, 512)
                        for sp_ in range(2):
                            ais_ = rtA(sp_)
                            if pend_ret[0] is not None:
                                pend_ret[0]()
                            pend_ret[0] = (lambda sp_=sp_, ais_=ais_, rtB=rtB: rtB(sp_, ais_))
                if pend_ret[0] is not None:
                    pend_ret[0]()
                    pend_ret[0] = None
                if not smp:
                    for s in range(4):
                        b = mrot()
                        for h in range(4):
                            S.mm(ps[:, b, h * 128:(h + 1) * 128],
                                 [(ktm[:, 2 * s + i2, h, :], vr[:, 2 * s + i2, h * 128:(h + 1) * 128]) for i2 in range(2)],
                                 [KTM[2 * s], KTM[2 * s + 1], VRB[2 * s], VRB[2 * s + 1]], [PB[b]])
                        si = s % 2
                        S.copy(sst[si], ps[:, b, :], [PB[b]], [SST[si]])
                        out_toks.append(S.dma("sp", sf_out[l, s].rearrange("h k v -> k h v"),
                                              sst[si][0:64, :].rearrange("p (h v) -> p h v", h=4), reads=[SST[si]]))
                        out_toks.append(S.dma("sp", sb_out[l, s].rearrange("h k v -> k h v"),
                                              sst[si][64:128, :].rearrange("p (h v) -> p h v", h=4), reads=[SST[si]]))

            def ph_four():
                AR.reset()
                uT = AR.alloc([4, NT], BF16)
                Y = AR.alloc([8, 4, 256], BF16)
                UT = TB("uT", 4)
                YB = [Buf("y%d" % i) for i in range(8)]
                wb4, WB4 = load_w(l, 5, 4096)
                w4 = slabview(wb4, 0, 8, 512)
                for g in range(4):
                    for tg in range(2):
                        t0 = tg * 512
                        b = brot_()
                        S.mm(ps[:, b, :], [(w4[:, kc, g * 128:(g + 1) * 128], hT[:, kc, t0:t0 + 512]) for kc in range(8)],
                             [WB4] + HT.s(range(8), t0, t0 + 512), [PB[b]])
                        S.copy(uT[:, g, t0:t0 + 512], ps[:, b, :], [PB[b]], UT.s(g, t0, t0 + 512),
                               eng=("act" if tg else "dve"))
                        drain("four", 1)
                for tt in range(8):
                    for gp in range(2):
                        b = brot_()
                        for g in (2 * gp, 2 * gp + 1):
                            S.mm(ps[:, b, (g % 2) * 256:(g % 2) * 256 + 256],
                                 [(uT[:, g, tt * 128:(tt + 1) * 128], cs128)], UT.s(g, tt * 128, tt * 128 + 128) + [CONST],
                                 [PB[b]])
                        S.copy(Y[:, tt, 2 * gp:2 * gp + 2, :], ps[:, b, :].rearrange("p (g n) -> p g n", g=2),
                               [PB[b]], [YB[tt]], eng=("act" if gp else "dve"))
                        drain("four", 1)
                if smp:
                    for hf in range(2):
                        ct, CTB = load_slab(lambda wbt, hf=hf: [(wbt[:, 0:4096], dft1024d[2 * hf].rearrange("p a b -> p (a b)"))])
                        stt_, STB = load_slab(lambda wbt, hf=hf: [(wbt[:, 0:4096], dft1024d[2 * hf + 1].rearrange("p a b -> p (a b)"))])
                        cv_, sv_ = slabview(ct, 0, 8, 512), slabview(stt_, 0, 8, 512)
                        for g in range(4):
                            b = brot_()
                            pairs = []
                            for nt in range(8):
                                pairs.append((Y[:, nt, g, 0:128], cv_[:, nt, :]))
                                pairs.append((Y[:, nt, g, 128:256], sv_[:, nt, :]))
                            S.mm(ps[:, b, :], pairs, YB + [CTB, STB], [PB[b]])
                            S.copy(obr[2][:, g, hf * 512:(hf + 1) * 512], ps[:, b, :], [PB[b]],
                                   OBR[2].s(g, hf * 512, hf * 512 + 512), eng=("act" if g % 2 else "dve"))
                else:
                    dt_, DTB = load_slab(lambda wbt: [(wbt[:, 0:1024], dft256d.rearrange("p a b c -> p (a b c)"))])
                    dv = dt_[:, 0:1024].rearrange("p (a b c) -> p a b c", a=2, b=2)
                    for s in range(4):
                        for g in range(4):
                            b = brot_()
                            pairs = []
                            for nt in range(2):
                                pairs.append((Y[:, 2 * s + nt, g, 0:128], dv[:, 0, nt, :]))
                                pairs.append((Y[:, 2 * s + nt, g, 128:256], dv[:, 1, nt, :]))
                            S.mm(ps[:, b, 0:256], pairs, [YB[2 * s], YB[2 * s + 1], DTB], [PB[b]])
                            S.copy(obr[2][:, g, s * 256:(s + 1) * 256], ps[:, b, 0:256], [PB[b]],
                                   OBR[2].s(g, s * 256, s * 256 + 256), eng=("act" if g % 2 else "dve"))
                            drain("four", 1)
                drain("four", -1)

            SQB = [Buf("sqb%d" % i) for i in range(4)]
            RSB = [Buf("rsb%d" % i) for i in range(4)]
            T1B = [Buf("t1b%d" % i) for i in range(4)]
            if defer_mod:
                ph_att()
                ph_four()
                ph_ret()
            else:
                ph_att()
                ph_ret()
                ph_four()

            AR.reset()
            mg = AR.alloc([8, NT], BF16)
            sgt = [AR.alloc([512], F32) for _ in range(3)]
            mac = [AR.alloc([512], F32) for _ in range(2)]
            mtp = [AR.alloc([512], F32) for _ in range(2)]
            if cur_pass[0] == 0:
                mod_ext[0] = ([AR.alloc([4096], BF16) for _ in range(4)], [Buf("my%d" % k_) for k_ in range(4)], Rot(range(4)))
            MG = TB("mg", 8)
            SGT = [Buf("sgt%d" % i) for i in range(3)]
            MAC = [Buf("mac%d" % i) for i in range(2)]
            MTP = [Buf("mtp%d" % i) for i in range(2)]
            grot = Rot(range(3))
            for j in range(8):
                wbm, WBm = load_w(l, 6 + j, 4608)
                gv = wbm[:, 0:3072].rearrange("p (k b n) -> p k b n", k=8, b=3)
                bv = wbm[:, 3072:4608].rearrange("p (k b n) -> p k b n", k=4, b=3)
                for tg in range(2):
                    t0 = tg * 512
                    mi = tg
                    for br in range(3):
                        bP = brot_()
                        S.mm(ps[:, bP, :], [(bv[:, kc, br, :], obr[br][:, kc, t0:t0 + 512]) for kc in range(4)],
                             [WBm] + OBR[br].s(range(4), t0, t0 + 512), [PB[bP]])
                        bG = brot_()
                        S.mm(ps[:, bG, :], [(gv[:, kc, br, :], hT[:, kc, t0:t0 + 512]) for kc in range(8)],
                             [WBm] + HT.s(range(8), t0, t0 + 512), [PB[bG]])
                        gi = grot()
                        S.act(sgt[gi], ps[:, bG, :], AF.Sigmoid, [PB[bG]], [SGT[gi]])
                        if br == 0:
                            S.tt(mac[mi], sgt[gi], ps[:, bP, :], ALU.mult, [SGT[gi], PB[bP]], [MAC[mi]])
                        else:
                            S.tt(mtp[mi], sgt[gi], ps[:, bP, :], ALU.mult, [SGT[gi], PB[bP]], [MTP[mi]])
                            if br == 1:
                                S.tt(mac[mi], mac[mi], mtp[mi], ALU.add, [MTP[mi]], [MAC[mi]])
                            else:
                                S.tt(mg[:, j, t0:t0 + 512], mac[mi], mtp[mi], ALU.add, [MTP[mi], MAC[mi]],
                                     MG.s(j, t0, t0 + 512))
                    if j >= 1:
                        drain("merge", 2)
            wos = []
            for io in range(2):
                wbo, WBo = load_w(l, 14 + io, 4096)
                wos.append((slabview(wbo, 0, 8, 512), WBo))
            for tg in range(2):
                t0 = tg * 512
                for i in range(8):
                    wo, WBo = wos[i // 4]
                    ii = i % 4
                    b = brot_()
                    S.mm(ps[:, b, :], [(wo[:, kc, ii * 128:(ii + 1) * 128], mg[:, kc, t0:t0 + 512]) for kc in range(8)],
                         [WBo] + MG.s(range(8), t0, t0 + 512), [PB[b]])
                    S.stt(xT[:, i, t0:t0 + 512], ps[:, b, :], md[:, 16 + i:17 + i], xT[:, i, t0:t0 + 512],
                          ALU.mult, ALU.add, [PB[b], MOD], XT.s(i, t0, t0 + 512))
                    drain("merge", 2)

            drain("merge", -1)

            S.stt(abT[:, 1, :], md[:, 32:40], 1.0, pp[:, P0 + 8:P0 + 16], ALU.add, ALU.mult, [MOD, CONST], [ABB])
            norm_mod(abT[:, 1, :], md[:, 24:32], brot_, MOD)
            AR.reset()
            aT = AR.alloc([NFF, NT], BF16)
            acc = [AR.alloc([NT], F32) for _ in range(2)]
            gl = [AR.alloc([NT], F32) for _ in range(2)]
            ACT_ = TB("aT", NFF)
            ACC = [Buf("acc%d" % i) for i in range(2)]
            GL = [Buf("gl%d" % i) for i in range(2)]
            CW = P0 + 67
            CBc = P0 + 67 + 66

            def segs(ap2d, lo, hi):
                return ap2d.rearrange("p (s l) -> p s l", s=nseq)[:, :, lo:hi]
            for cp in range(11):
                wbu, WBu = load_w(l, 16 + cp, 4096)
                wa, wv = slabview(wbu, 0, 8, 256), slabview(wbu, 2048, 8, 256)
                for cc in range(2):
                    c = 2 * cp + cc
                    a0, v0 = (0, 4) if c % 2 == 0 else (2, 6)
                    for tg in range(2):
                        t0 = tg * 512
                        if cp == 0 and cc == 0:
                            S.mm_split(ps[:, a0 + tg, :], [(wa[:, kc, cc * 128:(cc + 1) * 128], hT[:, kc, t0:t0 + 512]) for kc in range(8)],
                                       [HT.s(kc, t0, t0 + 512) for kc in range(8)], [WBu], [PB[a0 + tg]])
                        else:
                            S.mm(ps[:, a0 + tg, :], [(wa[:, kc, cc * 128:(cc + 1) * 128], hT[:, kc, t0:t0 + 512]) for kc in range(8)],
                                 [WBu] + HT.s(range(8), t0, t0 + 512), [PB[a0 + tg]])
                        S.mm(ps[:, v0 + tg, :], [(wv[:, kc, cc * 128:(cc + 1) * 128], hT[:, kc, t0:t0 + 512]) for kc in range(8)],
                             [WBu] + HT.s(range(8), t0, t0 + 512), [PB[v0 + tg]])
                    pa = ps[:, a0:a0 + 2, :].rearrange("p b n -> p (b n)")
                    pv_ = ps[:, v0:v0 + 2, :].rearrange("p b n -> p (b n)")
                    PA = [PB[a0], PB[a0 + 1]]
                    PV = [PB[v0], PB[v0 + 1]]
                    i = c % 2
                    S.act(acc[i], pa, AF.Identity, PA + [CONST], [ACC[i]],
                          bias=pp[:, CBc + c:CBc + c + 1], scale=pp[:, CW + 22 + c:CW + 22 + c + 1])
                    S.stt(segs(acc[i], 1, L), segs(pa, 0, L - 1), pp[:, CW + c:CW + c + 1], segs(acc[i], 1, L),
                          ALU.mult, ALU.add, PA + [CONST], [ACC[i]])
                    S.stt(segs(acc[i], 0, L - 1), segs(pa, 1, L), pp[:, CW + 44 + c:CW + 44 + c + 1],
                          segs(acc[i], 0, L - 1), ALU.mult, ALU.add, PA + [CONST], [ACC[i]])
                    S.act(gl[i], acc[i], AF.Gelu_apprx_tanh, [ACC[i]], [GL[i]])
                    S.tt(aT[:, c, :], gl[i], pv_, ALU.mult, [GL[i]] + PV, ACT_.s(c))
            for i in range(8):
                wbd, WBd = load_w(l, 27 + i, 2816)
                wd = slabview(wbd, 0, NFF, 128)
                for tg in range(2):
                    t0 = tg * 512
                    b = brot_()
                    S.mm(ps[:, b, :], [(wd[:, kc, :], aT[:, kc, t0:t0 + 512]) for kc in range(NFF)],
                         [WBd] + ACT_.s(range(NFF), t0, t0 + 512), [PB[b]])
                    S.stt(xT[:, i, t0:t0 + 512], ps[:, b, :], md[:, 40 + i:41 + i], xT[:, i, t0:t0 + 512],
                          ALU.mult, ALU.add, [PB[b], MOD], XT.s(i, t0, t0 + 512))

        for half in range(2):
            for c in range(8):
                S.dma("sp", xT[:, c, :], xin[half, :, c, :], writes=XT.s(c))
            for l in range(2):
                layer_pass(half, l)
            for c in range(8):
                out_toks.append(S.dma("sp", yout[half, :, c, :], xT[:, c, :], reads=XT.s(c)))
        S.wait_all("sp", out_toks)
        with nc.Block() as block:
            S.run(block)
    return nc


def _consts():
    bf = ml_dtypes.bfloat16
    cb = np.zeros((128, 768), np.float32)
    cb[:, 0:128] = 1.0 / 1024
    cb[:, 128:256] = 1.0 / 128
    for blk in range(2):
        cb[blk * 64:(blk + 1) * 64, 256 + blk * 64:256 + (blk + 1) * 64] = 1.0 / 64
    p = np.arange(128)
    cb[p ^ 16, 384 + p] = 1.0
    cidx = np.arange(128)[:, None] * np.arange(128)[None, :]
    cb[:, 512:640] = np.cos(2 * np.pi * cidx / 128) / np.sqrt(128)
    cb[:, 640:768] = np.sin(2 * np.pi * cidx / 128) / np.sqrt(128)
    d = p % 64
    axis, half, f = d // 32, (d % 32) // 16, d % 16
    inv = (10000.0 ** (-np.arange(16, dtype=np.float32) / 16)).astype(np.float32)
    tok = np.arange(NT)
    row_id, col_id = (tok // 64).astype(np.float32), (tok % 64).astype(np.float32)
    pos = np.where(axis[:, None] == 0, row_id[None, :], col_id[None, :]).astype(np.float32)
    ang = (pos * inv[f][:, None]).astype(np.float32)
    rope = np.zeros((128, 2, NT), np.float32)
    rope[:, 0, :] = np.cos(ang)
    rope[:, 1, :] = np.sin(ang) * np.where(half == 0, -1.0, 1.0)[:, None]

    def dft(Ln):
        n = np.arange(Ln)
        a = 2 * np.pi * ((n[:, None] * n[None, :]) % Ln) / Ln
        return np.cos(a) / np.sqrt(Ln), -np.sin(a) / np.sqrt(Ln)
    c256, s256 = dft(256)
    dft256 = np.zeros((128, 2, 2, 256), np.float32)
    for nt in range(2):
        dft256[:, 0, nt, :] = c256[nt * 128:(nt + 1) * 128, :]
        dft256[:, 1, nt, :] = s256[nt * 128:(nt + 1) * 128, :]
    c1k, s1k = dft(1024)
    dft1024 = np.zeros((4, 128, 8, 512), np.float32)
    for hf in range(2):
        for nt in range(8):
            dft1024[2 * hf, :, nt, :] = c1k[nt * 128:(nt + 1) * 128, hf * 512:(hf + 1) * 512]
            dft1024[2 * hf + 1, :, nt, :] = s1k[nt * 128:(nt + 1) * 128, hf * 512:(hf + 1) * 512]
    dd = (np.arange(TW)[None, :] - TOFF - np.arange(128)[:, None]).astype(np.float32)
    dtab = np.stack([np.maximum(dd, 0), np.maximum(-dd, 0), (dd == 0).astype(np.float32)]).astype(np.float32)
    n1 = np.stack([np.broadcast_to(tok + 1.0, (128, NT)), np.broadcast_to(1024.0 - tok, (128, NT))]).astype(np.float32)
    epp = np.zeros((128, 4), np.float32)
    for i in range(2):
        epp[:, i * 2 + 0] = 255 - (i * 128 + p)
        epp[:, i * 2 + 1] = i * 128 + p
    return dict(cb16=cb.astype(bf), rope=rope, dft256=dft256.astype(bf), dft1024=dft1024.astype(bf),
                dtab=dtab, n1tab=n1, epp=epp)


def _kcv(w, c0, n):
    kc = w.shape[0] // 128
    return w[:, c0:c0 + n].reshape(kc, 128, n).transpose(1, 0, 2)


def _pack_weights(w_ada, w_in, w_bra, w_brr, w_brf, w_out, w_up, w_down):
    wpk = np.zeros((2, NSLAB, 128, SLAB), np.float32)
    wad = np.zeros((2, 12, 128, 4096), np.float32)
    for l in range(2):
        wi = w_in[l]
        put = lambda idx, arr: wpk[l, idx, :, :arr.reshape(128, -1).shape[1]].__setitem__(slice(None), arr.reshape(128, -1))
        put(0, _kcv(wi, 0, 512))
        kvp = np.concatenate([_kcv(wi, 512, 64), _kcv(wi, 512, 64), _kcv(wi, 576, 64), _kcv(wi, 576, 64),
                              _kcv(wi, 640, 128)], axis=2)
        put(1, kvp)
        put(2, _kcv(wi, 768, 512))
        put(3, _kcv(wi, 1280, 512))
        put(4, _kcv(wi, 1792, 512))
        put(5, _kcv(wi, 2304, 512))
        brs = [w_bra[l], w_brr[l], w_brf[l]]
        for j in range(8):
            g = np.stack([_kcv(wi, 2816 + br * 1024 + j * 128, 128) for br in range(3)], axis=2)
            b = np.stack([_kcv(brs[br], j * 128, 128) for br in range(3)], axis=2)
            wpk[l, 6 + j, :, 0:3072] = g.reshape(128, -1)
            wpk[l, 6 + j, :, 3072:4608] = b.reshape(128, -1)
        for io in range(2):
            put(14 + io, _kcv(w_out[l], io * 512, 512))
        for cp in range(11):
            wpk[l, 16 + cp, :, 0:2048] = _kcv(w_up[l], cp * 256, 256).reshape(128, -1)
            wpk[l, 16 + cp, :, 2048:4096] = _kcv(w_up[l], D_FF + cp * 256, 256).reshape(128, -1)
        for i in range(8):
            put(27 + i, _kcv(w_down[l], i * 128, 128))
        for sidx in range(12):
            wad[l, sidx] = _kcv(w_ada[l], sidx * 512, 512).reshape(128, -1)
    return dict(wpk=wpk, wadapk=wad)


def _fm(v):
    return np.ascontiguousarray(v.reshape(-1, 128).T)


_CACHE = {}


def kernel(x_prompt, x_sample, cache_k, cache_v, state_ret_fwd, state_ret_bwd, c, c_ctx,
           w_ada, b_ada, norm1, w_in, q_norm, k_norm, ret_decay_f, ret_decay_b, ret_norm,
           w_br_att, w_br_ret, w_br_four, w_out, norm2, w_up, conv_w, conv_b, w_down):
    f32 = np.float32
    A = lambda a: np.ascontiguousarray(np.asarray(a, dtype=f32))
    x_prompt, x_sample, cache_k, cache_v = A(x_prompt), A(x_sample), A(cache_k), A(cache_v)
    state_ret_fwd, state_ret_bwd, c, c_ctx = A(state_ret_fwd), A(state_ret_bwd), A(c), A(c_ctx)
    if "nc" not in _CACHE:
        _CACHE["nc"] = build_program()
        _CACHE["consts"] = _consts()
    nc = _CACHE["nc"]
    consts = _CACHE["consts"]
    shared = _pack_weights(A(w_ada), A(w_in), A(w_br_att), A(w_br_ret), A(w_br_four), A(w_out), A(w_up), A(w_down))
    shared.update(consts)
    pp = np.zeros((128, 2 * PPL + 24), f32)
    rdf, rdb = A(ret_decay_f), A(ret_decay_b)
    for l in range(2):
        o = l * PPL
        pp[:, o:o + 8] = _fm(A(norm1)[l])
        pp[:, o + 8:o + 16] = _fm(A(norm2)[l])
        pp[:, o + 16:o + 64] = _fm(A(b_ada)[l])
        pp[:, o + 64] = np.tile(A(q_norm)[l], 2)
        pp[:, o + 65] = np.tile(A(k_norm)[l], 2)
        pp[:, o + 66] = A(ret_norm)[l]
        cw = A(conv_w)[l]
        for k in range(3):
            pp[:, o + 67 + k * 22:o + 67 + (k + 1) * 22] = _fm(cw[k])
        pp[:, o + 67 + 66:o + 67 + 88] = _fm(A(conv_b)[l])
        for dr, rd in enumerate((rdf, rdb)):
            for h in range(4):
                pp[:, 2 * PPL + l * 8 + dr * 4 + h] = rd[l, h]
            for j in range(2):
                pp[0:64, 2 * PPL + 16 + l * 4 + dr * 2 + j] = rd[l, 2 * j]
                pp[64:128, 2 * PPL + 16 + l * 4 + dr * 2 + j] = rd[l, 2 * j + 1]
    in_maps = []
    for i in range(8):
        m = dict(shared)
        xp = x_prompt[4 * i:4 * i + 4].reshape(NT, D)
        xs = x_sample[i]
        xin = np.stack([xp.reshape(NT, 8, 128).transpose(2, 1, 0), xs.reshape(NT, 8, 128).transpose(2, 1, 0)])
        m["xin"] = np.ascontiguousarray(xin)
        m["condT"] = np.ascontiguousarray(np.stack([_fm(c_ctx), _fm(c[i])], axis=-1))
        m["pp"] = pp
        ck = cache_k[i]
        ckt = ck.transpose(0, 2, 3, 1)
        ckz = np.zeros((2, 2, 2, 128, 512), f32)
        ckz[:, :, 0, 0:64, :] = ckt
        ckz[:, :, 1, 64:128, :] = ckt
        m["ckT"] = ckz
        m["cv"] = np.ascontiguousarray(cache_v[i].reshape(2, 512, 128))
        s0 = np.stack([state_ret_fwd[i], state_ret_bwd[i]], axis=0)
        m["s0"] = np.ascontiguousarray(s0.transpose(1, 0, 3, 2, 4).reshape(2, 128, 4, 128))
        in_maps.append(m)
    res = run_bass_kernel_spmd(nc, in_maps, core_ids=list(range(8)))
    R = res.results
    y_prompt = np.zeros((32, 256, D), f32)
    y_sample = np.zeros((8, 1024, D), f32)
    nck = np.zeros((32, 2, 256, 2, 64), f32)
    ncv = np.zeros((32, 2, 256, 2, 64), f32)
    nsf = np.zeros((32, 2, 4, 64, 128), f32)
    nsb = np.zeros((32, 2, 4, 64, 128), f32)
    for i in range(8):
        r = R[i]
        yo = np.asarray(r["yout"], f32)
        yt = yo.transpose(0, 3, 2, 1).reshape(2, NT, D)
        y_prompt[4 * i:4 * i + 4] = yt[0].reshape(4, 256, D)
        y_sample[i] = yt[1]
        cko = np.asarray(r["ck_out"], f32).reshape(2, 2, 64, 4, 256)
        nck[4 * i:4 * i + 4] = cko.transpose(3, 0, 4, 1, 2)
        cvo = np.asarray(r["cv_out"], f32).reshape(2, 4, 256, 2, 64)
        ncv[4 * i:4 * i + 4] = cvo.transpose(1, 0, 2, 3, 4)
        nsf[4 * i:4 * i + 4] = np.asarray(r["sf_out"], f32).transpose(1, 0, 2, 3, 4)
        nsb[4 * i:4 * i + 4] = np.asarray(r["sb_out"], f32).transpose(1, 0, 2, 3, 4)
    return (y_prompt, y_sample, nck, ncv, nsf, nsb)
```

```python
import contextlib
import math
import numpy as np
import ml_dtypes
import concourse.bass as bass
import concourse.mybir as mybir
from concourse.bass_utils import run_bass_kernel_spmd

F32 = mybir.dt.float32
BF16 = mybir.dt.bfloat16
AF = mybir.ActivationFunctionType
ALU = mybir.AluOpType

D = 1024
DEPTH = 2
NT = 1024
D_IN = 5888
D_FF = 2816
NFF = 22
EPS = 1e-6
TW = 1920
TOFF = 896
PPL = 155
SLAB = 4608
NWB = 4
NSLAB = 35
SERIALIZE = False


class Buf:
    __slots__ = ("name", "w", "r")

    def __init__(self, name):
        self.name = name
        self.w = None
        self.r = []


class TB:
    def __init__(self, name, nch, ntb=4):
        self.b = [[Buf("%s_%d_%d" % (name, c, t)) for t in range(ntb)] for c in range(nch)]

    def s(self, chunks, t0=0, t1=NT):
        if isinstance(chunks, int):
            chunks = [chunks]
        return [self.b[c][t] for c in chunks for t in range(t0 // 256, (t1 + 255) // 256)]


class Sched:
    ENGS = ("pe", "act", "dve", "pool", "sp")

    def __init__(self, nc, stack, serialize=False, ring=8):
        self.nc = nc
        self.sem = {e: stack.enter_context(nc.semaphore("s_" + e)) for e in self.ENGS}
        self.cnt = {e: 0 for e in self.ENGS}
        self.waited = {e: {} for e in self.ENGS}
        self.prog = {e: [] for e in self.ENGS}
        self.rings = {}
        for q in ("sp", "pool"):
            self.rings[q] = [[stack.enter_context(nc.semaphore("r_%s%d" % (q, i))), 0] for i in range(ring)]
        self.ring_pos = {q: 0 for q in self.rings}
        self.serialize = serialize
        self.last_tok = None
        self.pool_bar = None
        self.skip_self = False

    def _wait(self, eng, tok):
        if tok is None:
            return
        sem, val = tok
        if self.skip_self and sem is self.sem[eng]:
            return
        k = id(sem)
        if self.waited[eng].get(k, 0) >= val:
            return
        self.waited[eng][k] = val
        self.prog[eng].append(("wait", sem, val))

    def _deps(self, eng, reads, writes):
        for b in reads:
            self._wait(eng, b.w)
        for b in writes:
            self._wait(eng, b.w)
            for t in b.r:
                self._wait(eng, t)
        if self.serialize:
            self._wait(eng, self.last_tok)

    def _commit(self, tok, reads, writes):
        for b in reads:
            b.r.append(tok)
            if len(b.r) > 48:
                best = {}
                for s, v in b.r:
                    if best.get(id(s), (None, 0))[1] < v:
                        best[id(s)] = (s, v)
                b.r = list(best.values())
        for b in writes:
            b.w = tok
            b.r = []
        self.last_tok = tok

    def op(self, eng, fn, reads=(), writes=()):
        self._deps(eng, reads, writes)
        self.cnt[eng] += 1
        tok = (self.sem[eng], self.cnt[eng])
        self.prog[eng].append(("op", fn, self.sem[eng]))
        self._commit(tok, reads, writes)
        return tok

    def dma(self, q, out, in_, reads=(), writes=(), arena=False):
        if arena and q == "pool" and self.pool_bar:
            for t in self.pool_bar:
                self._wait(q, t)
            self.pool_bar = None
        self._deps(q, reads, writes)
        ring = self.rings[q]
        i = self.ring_pos[q]
        self.ring_pos[q] = (i + 1) % len(ring)
        sem, n = ring[i]
        if n > 0:
            self._wait(q, (sem, 16 * n))
        ring[i][1] = n + 1
        tok = (sem, 16 * (n + 1))
        self.prog[q].append(("dma", out, in_, sem))
        self._commit(tok, reads, writes)
        return tok

    def mm(self, out, pairs, reads, writes):
        pairs = list(pairs)

        def fn(e):
            n = len(pairs)
            ins = None
            for i, (l, r) in enumerate(pairs):
                ins = e.matmul(out, lhsT=l, rhs=r, start=(i == 0), stop=(i == n - 1))
            return ins
        return self.op("pe", fn, reads, writes)

    def mm_split(self, out, pairs, reads_list, common_reads, writes):
        pairs = list(pairs)
        n = len(pairs)
        for i, (l_, r_) in enumerate(pairs):
            self.skip_self = (i > 0)
            self.op("pe", lambda e, i=i, l_=l_, r_=r_: e.matmul(out, lhsT=l_, rhs=r_, start=(i == 0), stop=(i == n - 1)),
                    list(reads_list[i]) + (list(common_reads) if i == 0 else []), writes)
        self.skip_self = False

    def act(self, out, in_, func, reads, writes, bias=None, scale=None):
        kw = {}
        if bias is not None:
            kw["bias"] = bias
        if scale is not None:
            kw["scale"] = scale
        return self.op("act", lambda e: e.activation(out=out, in_=in_, func=func, **kw), reads, writes)

    def tt(self, out, in0, in1, op, reads, writes, eng="dve"):
        return self.op(eng, lambda e: e.tensor_tensor(out=out, in0=in0, in1=in1, op=op), reads, writes)

    def ts(self, out, in0, s1, s2, op0, op1, reads, writes, eng="dve"):
        if op1 is None:
            return self.op(eng, lambda e: e.tensor_scalar(out=out, in0=in0, scalar1=s1, scalar2=None, op0=op0),
                           reads, writes)
        return self.op(eng, lambda e: e.tensor_scalar(out=out, in0=in0, scalar1=s1, scalar2=s2, op0=op0, op1=op1),
                       reads, writes)

    def stt(self, out, in0, scalar, in1, op0, op1, reads, writes):
        return self.op("dve", lambda e: e.scalar_tensor_tensor(out=out, in0=in0, scalar=scalar, in1=in1,
                                                                op0=op0, op1=op1), reads, writes)

    def copy(self, out, in_, reads, writes, eng="dve"):
        if eng == "act":
            return self.op(eng, lambda e: e.activation(out=out, in_=in_, func=AF.Copy), reads, writes)
        return self.op(eng, lambda e: e.tensor_copy(out=out, in_=in_), reads, writes)

    def barrier(self):
        toks = [(self.sem[e], self.cnt[e]) for e in self.ENGS if self.cnt[e] > 0]
        for q in self.rings:
            for sem, n in self.rings[q]:
                if n > 0:
                    toks.append((sem, 16 * n))
        for e in ("act", "dve", "sp"):
            for t in toks:
                self._wait(e, t)
        self.pool_bar = toks

    def memset(self, ap, val, writes, eng="dve"):
        return self.op(eng, lambda e: e.memset(ap, val), (), writes)

    def wait_all(self, eng, toks):
        for t in toks:
            self._wait(eng, t)

    def run(self, block):
        def mk(eng):
            def body(e):
                for it in self.prog[eng]:
                    if it[0] == "wait":
                        e.wait_ge(it[1], it[2])
                    elif it[0] == "op":
                        it[1](e).then_inc(it[2], 1)
                    else:
                        e.dma_start(out=it[1], in_=it[2]).then_inc(it[3], 16)
            return body
        block.tensor(mk("pe"))
        block.scalar(mk("act"))
        block.vector(mk("dve"))
        block.gpsimd(mk("pool"))
        block.sync(mk("sp"))


class Rot:
    def __init__(self, items):
        self.items = list(items)
        self.i = 0

    def __call__(self):
        v = self.items[self.i % len(self.items)]
        self.i += 1
        return v


class Arena:
    def __init__(self, t, nelem16, sched=None):
        self.sched = sched
        self.t = t
        self.n = nelem16
        self.off = 0

    def reset(self, off=0):
        self.off = off
        if self.sched is not None:
            self.sched.barrier()

    def alloc(self, free, dt):
        n = int(np.prod(free))
        w = n * 2 if dt == F32 else n
        off = (self.off + 31) // 32 * 32
        assert off + w <= self.n, ("arena overflow", off, w, self.n)
        ap = self.t[:, off:off + w]
        if dt == F32:
            ap = ap.bitcast(F32)
        if len(free) == 2:
            ap = ap.rearrange("p (a b) -> p a b", a=free[0])
        elif len(free) == 3:
            ap = ap.rearrange("p (a b c) -> p a b c", a=free[0], b=free[1])
        elif len(free) == 4:
            ap = ap.rearrange("p (a b c d) -> p a b c d", a=free[0], b=free[1], c=free[2])
        self.off = off + w
        return ap


def build_program():
    nc = bass.Bass("TRN2", target_bir_lowering=False)

    def din(name, shape, dt=F32):
        return nc.dram_tensor(name, list(shape), dt, kind="ExternalInput").ap()

    def dout(name, shape, dt=F32):
        return nc.dram_tensor(name, list(shape), dt, kind="ExternalOutput").ap()

    xin = din("xin", [2, 128, 8, NT])
    condT = din("condT", [128, 8, 2])
    ppd = din("pp", [128, 2 * PPL + 24])
    wpk = din("wpk", [2, NSLAB, 128, SLAB])
    wadapk = din("wadapk", [2, 12, 128, 4096])
    ckT = din("ckT", [2, 2, 2, 128, 512])
    cvd = din("cv", [2, 512, 128])
    s0d = din("s0", [2, 128, 4, 128])
    cb16d = din("cb16", [128, 768], BF16)
    roped = din("rope", [128, 2, NT])
    dft256d = din("dft256", [128, 2, 2, 256], BF16)
    dft1024d = din("dft1024", [4, 128, 8, 512], BF16)
    dtabd = din("dtab", [3, 128, TW])
    n1tabd = din("n1tab", [2, 128, NT])
    eppd = din("epp", [128, 4])

    yout = dout("yout", [2, 128, 8, NT])
    ck_out = dout("ck_out", [2, 128, NT])
    cv_out = dout("cv_out", [2, NT, 128])
    sf_out = dout("sf_out", [2, 4, 4, 64, 128])
    sb_out = dout("sb_out", [2, 4, 4, 64, 128])

    tmask_d = nc.dram_tensor("tmask_d", [2, 4, 128, TW], BF16, kind="Internal").ap()
    decq_d = nc.dram_tensor("decq_d", [2, 128, 2, 2, NT], BF16, kind="Internal").ap()

    out_toks = []
    with contextlib.ExitStack() as st:
        S = Sched(nc, st, serialize=SERIALIZE)

        def sb(name, shape, dt):
            return st.enter_context(nc.sbuf_tensor(name, list(shape), dt))

        xT = sb("xT", [128, 8, NT], F32)
        hT = sb("hT", [128, 8, NT], BF16)
        obr = [sb("oatt", [128, 4, NT], BF16), sb("oret", [128, 4, NT], BF16), sb("ofou", [128, 4, NT], BF16)]
        wbuf = [sb("wb%d" % i, [128, SLAB], BF16) for i in range(NWB)]
        rope = sb("rope_t", [128, 2, NT], F32)
        cb16 = sb("cb16_t", [128, 768], BF16)
        pp = sb("pp_t", [128, 2 * PPL + 24], F32)
        modT = sb("modT", [128, 2, 2, 48], F32)
        abT = sb("abT", [128, 2, 8], F32)
        lgb_t = sb("lgb_t", [128, 24], F32)
        kdec = sb("kdec", [128, 2, 2, 8], F32)
        epp = sb("epp_t", [128, 4], F32)
        scond = sb("scond", [128, 8, 2], BF16)
        condt = sb("condt", [128, 8, 2], F32)
        tblf = sb("tblf", [128, 3, 480], F32)
        tble = sb("tble", [128, 4, 512], BF16)
        tbln = sb("tbln", [128, NT], F32)
        lgq = sb("lgq", [128, 8], F32)
        ARN = 37888
        arena_t = sb("arena", [128, ARN], BF16)
        AR = Arena(arena_t, ARN, S)
        ps = st.enter_context(nc.psum_tensor("ps", [128, 8, 512], F32))

        ones1024 = cb16[:, 0:128]
        ones128 = cb16[:, 128:256]
        bd64 = cb16[:, 256:384]
        pswap = cb16[:, 384:512]
        cs128 = cb16[:, 512:768]

        XT = TB("xT", 8)
        HT = TB("hT", 8)
        OBR = [TB("oatt", 4), TB("oret", 4), TB("ofou", 4)]
        WB = [Buf("wb%d" % i) for i in range(NWB)]
        PB = [Buf("ps%d" % i) for i in range(8)]
        CONST = Buf("const")
        ABB = Buf("ab")
        LG = Buf("lg")
        KDEC = Buf("kdec")
        wrot = Rot(range(NWB))

        def load_slab(pieces):
            i = wrot()
            for (dst, src) in pieces(wbuf[i]):
                S.dma("pool", dst, src, reads=[], writes=[WB[i]])
            return wbuf[i], WB[i]

        def load_w(l, idx, n):
            return load_slab(lambda wbt: [(wbt[:, 0:n], wpk[l, idx, :, 0:n])])

        def wcols(w_l, c0, nc_, kc=8):
            return w_l.rearrange("(kc p) n -> p kc n", p=128)[:, :, c0:c0 + nc_]

        def slabview(wb, off, kc, ncols):
            return wb[:, off:off + kc * ncols].rearrange("p (k n) -> p k n", k=kc)

        S.dma("sp", cb16[:], cb16d[:, :], writes=[CONST])
        S.dma("sp", pp[:], ppd[:, :], writes=[CONST])
        S.dma("sp", rope[:], roped[:, :, :], writes=[CONST])
        S.dma("sp", epp[:], eppd[:, :], writes=[CONST])
        S.dma("sp", condt[:], condT[:, :, :], writes=[CONST])

        SC = Buf("scond")
        MODL = [Buf("mod0"), Buf("mod1")]
        S.act(scond[:], condt[:], AF.Silu, [CONST], [SC])
        brot0 = Rot(range(8))

        mod_ext = [None]
        drot = Rot([6, 7])

        def mod_slab(l, sidx, brot):
            if mod_ext[0] is None:
                wb, WBb = load_slab(lambda wbt: [(wbt[:, 0:4096], wadapk[l, sidx])])
            else:
                bufs_, BUFS_, rot_ = mod_ext[0]
                i_ = rot_()
                wb, WBb = bufs_[i_], BUFS_[i_]
                S.dma("pool", wb[:, 0:4096], wadapk[l, sidx], writes=[WBb], arena=True)
                brot = drot
            wv = slabview(wb, 0, 8, 512)
            b = brot()
            for j in range(4):
                S.mm(ps[:, b, 2 * j:2 * j + 2],
                     [(wv[:, kc, j * 128:(j + 1) * 128], scond[:, kc, :]) for kc in range(8)],
                     [WBb, SC], [PB[b]])
            for cnd in range(2):
                S.tt(modT[:, l, cnd, sidx * 4:sidx * 4 + 4],
                     ps[:, b, 0:8].rearrange("p (j c) -> p j c", c=2)[:, :, cnd],
                     pp[:, l * PPL + 16 + sidx * 4: l * PPL + 16 + sidx * 4 + 4], ALU.add,
                     [PB[b], CONST], [MODL[l]])
        RDO = 2 * PPL
        S.act(lgb_t[:], pp[:, RDO:RDO + 24], AF.Exp, [CONST], [LG], scale=-1.0)
        S.act(lgb_t[:], lgb_t[:], AF.Ln, [LG], [LG], bias=1.0, scale=1.0)
        S.ts(lgb_t[:], lgb_t[:], -1.0, None, ALU.mult, None, [LG], [LG])

        def lg_b(l, dr, h):
            return lgb_t[:, l * 8 + dr * 4 + h: l * 8 + dr * 4 + h + 1]

        def lg_p(l, dr, j):
            c = 16 + l * 4 + dr * 2 + j
            return lgb_t[:, c:c + 1]

        import collections
        DCH = Buf("dch")
        DCN = Buf("dcn")
        EB = [Buf("te0"), Buf("te1"), Buf("te2"), Buf("te3")]
        TMDQ = [[[Buf("tmd%d%d%d" % (l, h, q)) for q in range(4)] for h in range(4)] for l in range(2)]
        DQDH = [[Buf("dqd%d_%d" % (l, k_)) for k_ in range(8)] for l in range(2)]
        ering = Rot(range(3))

        def tmask_q(l, h, q, load):
            if load:
                S.dma("sp", tblf[:, :, :], dtabd[:, :, q * 480:(q + 1) * 480].rearrange("a p n -> p a n"),
                      writes=[DCH])
            i = ering()
            e1, e2 = tble[:, i, 0:480], tble[:, 3, 0:480]
            S.act(e1, tblf[:, 0, 0:480], AF.Exp, [DCH, LG], [EB[i]], scale=lg_b(l, 0, h))
            S.act(e2, tblf[:, 1, 0:480], AF.Exp, [DCH, LG], [EB[3]], scale=lg_b(l, 1, h))
            S.tt(e1, e1, e2, ALU.mult, [EB[3]], [EB[i]])
            S.tt(e1, e1, tblf[:, 2, 0:480], ALU.add, [DCH], [EB[i]])
            S.dma("sp", tmask_d[l, h][:, q * 480:(q + 1) * 480], e1, reads=[EB[i]], writes=[TMDQ[l][h][q]])

        S.dma("sp", tbln[:, :], n1tabd[0], writes=[DCN])
        for l_ in range(2):
            for j_ in range(2):
                k_ = l_ * 2 + j_
                S.ts(lgq[:, k_:k_ + 1], lg_p(l_, 1, j_), -1.0, None, ALU.mult, None, [LG], [LG])
                S.ts(lgq[:, 4 + k_:5 + k_], lg_p(l_, 1, j_), 1025.0, None, ALU.mult, None, [LG], [LG])

        def decq_h(l, j, dr, hf):
            i = ering()
            src = tbln[:, hf * 512:(hf + 1) * 512]
            if dr == 0:
                S.act(tble[:, i, :], src, AF.Exp, [DCN, LG], [EB[i]], scale=lg_p(l, 0, j))
            else:
                k_ = l * 2 + j
                S.act(tble[:, i, :], src, AF.Exp, [DCN, LG], [EB[i]], scale=lgq[:, k_:k_ + 1],
                      bias=lgq[:, 4 + k_:5 + k_])
            S.dma("sp", decq_d[l, :, j, dr, hf * 512:(hf + 1) * 512], tble[:, i, :], reads=[EB[i]],
                  writes=[DQDH[l][j * 4 + dr * 2 + hf]])

        def tmask_steps(l):
            return [(lambda l=l, h=h, q=q: tmask_q(l, h, q, h == 0)) for q in range(4) for h in range(4)]

        def decq_steps(l):
            return [(lambda l=l, j=j, dr=dr, hf=hf: decq_h(l, j, dr, hf))
                    for j in range(2) for dr in range(2) for hf in range(2)]
        for l in range(2):
            for i in range(2):
                for dr in range(2):
                    S.act(kdec[:, l, i, dr * 4:dr * 4 + 4], lgb_t[:, l * 8 + dr * 4:l * 8 + dr * 4 + 4], AF.Exp,
                          [LG, CONST], [KDEC], scale=epp[:, i * 2 + dr:i * 2 + dr + 1])
        for sidx in range(4):
            mod_slab(0, sidx, brot0)
        Q = collections.defaultdict(collections.deque)
        cur_brot = [brot0]
        cur_pass = [0]
        for sidx in range(4, 12):
            Q[(0, "att")].append(lambda sidx=sidx: mod_slab(0, sidx, cur_brot[0]))
        t0s = tmask_steps(0)
        Q[(0, "four")].extend(t0s[:8])
        Q[(0, "ret")].extend(t0s[8:])
        ml1 = [(lambda sidx=sidx: mod_slab(1, sidx, cur_brot[0])) for sidx in range(6)]
        t1s = tmask_steps(1)
        while ml1 or t1s:
            for lst in (ml1, t1s, t1s):
                if lst:
                    Q[(0, "merge")].append(lst.pop(0))
        for sidx in range(6, 12):
            Q[(1, "att")].append(lambda sidx=sidx: mod_slab(1, sidx, cur_brot[0]))
        Q[(1, "merge")].extend(decq_steps(0) + decq_steps(1))

        def drain(name, n=1):
            q_ = Q[(cur_pass[0], name)]
            while q_ and n != 0:
                q_.popleft()()
                n -= 1

        def norm_mod(A, Bv, brot_, MOD):
            AR.reset()
            sq = [AR.alloc([8, 512], BF16) for _ in range(2)]
            tmp = [AR.alloc([512], F32) for _ in range(4)]
            SQ = [Buf("sq0"), Buf("sq1")]
            TMP = [Buf("tmp%d" % i) for i in range(4)]
            bks = []
            for tg in range(2):
                t0, t1 = tg * 512, tg * 512 + 512
                S.act(sq[tg], xT[:, :, t0:t1], AF.Square, XT.s(range(8), t0, t1), [SQ[tg]])
            for tg in range(2):
                b = brot_()
                bks.append(b)
                S.mm(ps[:, b, :], [(ones1024, sq[tg][:, c, :]) for c in range(8)], [SQ[tg], CONST], [PB[b]])
                S.act(ps[:, b, :], ps[:, b, :], AF.Ln, [], [PB[b]], bias=EPS, scale=1.0)
                S.act(ps[:, b, :], ps[:, b, :], AF.Exp, [], [PB[b]], scale=-0.5)
            k = 0
            for tg in range(2):
                t0, t1 = tg * 512, tg * 512 + 512
                b = bks[tg]
                for c in range(8):
                    i = k % 4
                    k += 1
                    S.stt(tmp[i], xT[:, c, t0:t1], A[:, c:c + 1], ps[:, b, :], ALU.mult, ALU.mult,
                          XT.s(c, t0, t1) + [PB[b], ABB, MOD], [TMP[i]])
                    S.act(hT[:, c, t0:t1], tmp[i], AF.Identity, [TMP[i], MOD], HT.s(c, t0, t1),
                          bias=Bv[:, c:c + 1], scale=1.0)

        def rstd_from(psb, PBb, out, OUTB):
            S.act(out, psb, AF.Ln, [PBb], [OUTB], bias=EPS, scale=1.0)
            S.act(out, out, AF.Exp, [OUTB], [OUTB], scale=-0.5)

        def layer_pass(half, l):
            smp = (half == 1)
            nseq, L = (1, 1024) if smp else (4, 256)
            ntl = L // 128
            NQ = 512 if smp else 256
            P0 = l * PPL
            md = modT[:, l, half, :]
            MOD = MODL[l]
            defer_mod = (half == 0 and l == 0)
            cur_pass[0] = half * 2 + l
            brot_ = Rot(range(8))
            cur_brot[0] = brot_

            S.stt(abT[:, 0, :], md[:, 8:16], 1.0, pp[:, P0:P0 + 8], ALU.add, ALU.mult, [MOD, CONST], [ABB])

            norm_mod(abT[:, 0, :], md[:, 0:8], brot_, MOD)

            def ph_att():
                AR.reset()
                qT = AR.alloc([4, NT], BF16)
                nkeys = 1536 if smp else 1024
                KT = AR.alloc([2, 2, nkeys], BF16)
                nvt = 12 if smp else 8
                VA = AR.alloc([nvt, 2, 128], BF16)
                etb = [AR.alloc([512], BF16) for _ in range(4)]
                sqb = [AR.alloc([512], BF16) for _ in range(4)]
                rsb = [AR.alloc([512], F32) for _ in range(4)]
                if smp:
                    qnb = [AR.alloc([512], BF16) for _ in range(4)]
                    t1b = [AR.alloc([512], F32) for _ in range(3)]
                recb = [AR.alloc([512], F32) for _ in range(2)]
                if not smp:
                    kst = AR.alloc([2, NT], F32)
                    vst = AR.alloc([8, 128], F32)
                    KST, VST = Buf("kst"), Buf("vst")
                if Q[(cur_pass[0], "att")]:
                    mod_ext[0] = ([AR.alloc([4096], BF16) for _ in range(2)], [Buf("mx%d" % k_) for k_ in range(2)], Rot(range(2)))
                QT = TB("qT", 4)
                KTB = TB("KT", 2, 6)
                VAB = [Buf("va%d" % i) for i in range(nvt)]
                ETB = [Buf("et%d" % i) for i in range(4)]
                SQB = [Buf("sqb%d" % i) for i in range(4)]
                RSB = [Buf("rsb%d" % i) for i in range(4)]
                QNB = [Buf("qnb%d" % i) for i in range(4)]
                T1B = [Buf("t1b%d" % i) for i in range(4)]
                T2B = [Buf("t2b%d" % i) for i in range(4)]
                RECB = [Buf("rec%d" % i) for i in range(2)]
                rr = Rot(range(2))


                wbq, WBq = load_w(l, 0, 4096)
                wq = slabview(wbq, 0, 8, 512)

                wbk, WBk = load_w(l, 1, 3072)
                wk = slabview(wbk, 0, 8, 384)
                if smp:
                    for kv in range(2):
                        for z in range(2):
                            S.dma("pool", KT[:, kv, z, 1024:1536], ckT[l, kv, z], writes=KTB.s(kv, 1024, 1536), arena=True)
                    for kv in range(2):
                        S.dma("pool", VA[:, 8:12, kv, 0:64],
                              cvd[l].rearrange("(tt p) f -> p tt f", p=128)[:, :, kv * 64:(kv + 1) * 64],
                              writes=VAB[8:12], arena=True)

                units = []
                for tg in range(2):
                    t0 = tg * 512
                    for j in range(4):
                        units.append(dict(w=wq, WBw=WBq, c0=j * 128, g=P0 + 64, out=qT[:, j, t0:t0 + 512],
                                          OUTB=QT.s(j, t0, t0 + 512), t0=t0, kst=None))
                    for kv in range(2):
                        units.append(dict(w=wk, WBw=WBk, c0=kv * 128, g=P0 + 65, out=None, kv=kv,
                                          OUTB=KTB.s(kv, t0, t0 + 512), t0=t0,
                                          kst=(None if smp else kst[:, kv, t0:t0 + 512])))
                for ui, u in enumerate(units):
                    u["i"] = ui % 4
                    u["i3"] = ui % 3
                    u["split"] = ui in (0, 1, 6, 7)

                rotA, rotB, rotC = Rot([0, 1, 2, 3]), Rot([4, 5]), Rot([6, 7])

                def stA(u):
                    t0, c0, w = u["t0"], u["c0"], u["w"]
                    b = rotA()
                    u["b"] = b
                    if u["split"]:
                        S.mm_split(ps[:, b, :], [(w[:, kc, c0:c0 + 128], hT[:, kc, t0:t0 + 512]) for kc in range(8)],
                                   [HT.s(kc, t0, t0 + 512) for kc in range(8)], [u["WBw"]], [PB[b]])
                    else:
                        S.mm(ps[:, b, :], [(w[:, kc, c0:c0 + 128], hT[:, kc, t0:t0 + 512]) for kc in range(8)],
                             [u["WBw"]] + HT.s(range(8), t0, t0 + 512), [PB[b]])
                    S.act(sqb[u["i"]], ps[:, b, :], AF.Square, [PB[b]], [SQB[u["i"]]])

                def stB(u):
                    i, b, g = u["i"], u["b"], u["g"]
                    b2 = rotB()
                    S.mm(ps[:, b2, :], [(bd64, sqb[i])], [SQB[i], CONST], [PB[b2]])
                    rstd_from(ps[:, b2, :], PB[b2], rsb[i], RSB[i])
                    if smp:
                        S.stt(ps[:, b, :], ps[:, b, :], pp[:, g:g + 1], rsb[i], ALU.mult, ALU.mult,
                              [RSB[i], CONST], [PB[b]])
                        S.copy(qnb[i], ps[:, b, :], [PB[b]], [QNB[i]], eng="act")
                    elif u["kst"] is not None:
                        S.stt(u["kst"], ps[:, b, :], pp[:, g:g + 1], rsb[i], ALU.mult, ALU.mult,
                              [PB[b], RSB[i], CONST], [KST])
                        for z in range(2):
                            S.copy(KT[z * 64:(z + 1) * 64, u["kv"], z, u["t0"]:u["t0"] + 512], u["kst"][z * 64:(z + 1) * 64, :],
                                   [KST], u["OUTB"], eng=("act" if z == 0 else "dve"))
                    else:
                        S.stt(u["out"], ps[:, b, :], pp[:, g:g + 1], rsb[i], ALU.mult, ALU.mult,
                              [PB[b], RSB[i], CONST], u["OUTB"])

                def stC(u):
                    if not smp:
                        return
                    i, t0 = u["i"], u["t0"]
                    i3 = u["i3"]
                    b = u["b"]
                    S.tt(t1b[i3], ps[:, b, :], rope[:, 0, t0:t0 + 512], ALU.mult, [PB[b], CONST], [T1B[i3]])
                    b3 = rotC()
                    S.mm(ps[:, b3, :], [(pswap, qnb[i])], [QNB[i], CONST], [PB[b3]])
                    S.tt(ps[:, b3, :], ps[:, b3, :], rope[:, 1, t0:t0 + 512], ALU.mult, [CONST], [PB[b3]])
                    if u["out"] is not None:
                        S.tt(u["out"], t1b[i3], ps[:, b3, :], ALU.add, [T1B[i3], PB[b3]], u["OUTB"])
                    else:
                        for z in range(2):
                            S.tt(KT[z * 64:(z + 1) * 64, u["kv"], z, t0:t0 + 512], t1b[i3][z * 64:(z + 1) * 64, :],
                                 ps[z * 64:(z + 1) * 64, b3, :], ALU.add, [T1B[i3], PB[b3]], u["OUTB"])
                stages = [stA, stB, stC]
                for step in range(len(units) + len(stages) - 1):
                    for si, stg in enumerate(stages):
                        ui = step - si
                        if 0 <= ui < len(units):
                            stg(units[ui])
                    if step == 2:
                        S.memset(KT[:, :, :, 0:1024], 0.0, KTB.s([0, 1], 0, 1024))
                        S.memset(VA[:, :, :, 64:128], 1.0, VAB)
                    if step >= 4:
                        drain("att", 1)
                if not smp:
                    for kv in range(2):
                        out_toks.append(S.dma("sp", ck_out[l, kv * 64:(kv + 1) * 64, :], kst[0:64, kv, :], reads=[KST]))
                for tt in range(8):
                    b = brot_()
                    S.mm(ps[:, b, 0:128], [(hT[:, kc, tt * 128:(tt + 1) * 128], wk[:, kc, 256:384]) for kc in range(8)],
                         [WBk] + HT.s(range(8), tt * 128, tt * 128 + 128), [PB[b]])
                    S.copy(VA[:, tt, :, 0:64], ps[:, b, 0:128].rearrange("p (k d) -> p k d", k=2), [PB[b]], [VAB[tt]])
                    if not smp:
                        S.copy(vst[:, tt, :], ps[:, b, 0:128], [PB[b]], [VST], eng="act")
                if not smp:
                    out_toks.append(S.dma("sp", cv_out[l].rearrange("(tt p) f -> p tt f", p=128), vst, reads=[VST]))

                orot = Rot([6, 7])
                srot = Rot([0, 1, 2, 3, 4, 5])
                erot = Rot(range(4))
                scale = 64 ** -0.5
                if smp:
                    sbanks = [Rot([0, 1]), Rot([2, 3])]
                    obanks = Rot([4, 6])
                    e4 = Rot(range(4))
                    for h in range(8):
                        kv, z, ch = h // 4, h % 2, h // 2
                        base = z * 64
                        ob0 = obanks()
                        obs = [ob0, ob0 + 1]

                        def s_exp(st, kt, kv=kv, z=z, ch=ch):
                            sbk = sbanks[st]()
                            q0 = st * 512
                            S.mm(ps[:, sbk, :], [(KT[:, kv, z, kt * 128:(kt + 1) * 128], qT[:, ch, q0:q0 + 512])],
                                 KTB.s(kv, kt * 128, kt * 128 + 128) + QT.s(ch, q0, q0 + 512), [PB[sbk]])
                            ei = e4()
                            S.act(etb[ei], ps[:, sbk, :], AF.Exp, [PB[sbk]], [ETB[ei]], scale=scale)
                            return ei

                        def pv(st, kt, ei, kv=kv, obs=obs):
                            ob = obs[st]
                            S.op("pe", lambda e: e.matmul(ps[:, ob, :], lhsT=VA[:, kt, kv, :], rhs=etb[ei],
                                                          start=(kt == 0), stop=(kt == 11)),
                                 [VAB[kt], ETB[ei]], [PB[ob]])
                        cur = [s_exp(0, 0), s_exp(1, 0)]
                        for kt in range(12):
                            for st in range(2):
                                nxt = s_exp(st, kt + 1) if kt + 1 < 12 else None
                                pv(st, kt, cur[st])
                                cur[st] = nxt
                        for st in range(2):
                            ob, q0 = obs[st], st * 512
                            ri = rr()
                            S.op("dve", lambda e, ri=ri, ob=ob: e.reciprocal(out=recb[ri][64:128, :], in_=ps[64:128, ob, :]),
                                 [PB[ob]], [RECB[ri]])
                            S.tt(obr[0][base:base + 64, ch, q0:q0 + 512], ps[0:64, ob, :], recb[ri][64:128, :],
                                 ALU.mult, [PB[ob], RECB[ri]], OBR[0].s(ch, q0, q0 + 512))
                else:
                    aunits = [(sq_, p_) for sq_ in range(4) for p_ in range(4)]

                    def atA(u):
                        sq_, p_ = u
                        kv, ch, q0 = p_ // 2, p_, sq_ * 256
                        eis = []
                        for kt in range(2):
                            k0 = sq_ * 256 + kt * 128
                            sbk = srot()

                            def smm(e, sbk=sbk, kv=kv, ch=ch, q0=q0, k0=k0):
                                e.matmul(ps[:, sbk, 0:256], lhsT=KT[:, kv, 0, k0:k0 + 128], rhs=qT[:, ch, q0:q0 + 256],
                                         start=True, stop=True)
                                return e.matmul(ps[:, sbk, 256:512], lhsT=KT[:, kv, 1, k0:k0 + 128],
                                                rhs=qT[:, ch, q0:q0 + 256], start=True, stop=True)
                            S.op("pe", smm, KTB.s(kv, k0, k0 + 128) + QT.s(ch, q0, q0 + 256), [PB[sbk]])
                            ei = erot()
                            S.act(etb[ei], ps[:, sbk, :], AF.Exp, [PB[sbk]], [ETB[ei]], scale=scale)
                            eis.append(ei)
                        return eis

                    def atB(u, eis):
                        sq_, p_ = u
                        kv, ch, q0 = p_ // 2, p_, sq_ * 256
                        ob = orot()

                        def pvm(e, ob=ob, kv=kv, sq_=sq_, eis=eis):
                            ins = None
                            for z in range(2):
                                for kt in range(2):
                                    ins = e.matmul(ps[:, ob, z * 256:(z + 1) * 256], lhsT=VA[:, 2 * sq_ + kt, kv, :],
                                                   rhs=etb[eis[kt]][:, z * 256:(z + 1) * 256], start=(kt == 0), stop=(kt == 1))
                            return ins
                        S.op("pe", pvm, [VAB[2 * sq_], VAB[2 * sq_ + 1], ETB[eis[0]], ETB[eis[1]]], [PB[ob]])
                        ri = rr()
                        S.act(recb[ri][64:128, :], ps[64:128, ob, :], AF.Ln, [PB[ob]], [RECB[ri]])
                        S.act(recb[ri][64:128, :], recb[ri][64:128, :], AF.Exp, [RECB[ri]], [RECB[ri]], scale=-1.0)
                        for z in range(2):
                            S.tt(obr[0][z * 64:(z + 1) * 64, ch, q0:q0 + 256], ps[0:64, ob, z * 256:(z + 1) * 256],
                                 recb[ri][64:128, z * 256:(z + 1) * 256], ALU.mult, [PB[ob], RECB[ri]],
                                 OBR[0].s(ch, q0, q0 + 256))
                    prev = None
                    for u in aunits:
                        drain("att", 1)
                        eis = atA(u)
                        if prev is not None:
                            atB(*prev)
                        prev = (u, eis)
                    atB(*prev)
                drain("att", -1)

            def ph_ret():
                AR.reset()
                qr = AR.alloc([2, NT], BF16)
                kr = AR.alloc([2, 2, NT], BF16)
                vr = AR.alloc([8, 512], BF16)
                sg = AR.alloc([4, NT], BF16)
                atb = [AR.alloc([512], BF16) for _ in range(4)]
                tmk = [AR.alloc([TW], BF16) for _ in range(2)]
                sqb = [AR.alloc([512], BF16) for _ in range(2)]
                rsb = [AR.alloc([512], F32) for _ in range(2)]
                t1b = [AR.alloc([512], F32) for _ in range(2)]
                QR, KR = TB("qr", 2), TB("kr", 2)
                VRB = [Buf("vr%d" % i) for i in range(8)]
                SG = TB("sg", 4)
                ATB = [Buf("at%d" % i) for i in range(4)]
                TMK = [Buf("tmk%d" % i) for i in range(2)]
                wb1, WB1 = load_w(l, 2, 4096)
                wb2, WB2 = load_w(l, 3, 4096)
                wb3, WB3 = load_w(l, 4, 4096)
                if smp:
                    qd = AR.alloc([4, NT], BF16)
                    dqt = AR.alloc([2, 2, NT], BF16)
                    s0t = AR.alloc([4, 128], BF16)
                    QD, DQT, S0B = TB("qd", 4), Buf("dqt"), Buf("s0")
                    S.dma("sp", dqt, decq_d[l], reads=DQDH[l], writes=[DQT])
                    S.dma("pool", s0t, s0d[l], writes=[S0B], arena=True)
                else:
                    ktm = AR.alloc([8, 4, 128], BF16)
                    sst = [AR.alloc([512], F32) for _ in range(2)]
                    KTM = [Buf("ktm%d" % i) for i in range(8)]
                    SST = [Buf("sst%d" % i) for i in range(2)]
                S.memset(kr, 0.0, KR.s([0, 1]))

                w1 = slabview(wb1, 0, 8, 512)
                for tg in range(2):
                    for j in range(2):
                        t0 = tg * 512
                        b = brot_()
                        S.mm(ps[:, b, :], [(w1[:, kc, j * 128:(j + 1) * 128], hT[:, kc, t0:t0 + 512]) for kc in range(8)],
                             [WB1] + HT.s(range(8), t0, t0 + 512), [PB[b]])
                        S.copy(qr[:, j, t0:t0 + 512], ps[:, b, :], [PB[b]], QR.s(j, t0, t0 + 512), eng="act")
                        b = brot_()
                        S.mm(ps[:, b, :], [(w1[:, kc, 256 + j * 128:256 + (j + 1) * 128], hT[:, kc, t0:t0 + 512])
                                           for kc in range(8)],
                             [WB1] + HT.s(range(8), t0, t0 + 512), [PB[b]])
                        for z in range(2):
                            S.ts(kr[z * 64:(z + 1) * 64, j, z, t0:t0 + 512], ps[z * 64:(z + 1) * 64, b, :], 0.125, None,
                                 ALU.mult, None, [PB[b]], KR.s(j, t0, t0 + 512))
                if not smp:
                    for tt in range(8):
                        b = brot_()
                        S.mm(ps[:, b, 0:256], [(hT[:, kc, tt * 128:(tt + 1) * 128], w1[:, kc, 256:512]) for kc in range(8)],
                             [WB1] + HT.s(range(8), tt * 128, tt * 128 + 128), [PB[b]])
                        for dr in range(2):
                            for h in range(4):
                                S.ts(ktm[:, tt, h, dr * 64:(dr + 1) * 64], ps[:, b, h * 64:(h + 1) * 64],
                                     kdec[:, l, tt % 2, dr * 4 + h:dr * 4 + h + 1], 0.125, ALU.mult, ALU.mult,
                                     [PB[b], KDEC], [KTM[tt]])
                w2 = slabview(wb2, 0, 8, 512)
                for tt in range(8):
                    b = brot_()
                    S.mm(ps[:, b, :], [(hT[:, kc, tt * 128:(tt + 1) * 128], w2[:, kc, :]) for kc in range(8)],
                         [WB2] + HT.s(range(8), tt * 128, tt * 128 + 128), [PB[b]])
                    S.copy(vr[:, tt, :], ps[:, b, :], [PB[b]], [VRB[tt]], eng=("act" if tt % 2 else "dve"))
                    drain("ret", 1)
                w3 = slabview(wb3, 0, 8, 512)
                for h in range(4):
                    for tg in range(2):
                        t0 = tg * 512
                        b = brot_()
                        S.mm(ps[:, b, :], [(w3[:, kc, h * 128:(h + 1) * 128], hT[:, kc, t0:t0 + 512]) for kc in range(8)],
                             [WB3] + HT.s(range(8), t0, t0 + 512), [PB[b]])
                        S.act(sg[:, h, t0:t0 + 512], ps[:, b, :], AF.Silu, [PB[b]], SG.s(h, t0, t0 + 512))
                        drain("ret", 1)
                if smp:
                    for h in range(4):
                        j, bs = h // 2, (h % 2) * 64
                        for dr in range(2):
                            S.tt(qd[dr * 64:(dr + 1) * 64, h, :], qr[bs:bs + 64, j, :], dqt[bs:bs + 64, j, dr, :], ALU.mult,
                                 QR.s(j) + [DQT], QD.s(h))

                drain("ret", -1)
                if smp:
                    obanks = Rot([4, 6])
                    rsb_ = [Rot([0, 1]), Rot([2, 3])]
                    mrot = Rot([0, 2, 1, 3])
                else:
                    orot = Rot([6, 7])
                    srot = Rot([0, 1, 2, 3])
                    mrot = Rot([4, 5])
                arot = Rot(range(4))
                trot = Rot(range(2))
                pend_ret = [None]
                rr = Rot(range(2))

                def post_norm(ob, h, n0, W, b2=None):
                    i = rr()
                    S.act(sqb[i][:, 0:W], ps[:, ob, 0:W], AF.Square, [PB[ob]], [SQB[i]])
                    if b2 is None:
                        b2 = mrot()
                    S.mm(ps[:, b2, 0:W], [(ones128, sqb[i][:, 0:W])], [SQB[i], CONST], [PB[b2]])
                    rstd_from(ps[:, b2, 0:W], PB[b2], rsb[i][:, 0:W], RSB[i])
                    S.stt(t1b[i][:, 0:W], ps[:, ob, 0:W], pp[:, P0 + 66:P0 + 67], rsb[i][:, 0:W],
                          ALU.mult, ALU.mult, [PB[ob], RSB[i], CONST], [T1B[i]])
                    S.tt(obr[1][:, h, n0:n0 + W], t1b[i][:, 0:W], sg[:, h, n0:n0 + W], ALU.mult,
                         [T1B[i]] + SG.s(h, n0, n0 + W), OBR[1].s(h, n0, n0 + W))

                for h in range(4):
                    z, j = h % 2, h // 2
                    ti = trot()
                    S.dma("sp", tmk[ti], tmask_d[l, h], reads=TMDQ[l][h], writes=[TMK[ti]])
                    if smp:
                        ob0 = obanks()
                        obs = [ob0, ob0 + 1]
                        for st in range(2):
                            S.op("pe", lambda e, ob=obs[st], h=h, n0=st * 512: e.matmul(
                                ps[:, ob, :], lhsT=s0t[:, h, :], rhs=qd[:, h, n0:n0 + 512], start=True, stop=False),
                                 [S0B] + QD.s(h, st * 512, st * 512 + 512), [PB[obs[st]]])

                        def sc_mask(st, mt, j=j, z=z, ti=ti):
                            n0, m0 = st * 512, mt * 128
                            sbk = rsb_[st]()
                            S.mm(ps[:, sbk, :], [(kr[:, j, z, m0:m0 + 128], qr[:, j, n0:n0 + 512])],
                                 KR.s(j, m0, m0 + 128) + QR.s(j, n0, n0 + 512), [PB[sbk]])
                            ai = arot()
                            off = TOFF + n0 - m0
                            S.tt(atb[ai], ps[:, sbk, :], tmk[ti][:, off:off + 512], ALU.mult,
                                 [PB[sbk], TMK[ti]], [ATB[ai]])
                            return ai
                        cur = [sc_mask(0, 0), sc_mask(1, 0)]
                        if pend_ret[0] is not None:
                            pend_ret[0]()
                            pend_ret[0] = None
                        for mt in range(8):
                            for st in range(2):
                                ai = cur[st]
                                S.op("pe", lambda e, ob=obs[st], mt=mt, ai=ai, h=h: e.matmul(
                                    ps[:, ob, :], lhsT=vr[:, mt, h * 128:(h + 1) * 128], rhs=atb[ai],
                                    start=False, stop=(mt == 7)), [VRB[mt], ATB[ai]], [PB[obs[st]]])
                                cur[st] = sc_mask(st, mt + 1) if mt + 1 < 8 else None
                        pend_ret[0] = (lambda obs=obs, h=h: [post_norm(obs[st_], h, st_ * 512, 512, b2=rsb_[st_]())
                                                            for st_ in range(2)])
                    else:
                        def rtA(sp, j=j, z=z, ti=ti):
                            ais = []
                            for mt in range(2):
                                sbk = srot()

                                def smm(e, sbk=sbk, mt=mt, sp=sp, j=j, z=z):
                                    ins = None
                                    for a_ in range(2):
                                        q0 = sp * 512 + a_ * 256
                                        ins = e.matmul(ps[:, sbk, a_ * 256:(a_ + 1) * 256],
                                                       lhsT=kr[:, j, z, q0 + mt * 128:q0 + mt * 128 + 128],
                                                       rhs=qr[:, j, q0:q0 + 256], start=True, stop=True)
                                    return ins
                                S.op("pe", smm, KR.s(j, sp * 512, sp * 512 + 512) + QR.s(j, sp * 512, sp * 512 + 512), [PB[sbk]])
                                ai = arot()
                                off = TOFF - mt * 128
                                S.tt(atb[ai].rearrange("p (a n) -> p a n", a=2), ps[:, sbk, :].rearrange("p (a n) -> p a n", a=2),
                                     tmk[ti][:, off:off + 256].unsqueeze(1).to_broadcast([128, 2, 256]), ALU.mult,
                                     [PB[sbk], TMK[ti]], [ATB[ai]])
                                ais.append(ai)
                            return ais

                        def rtB(sp, ais, h=h):
                            ob = orot()

                            def pvm(e, ob=ob, sp=sp, ais=ais, h=h):
                                ins = None
                                for a_ in range(2):
                                    for mt in range(2):
                                        ins = e.matmul(ps[:, ob, a_ * 256:(a_ + 1) * 256],
                                                       lhsT=vr[:, sp * 4 + a_ * 2 + mt, h * 128:(h + 1) * 128],
                                                       rhs=atb[ais[mt]][:, a_ * 256:(a_ + 1) * 256],
                                                       start=(mt == 0), stop=(mt == 1))
                                return ins
                            S.op("pe", pvm, [VRB[sp * 4 + k_] for k_ in range(4)] + [ATB[ais[0]], ATB[ais[1]]], [PB[ob]])
                            post_norm(ob, h, sp * 512, 512)
                        for sp_ in range(2):
                            ais_ = rtA(sp_)
                            if pend_ret[0] is not None:
                                pend_ret[0]()
                            pend_ret[0] = (lambda sp_=sp_, ais_=ais_, rtB=rtB: rtB(sp_, ais_))
                if pend_ret[0] is not None:
                    pend_ret[0]()
                    pend_ret[0] = None
                if not smp:
                    for s in range(4):
                        b = mrot()
                        for h in range(4):
                            S.mm(ps[:, b, h * 128:(h + 1) * 128],
                                 [(ktm[:, 2 * s + i2, h, :], vr[:, 2 * s + i2, h * 128:(h + 1) * 128]) for i2 in range(2)],
                                 [KTM[2 * s], KTM[2 * s + 1], VRB[2 * s], VRB[2 * s + 1]], [PB[b]])
                        si = s % 2
                        S.copy(sst[si], ps[:, b, :], [PB[b]], [SST[si]])
                        out_toks.append(S.dma("sp", sf_out[l, s].rearrange("h k v -> k h v"),
                                              sst[si][0:64, :].rearrange("p (h v) -> p h v", h=4), reads=[SST[si]]))
                        out_toks.append(S.dma("sp", sb_out[l, s].rearrange("h k v -> k h v"),
                                              sst[si][64:128, :].rearrange("p (h v) -> p h v", h=4), reads=[SST[si]]))

            def ph_four():
                AR.reset()
                uT = AR.alloc([4, NT], BF16)
                Y = AR.alloc([8, 4, 256], BF16)
                UT = TB("uT", 4)
                YB = [Buf("y%d" % i) for i in range(8)]
                wb4, WB4 = load_w(l, 5, 4096)
                w4 = slabview(wb4, 0, 8, 512)
                for g in range(4):
                    for tg in range(2):
                        t0 = tg * 512
                        b = brot_()
                        S.mm(ps[:, b, :], [(w4[:, kc, g * 128:(g + 1) * 128], hT[:, kc, t0:t0 + 512]) for kc in range(8)],
                             [WB4] + HT.s(range(8), t0, t0 + 512), [PB[b]])
                        S.copy(uT[:, g, t0:t0 + 512], ps[:, b, :], [PB[b]], UT.s(g, t0, t0 + 512),
                               eng=("act" if tg else "dve"))
                        drain("four", 1)
                for tt in range(8):
                    for gp in range(2):
                        b = brot_()
                        for g in (2 * gp, 2 * gp + 1):
                            S.mm(ps[:, b, (g % 2) * 256:(g % 2) * 256 + 256],
                                 [(uT[:, g, tt * 128:(tt + 1) * 128], cs128)], UT.s(g, tt * 128, tt * 128 + 128) + [CONST],
                                 [PB[b]])
                        S.copy(Y[:, tt, 2 * gp:2 * gp + 2, :], ps[:, b, :].rearrange("p (g n) -> p g n", g=2),
                               [PB[b]], [YB[tt]], eng=("act" if gp else "dve"))
                        drain("four", 1)
                if smp:
                    for hf in range(2):
                        ct, CTB = load_slab(lambda wbt, hf=hf: [(wbt[:, 0:4096], dft1024d[2 * hf].rearrange("p a b -> p (a b)"))])
                        stt_, STB = load_slab(lambda wbt, hf=hf: [(wbt[:, 0:4096], dft1024d[2 * hf + 1].rearrange("p a b -> p (a b)"))])
                        cv_, sv_ = slabview(ct, 0, 8, 512), slabview(stt_, 0, 8, 512)
                        for g in range(4):
                            b = brot_()
                            pairs = []
                            for nt in range(8):
                                pairs.append((Y[:, nt, g, 0:128], cv_[:, nt, :]))
                                pairs.append((Y[:, nt, g, 128:256], sv_[:, nt, :]))
                            S.mm(ps[:, b, :], pairs, YB + [CTB, STB], [PB[b]])
                            S.copy(obr[2][:, g, hf * 512:(hf + 1) * 512], ps[:, b, :], [PB[b]],
                                   OBR[2].s(g, hf * 512, hf * 512 + 512), eng=("act" if g % 2 else "dve"))
                else:
                    dt_, DTB = load_slab(lambda wbt: [(wbt[:, 0:1024], dft256d.rearrange("p a b c -> p (a b c)"))])
                    dv = dt_[:, 0:1024].rearrange("p (a b c) -> p a b c", a=2, b=2)
                    for s in range(4):
                        for g in range(4):
                            b = brot_()
                            pairs = []
                            for nt in range(2):
                                pairs.append((Y[:, 2 * s + nt, g, 0:128], dv[:, 0, nt, :]))
                                pairs.append((Y[:, 2 * s + nt, g, 128:256], dv[:, 1, nt, :]))
                            S.mm(ps[:, b, 0:256], pairs, [YB[2 * s], YB[2 * s + 1], DTB], [PB[b]])
                            S.copy(obr[2][:, g, s * 256:(s + 1) * 256], ps[:, b, 0:256], [PB[b]],
                                   OBR[2].s(g, s * 256, s * 256 + 256), eng=("act" if g % 2 else "dve"))
                            drain("four", 1)
                drain("four", -1)

            SQB = [Buf("sqb%d" % i) for i in range(4)]
            RSB = [Buf("rsb%d" % i) for i in range(4)]
            T1B = [Buf("t1b%d" % i) for i in range(4)]
            if defer_mod:
                ph_att()
                ph_four()
                ph_ret()
            else:
                ph_att()
                ph_ret()
                ph_four()

            AR.reset()
            mg = AR.alloc([8, NT], BF16)
            sgt = [AR.alloc([512], F32) for _ in range(3)]
            mac = [AR.alloc([512], F32) for _ in range(2)]
            mtp = [AR.alloc([512], F32) for _ in range(2)]
            if cur_pass[0] == 0:
                mod_ext[0] = ([AR.alloc([4096], BF16) for _ in range(4)], [Buf("my%d" % k_) for k_ in range(4)], Rot(range(4)))
            MG = TB("mg", 8)
            SGT = [Buf("sgt%d" % i) for i in range(3)]
            MAC = [Buf("mac%d" % i) for i in range(2)]
            MTP = [Buf("mtp%d" % i) for i in range(2)]
            grot = Rot(range(3))
            for j in range(8):
                wbm, WBm = load_w(l, 6 + j, 4608)
                gv = wbm[:, 0:3072].rearrange("p (k b n) -> p k b n", k=8, b=3)
                bv = wbm[:, 3072:4608].rearrange("p (k b n) -> p k b n", k=4, b=3)
                for tg in range(2):
                    t0 = tg * 512
                    mi = tg
                    for br in range(3):
                        bP = brot_()
                        S.mm(ps[:, bP, :], [(bv[:, kc, br, :], obr[br][:, kc, t0:t0 + 512]) for kc in range(4)],
                             [WBm] + OBR[br].s(range(4), t0, t0 + 512), [PB[bP]])
                        bG = brot_()
                        S.mm(ps[:, bG, :], [(gv[:, kc, br, :], hT[:, kc, t0:t0 + 512]) for kc in range(8)],
                             [WBm] + HT.s(range(8), t0, t0 + 512), [PB[bG]])
                        gi = grot()
                        S.act(sgt[gi], ps[:, bG, :], AF.Sigmoid, [PB[bG]], [SGT[gi]])
                        if br == 0:
                            S.tt(mac[mi], sgt[gi], ps[:, bP, :], ALU.mult, [SGT[gi], PB[bP]], [MAC[mi]])
                        else:
                            S.tt(mtp[mi], sgt[gi], ps[:, bP, :], ALU.mult, [SGT[gi], PB[bP]], [MTP[mi]])
                            if br == 1:
                                S.tt(mac[mi], mac[mi], mtp[mi], ALU.add, [MTP[mi]], [MAC[mi]])
                            else:
                                S.tt(mg[:, j, t0:t0 + 512], mac[mi], mtp[mi], ALU.add, [MTP[mi], MAC[mi]],
                                     MG.s(j, t0, t0 + 512))
                    if j >= 1:
                        drain("merge", 2)
            wos = []
            for io in range(2):
                wbo, WBo = load_w(l, 14 + io, 4096)
                wos.append((slabview(wbo, 0, 8, 512), WBo))
            for tg in range(2):
                t0 = tg * 512
                for i in range(8):
                    wo, WBo = wos[i // 4]
                    ii = i % 4
                    b = brot_()
                    S.mm(ps[:, b, :], [(wo[:, kc, ii * 128:(ii + 1) * 128], mg[:, kc, t0:t0 + 512]) for kc in range(8)],
                         [WBo] + MG.s(range(8), t0, t0 + 512), [PB[b]])
                    S.stt(xT[:, i, t0:t0 + 512], ps[:, b, :], md[:, 16 + i:17 + i], xT[:, i, t0:t0 + 512],
                          ALU.mult, ALU.add, [PB[b], MOD], XT.s(i, t0, t0 + 512))
                    drain("merge", 2)

            drain("merge", -1)

            S.stt(abT[:, 1, :], md[:, 32:40], 1.0, pp[:, P0 + 8:P0 + 16], ALU.add, ALU.mult, [MOD, CONST], [ABB])
            norm_mod(abT[:, 1, :], md[:, 24:32], brot_, MOD)
            AR.reset()
            aT = AR.alloc([NFF, NT], BF16)
            acc = [AR.alloc([NT], F32) for _ in range(2)]
            gl = [AR.alloc([NT], F32) for _ in range(2)]
            ACT_ = TB("aT", NFF)
            ACC = [Buf("acc%d" % i) for i in range(2)]
            GL = [Buf("gl%d" % i) for i in range(2)]
            CW = P0 + 67
            CBc = P0 + 67 + 66

            def segs(ap2d, lo, hi):
                return ap2d.rearrange("p (s l) -> p s l", s=nseq)[:, :, lo:hi]
            for cp in range(11):
                wbu, WBu = load_w(l, 16 + cp, 4096)
                wa, wv = slabview(wbu, 0, 8, 256), slabview(wbu, 2048, 8, 256)
                for cc in range(2):
                    c = 2 * cp + cc
                    a0, v0 = (0, 4) if c % 2 == 0 else (2, 6)
                    for tg in range(2):
                        t0 = tg * 512
                        if cp == 0 and cc == 0:
                            S.mm_split(ps[:, a0 + tg, :], [(wa[:, kc, cc * 128:(cc + 1) * 128], hT[:, kc, t0:t0 + 512]) for kc in range(8)],
                                       [HT.s(kc, t0, t0 + 512) for kc in range(8)], [WBu], [PB[a0 + tg]])
                        else:
                            S.mm(ps[:, a0 + tg, :], [(wa[:, kc, cc * 128:(cc + 1) * 128], hT[:, kc, t0:t0 + 512]) for kc in range(8)],
                                 [WBu] + HT.s(range(8), t0, t0 + 512), [PB[a0 + tg]])
                        S.mm(ps[:, v0 + tg, :], [(wv[:, kc, cc * 128:(cc + 1) * 128], hT[:, kc, t0:t0 + 512]) for kc in range(8)],
                             [WBu] + HT.s(range(8), t0, t0 + 512), [PB[v0 + tg]])
                    pa = ps[:, a0:a0 + 2, :].rearrange("p b n -> p (b n)")
                    pv_ = ps[:, v0:v0 + 2, :].rearrange("p b n -> p (b n)")
                    PA = [PB[a0], PB[a0 + 1]]
                    PV = [PB[v0], PB[v0 + 1]]
                    i = c % 2
                    S.act(acc[i], pa, AF.Identity, PA + [CONST], [ACC[i]],
                          bias=pp[:, CBc + c:CBc + c + 1], scale=pp[:, CW + 22 + c:CW + 22 + c + 1])
                    S.stt(segs(acc[i], 1, L), segs(pa, 0, L - 1), pp[:, CW + c:CW + c + 1], segs(acc[i], 1, L),
                          ALU.mult, ALU.add, PA + [CONST], [ACC[i]])
                    S.stt(segs(acc[i], 0, L - 1), segs(pa, 1, L), pp[:, CW + 44 + c:CW + 44 + c + 1],
                          segs(acc[i], 0, L - 1), ALU.mult, ALU.add, PA + [CONST], [ACC[i]])
                    S.act(gl[i], acc[i], AF.Gelu_apprx_tanh, [ACC[i]], [GL[i]])
                    S.tt(aT[:, c, :], gl[i], pv_, ALU.mult, [GL[i]] + PV, ACT_.s(c))
            for i in range(8):
                wbd, WBd = load_w(l, 27 + i, 2816)
                wd = slabview(wbd, 0, NFF, 128)
                for tg in range(2):
                    t0 = tg * 512
                    b = brot_()
                    S.mm(ps[:, b, :], [(wd[:, kc, :], aT[:, kc, t0:t0 + 512]) for kc in range(NFF)],
                         [WBd] + ACT_.s(range(NFF), t0, t0 + 512), [PB[b]])
                    S.stt(xT[:, i, t0:t0 + 512], ps[:, b, :], md[:, 40 + i:41 + i], xT[:, i, t0:t0 + 512],
                          ALU.mult, ALU.add, [PB[b], MOD], XT.s(i, t0, t0 + 512))

        for half in range(2):
            for c in range(8):
                S.dma("sp", xT[:, c, :], xin[half, :, c, :], writes=XT.s(c))
            for l in range(2):
                layer_pass(half, l)
            for c in range(8):
                out_toks.append(S.dma("sp", yout[half, :, c, :], xT[:, c, :], reads=XT.s(c)))
        S.wait_all("sp", out_toks)
        with nc.Block() as block:
            S.run(block)
    return nc


def _consts():
    bf = ml_dtypes.bfloat16
    cb = np.zeros((128, 768), np.float32)
    cb[:, 0:128] = 1.0 / 1024
    cb[:, 128:256] = 1.0 / 128
    for blk in range(2):
        cb[blk * 64:(blk + 1) * 64, 256 + blk * 64:256 + (blk + 1) * 64] = 1.0 / 64
    p = np.arange(128)
    cb[p ^ 16, 384 + p] = 1.0
    cidx = np.arange(128)[:, None] * np.arange(128)[None, :]
    cb[:, 512:640] = np.cos(2 * np.pi * cidx / 128) / np.sqrt(128)
    cb[:, 640:768] = np.sin(2 * np.pi * cidx / 128) / np.sqrt(128)
    d = p % 64
    axis, half, f = d // 32, (d % 32) // 16, d % 16
    inv = (10000.0 ** (-np.arange(16, dtype=np.float32) / 16)).astype(np.float32)
    tok = np.arange(NT)
    row_id, col_id = (tok // 64).astype(np.float32), (tok % 64).astype(np.float32)
    pos = np.where(axis[:, None] == 0, row_id[None, :], col_id[None, :]).astype(np.float32)
    ang = (pos * inv[f][:, None]).astype(np.float32)
    rope = np.zeros((128, 2, NT), np.float32)
    rope[:, 0, :] = np.cos(ang)
    rope[:, 1, :] = np.sin(ang) * np.where(half == 0, -1.0, 1.0)[:, None]

    def dft(Ln):
        n = np.arange(Ln)
        a = 2 * np.pi * ((n[:, None] * n[None, :]) % Ln) / Ln
        return np.cos(a) / np.sqrt(Ln), -np.sin(a) / np.sqrt(Ln)
    c256, s256 = dft(256)
    dft256 = np.zeros((128, 2, 2, 256), np.float32)
    for nt in range(2):
        dft256[:, 0, nt, :] = c256[nt * 128:(nt + 1) * 128, :]
        dft256[:, 1, nt, :] = s256[nt * 128:(nt + 1) * 128, :]
    c1k, s1k = dft(1024)
    dft1024 = np.zeros((4, 128, 8, 512), np.float32)
    for hf in range(2):
        for nt in range(8):
            dft1024[2 * hf, :, nt, :] = c1k[nt * 128:(nt + 1) * 128, hf * 512:(hf + 1) * 512]
            dft1024[2 * hf + 1, :, nt, :] = s1k[nt * 128:(nt + 1) * 128, hf * 512:(hf + 1) * 512]
    dd = (np.arange(TW)[None, :] - TOFF - np.arange(128)[:, None]).astype(np.float32)
    dtab = np.stack([np.maximum(dd, 0), np.maximum(-dd, 0), (dd == 0).astype(np.float32)]).astype(np.float32)
    n1 = np.stack([np.broadcast_to(tok + 1.0, (128, NT)), np.broadcast_to(1024.0 - tok, (128, NT))]).astype(np.float32)
    epp = np.zeros((128, 4), np.float32)
    for i in range(2):
        epp[:, i * 2 + 0] = 255 - (i * 128 + p)
        epp[:, i * 2 + 1] = i * 128 + p
    return dict(cb16=cb.astype(bf), rope=rope, dft256=dft256.astype(bf), dft1024=dft1024.astype(bf),
                dtab=dtab, n1tab=n1, epp=epp)


def _kcv(w, c0, n):
    kc = w.shape[0] // 128
    return w[:, c0:c0 + n].reshape(kc, 128, n).transpose(1, 0, 2)


def _pack_weights(w_ada, w_in, w_bra, w_brr, w_brf, w_out, w_up, w_down):
    wpk = np.zeros((2, NSLAB, 128, SLAB), np.float32)
    wad = np.zeros((2, 12, 128, 4096), np.float32)
    for l in range(2):
        wi = w_in[l]
        put = lambda idx, arr: wpk[l, idx, :, :arr.reshape(128, -1).shape[1]].__setitem__(slice(None), arr.reshape(128, -1))
        put(0, _kcv(wi, 0, 512))
        kvp = np.concatenate([_kcv(wi, 512, 64), _kcv(wi, 512, 64), _kcv(wi, 576, 64), _kcv(wi, 576, 64),
                              _kcv(wi, 640, 128)], axis=2)
        put(1, kvp)
        put(2, _kcv(wi, 768, 512))
        put(3, _kcv(wi, 1280, 512))
        put(4, _kcv(wi, 1792, 512))
        put(5, _kcv(wi, 2304, 512))
        brs = [w_bra[l], w_brr[l], w_brf[l]]
        for j in range(8):
            g = np.stack([_kcv(wi, 2816 + br * 1024 + j * 128, 128) for br in range(3)], axis=2)
            b = np.stack([_kcv(brs[br], j * 128, 128) for br in range(3)], axis=2)
            wpk[l, 6 + j, :, 0:3072] = g.reshape(128, -1)
            wpk[l, 6 + j, :, 3072:4608] = b.reshape(128, -1)
        for io in range(2):
            put(14 + io, _kcv(w_out[l], io * 512, 512))
        for cp in range(11):
            wpk[l, 16 + cp, :, 0:2048] = _kcv(w_up[l], cp * 256, 256).reshape(128, -1)
            wpk[l, 16 + cp, :, 2048:4096] = _kcv(w_up[l], D_FF + cp * 256, 256).reshape(128, -1)
        for i in range(8):
            put(27 + i, _kcv(w_down[l], i * 128, 128))
        for sidx in range(12):
            wad[l, sidx] = _kcv(w_ada[l], sidx * 512, 512).reshape(128, -1)
    return dict(wpk=wpk, wadapk=wad)


def _fm(v):
    return np.ascontiguousarray(v.reshape(-1, 128).T)


_CACHE = {}


def kernel(x_prompt, x_sample, cache_k, cache_v, state_ret_fwd, state_ret_bwd, c, c_ctx,
           w_ada, b_ada, norm1, w_in, q_norm, k_norm, ret_decay_f, ret_decay_b, ret_norm,
           w_br_att, w_br_ret, w_br_four, w_out, norm2, w_up, conv_w, conv_b, w_down):
    f32 = np.float32
    A = lambda a: np.ascontiguousarray(np.asarray(a, dtype=f32))
    x_prompt, x_sample, cache_k, cache_v = A(x_prompt), A(x_sample), A(cache_k), A(cache_v)
    state_ret_fwd, state_ret_bwd, c, c_ctx = A(state_ret_fwd), A(state_ret_bwd), A(c), A(c_ctx)
    if "nc" not in _CACHE:
        _CACHE["nc"] = build_program()
        _CACHE["consts"] = _consts()
    nc = _CACHE["nc"]
    consts = _CACHE["consts"]
    shared = _pack_weights(A(w_ada), A(w_in), A(w_br_att), A(w_br_ret), A(w_br_four), A(w_out), A(w_up), A(w_down))
    shared.update(consts)
    pp = np.zeros((128, 2 * PPL + 24), f32)
    rdf, rdb = A(ret_decay_f), A(ret_decay_b)
    for l in range(2):
        o = l * PPL
        pp[:, o:o + 8] = _fm(A(norm1)[l])
        pp[:, o + 8:o + 16] = _fm(A(norm2)[l])
        pp[:, o + 16:o + 64] = _fm(A(b_ada)[l])
        pp[:, o + 64] = np.tile(A(q_norm)[l], 2)
        pp[:, o + 65] = np.tile(A(k_norm)[l], 2)
        pp[:, o + 66] = A(ret_norm)[l]
        cw = A(conv_w)[l]
        for k in range(3):
            pp[:, o + 67 + k * 22:o + 67 + (k + 1) * 22] = _fm(cw[k])
        pp[:, o + 67 + 66:o + 67 + 88] = _fm(A(conv_b)[l])
        for dr, rd in enumerate((rdf, rdb)):
            for h in range(4):
                pp[:, 2 * PPL + l * 8 + dr * 4 + h] = rd[l, h]
            for j in range(2):
                pp[0:64, 2 * PPL + 16 + l * 4 + dr * 2 + j] = rd[l, 2 * j]
                pp[64:128, 2 * PPL + 16 + l * 4 + dr * 2 + j] = rd[l, 2 * j + 1]
    in_maps = []
    for i in range(8):
        m = dict(shared)
        xp = x_prompt[4 * i:4 * i + 4].reshape(NT, D)
        xs = x_sample[i]
        xin = np.stack([xp.reshape(NT, 8, 128).transpose(2, 1, 0), xs.reshape(NT, 8, 128).transpose(2, 1, 0)])
        m["xin"] = np.ascontiguousarray(xin)
        m["condT"] = np.ascontiguousarray(np.stack([_fm(c_ctx), _fm(c[i])], axis=-1))
        m["pp"] = pp
        ck = cache_k[i]
        ckt = ck.transpose(0, 2, 3, 1)
        ckz = np.zeros((2, 2, 2, 128, 512), f32)
        ckz[:, :, 0, 0:64, :] = ckt
        ckz[:, :, 1, 64:128, :] = ckt
        m["ckT"] = ckz
        m["cv"] = np.ascontiguousarray(cache_v[i].reshape(2, 512, 128))
        s0 = np.stack([state_ret_fwd[i], state_ret_bwd[i]], axis=0)
        m["s0"] = np.ascontiguousarray(s0.transpose(1, 0, 3, 2, 4).reshape(2, 128, 4, 128))
        in_maps.append(m)
    res = run_bass_kernel_spmd(nc, in_maps, core_ids=list(range(8)))
    R = res.results
    y_prompt = np.zeros((32, 256, D), f32)
    y_sample = np.zeros((8, 1024, D), f32)
    nck = np.zeros((32, 2, 256, 2, 64), f32)
    ncv = np.zeros((32, 2, 256, 2, 64), f32)
    nsf = np.zeros((32, 2, 4, 64, 128), f32)
    nsb = np.zeros((32, 2, 4, 64, 128), f32)
    for i in range(8):
        r = R[i]
        yo = np.asarray(r["yout"], f32)
        yt = yo.transpose(0, 3, 2, 1).reshape(2, NT, D)
        y_prompt[4 * i:4 * i + 4] = yt[0].reshape(4, 256, D)
        y_sample[i] = yt[1]
        cko = np.asarray(r["ck_out"], f32).reshape(2, 2, 64, 4, 256)
        nck[4 * i:4 * i + 4] = cko.transpose(3, 0, 4, 1, 2)
        cvo = np.asarray(r["cv_out"], f32).reshape(2, 4, 256, 2, 64)
        ncv[4 * i:4 * i + 4] = cvo.transpose(1, 0, 2, 3, 4)
        nsf[4 * i:4 * i + 4] = np.asarray(r["sf_out"], f32).transpose(1, 0, 2, 3, 4)
        nsb[4 * i:4 * i + 4] = np.asarray(r["sb_out"], f32).transpose(1, 0, 2, 3, 4)
    return (y_prompt, y_sample, nck, ncv, nsf, nsb)
```

```python
import contextlib
import math
import numpy as np
import ml_dtypes
import concourse.bass as bass
import concourse.mybir as mybir
from concourse.bass_utils import run_bass_kernel_spmd

F32 = mybir.dt.float32
BF16 = mybir.dt.bfloat16
AF = mybir.ActivationFunctionType
ALU = mybir.AluOpType

D = 1024
DEPTH = 2
NT = 1024
D_IN = 5888
D_FF = 2816
NFF = 22
EPS = 1e-6
TW = 1920
TOFF = 896
PPL = 155
SLAB = 4608
NWB = 4
NSLAB = 35
SERIALIZE = False


class Buf:
    __slots__ = ("name", "w", "r")

    def __init__(self, name):
        self.name = name
        self.w = None
        self.r = []


class TB:
    def __init__(self, name, nch, ntb=4):
        self.b = [[Buf("%s_%d_%d" % (name, c, t)) for t in range(ntb)] for c in range(nch)]

    def s(self, chunks, t0=0, t1=NT):
        if isinstance(chunks, int):
            chunks = [chunks]
        return [self.b[c][t] for c in chunks for t in range(t0 // 256, (t1 + 255) // 256)]


class Sched:
    ENGS = ("pe", "act", "dve", "pool", "sp")

    def __init__(self, nc, stack, serialize=False, ring=8):
        self.nc = nc
        self.sem = {e: stack.enter_context(nc.semaphore("s_" + e)) for e in self.ENGS}
        self.cnt = {e: 0 for e in self.ENGS}
        self.waited = {e: {} for e in self.ENGS}
        self.prog = {e: [] for e in self.ENGS}
        self.rings = {}
        for q in ("sp", "pool"):
            self.rings[q] = [[stack.enter_context(nc.semaphore("r_%s%d" % (q, i))), 0] for i in range(ring)]
        self.ring_pos = {q: 0 for q in self.rings}
        self.serialize = serialize
        self.last_tok = None
        self.pool_bar = None
        self.skip_self = False

    def _wait(self, eng, tok):
        if tok is None:
            return
        sem, val = tok
        if self.skip_self and sem is self.sem[eng]:
            return
        k = id(sem)
        if self.waited[eng].get(k, 0) >= val:
            return
        self.waited[eng][k] = val
        self.prog[eng].append(("wait", sem, val))

    def _deps(self, eng, reads, writes):
        for b in reads:
            self._wait(eng, b.w)
        for b in writes:
            self._wait(eng, b.w)
            for t in b.r:
                self._wait(eng, t)
        if self.serialize:
            self._wait(eng, self.last_tok)

    def _commit(self, tok, reads, writes):
        for b in reads:
            b.r.append(tok)
            if len(b.r) > 48:
                best = {}
                for s, v in b.r:
                    if best.get(id(s), (None, 0))[1] < v:
                        best[id(s)] = (s, v)
                b.r = list(best.values())
        for b in writes:
            b.w = tok
            b.r = []
        self.last_tok = tok

    def op(self, eng, fn, reads=(), writes=()):
        self._deps(eng, reads, writes)
        self.cnt[eng] += 1
        tok = (self.sem[eng], self.cnt[eng])
        self.prog[eng].append(("op", fn, self.sem[eng]))
        self._commit(tok, reads, writes)
        return tok

    def dma(self, q, out, in_, reads=(), writes=(), arena=False):
        if arena and q == "pool" and self.pool_bar:
            for t in self.pool_bar:
                self._wait(q, t)
            self.pool_bar = None
        self._deps(q, reads, writes)
        ring = self.rings[q]
        i = self.ring_pos[q]
        self.ring_pos[q] = (i + 1) % len(ring)
        sem, n = ring[i]
        if n > 0:
            self._wait(q, (sem, 16 * n))
        ring[i][1] = n + 1
        tok = (sem, 16 * (n + 1))
        self.prog[q].append(("dma", out, in_, sem))
        self._commit(tok, reads, writes)
        return tok

    def mm(self, out, pairs, reads, writes):
        pairs = list(pairs)

        def fn(e):
            n = len(pairs)
            ins = None
            for i, (l, r) in enumerate(pairs):
                ins = e.matmul(out, lhsT=l, rhs=r, start=(i == 0), stop=(i == n - 1))
            return ins
        return self.op("pe", fn, reads, writes)

    def mm_split(self, out, pairs, reads_list, common_reads, writes):
        pairs = list(pairs)
        n = len(pairs)
        for i, (l_, r_) in enumerate(pairs):
            self.skip_self = (i > 0)
            self.op("pe", lambda e, i=i, l_=l_, r_=r_: e.matmul(out, lhsT=l_, rhs=r_, start=(i == 0), stop=(i == n - 1)),
                    list(reads_list[i]) + (list(common_reads) if i == 0 else []), writes)
        self.skip_self = False

    def act(self, out, in_, func, reads, writes, bias=None, scale=None):
        kw = {}
        if bias is not None:
            kw["bias"] = bias
        if scale is not None:
            kw["scale"] = scale
        return self.op("act", lambda e: e.activation(out=out, in_=in_, func=func, **kw), reads, writes)

    def tt(self, out, in0, in1, op, reads, writes, eng="dve"):
        return self.op(eng, lambda e: e.tensor_tensor(out=out, in0=in0, in1=in1, op=op), reads, writes)

    def ts(self, out, in0, s1, s2, op0, op1, reads, writes, eng="dve"):
        if op1 is None:
            return self.op(eng, lambda e: e.tensor_scalar(out=out, in0=in0, scalar1=s1, scalar2=None, op0=op0),
                           reads, writes)
        return self.op(eng, lambda e: e.tensor_scalar(out=out, in0=in0, scalar1=s1, scalar2=s2, op0=op0, op1=op1),
                       reads, writes)

    def stt(self, out, in0, scalar, in1, op0, op1, reads, writes):
        return self.op("dve", lambda e: e.scalar_tensor_tensor(out=out, in0=in0, scalar=scalar, in1=in1,
                                                                op0=op0, op1=op1), reads, writes)

    def copy(self, out, in_, reads, writes, eng="dve"):
        if eng == "act":
            return self.op(eng, lambda e: e.activation(out=out, in_=in_, func=AF.Copy), reads, writes)
        return self.op(eng, lambda e: e.tensor_copy(out=out, in_=in_), reads, writes)

    def barrier(self):
        toks = [(self.sem[e], self.cnt[e]) for e in self.ENGS if self.cnt[e] > 0]
        for q in self.rings:
            for sem, n in self.rings[q]:
                if n > 0:
                    toks.append((sem, 16 * n))
        for e in ("act", "dve", "sp"):
            for t in toks:
                self._wait(e, t)
        self.pool_bar = toks

    def memset(self, ap, val, writes, eng="dve"):
        return self.op(eng, lambda e: e.memset(ap, val), (), writes)

    def wait_all(self, eng, toks):
        for t in toks:
            self._wait(eng, t)

    def run(self, block):
        def mk(eng):
            def body(e):
                for it in self.prog[eng]:
                    if it[0] == "wait":
                        e.wait_ge(it[1], it[2])
                    elif it[0] == "op":
                        it[1](e).then_inc(it[2], 1)
                    else:
                        e.dma_start(out=it[1], in_=it[2]).then_inc(it[3], 16)
            return body
        block.tensor(mk("pe"))
        block.scalar(mk("act"))
        block.vector(mk("dve"))
        block.gpsimd(mk("pool"))
        block.sync(mk("sp"))


class Rot:
    def __init__(self, items):
        self.items = list(items)
        self.i = 0

    def __call__(self):
        v = self.items[self.i % len(self.items)]
        self.i += 1
        return v


class Arena:
    def __init__(self, t, nelem16, sched=None):
        self.sched = sched
        self.t = t
        self.n = nelem16
        self.off = 0

    def reset(self, off=0):
        self.off = off
        if self.sched is not None:
            self.sched.barrier()

    def alloc(self, free, dt):
        n = int(np.prod(free))
        w = n * 2 if dt == F32 else n
        off = (self.off + 31) // 32 * 32
        assert off + w <= self.n, ("arena overflow", off, w, self.n)
        ap = self.t[:, off:off + w]
        if dt == F32:
            ap = ap.bitcast(F32)
        if len(free) == 2:
            ap = ap.rearrange("p (a b) -> p a b", a=free[0])
        elif len(free) == 3:
            ap = ap.rearrange("p (a b c) -> p a b c", a=free[0], b=free[1])
        elif len(free) == 4:
            ap = ap.rearrange("p (a b c d) -> p a b c d", a=free[0], b=free[1], c=free[2])
        self.off = off + w
        return ap


def build_program():
    nc = bass.Bass("TRN2", target_bir_lowering=False)

    def din(name, shape, dt=F32):
        return nc.dram_tensor(name, list(shape), dt, kind="ExternalInput").ap()

    def dout(name, shape, dt=F32):
        return nc.dram_tensor(name, list(shape), dt, kind="ExternalOutput").ap()

    xin = din("xin", [2, 128, 8, NT])
    condT = din("condT", [128, 8, 2])
    ppd = din("pp", [128, 2 * PPL + 24])
    wpk = din("wpk", [2, NSLAB, 128, SLAB])
    wadapk = din("wadapk", [2, 12, 128, 4096])
    ckT = din("ckT", [2, 2, 2, 128, 512])
    cvd = din("cv", [2, 512, 128])
    s0d = din("s0", [2, 128, 4, 128])
    cb16d = din("cb16", [128, 768], BF16)
    roped = din("rope", [128, 2, NT])
    dft256d = din("dft256", [128, 2, 2, 256], BF16)
    dft1024d = din("dft1024", [4, 128, 8, 512], BF16)
    dtabd = din("dtab", [3, 128, TW])
    n1tabd = din("n1tab", [2, 128, NT])
    eppd = din("epp", [128, 4])

    yout = dout("yout", [2, 128, 8, NT])
    ck_out = dout("ck_out", [2, 128, NT])
    cv_out = dout("cv_out", [2, NT, 128])
    sf_out = dout("sf_out", [2, 4, 4, 64, 128])
    sb_out = dout("sb_out", [2, 4, 4, 64, 128])

    tmask_d = nc.dram_tensor("tmask_d", [2, 4, 128, TW], BF16, kind="Internal").ap()
    decq_d = nc.dram_tensor("decq_d", [2, 128, 2, 2, NT], BF16, kind="Internal").ap()

    out_toks = []
    with contextlib.ExitStack() as st:
        S = Sched(nc, st, serialize=SERIALIZE)

        def sb(name, shape, dt):
            return st.enter_context(nc.sbuf_tensor(name, list(shape), dt))

        xT = sb("xT", [128, 8, NT], F32)
        hT = sb("hT", [128, 8, NT], BF16)
        obr = [sb("oatt", [128, 4, NT], BF16), sb("oret", [128, 4, NT], BF16), sb("ofou", [128, 4, NT], BF16)]
        wbuf = [sb("wb%d" % i, [128, SLAB], BF16) for i in range(NWB)]
        rope = sb("rope_t", [128, 2, NT], F32)
        cb16 = sb("cb16_t", [128, 768], BF16)
        pp = sb("pp_t", [128, 2 * PPL + 24], F32)
        modT = sb("modT", [128, 2, 2, 48], F32)
        abT = sb("abT", [128, 2, 8], F32)
        lgb_t = sb("lgb_t", [128, 24], F32)
        kdec = sb("kdec", [128, 2, 2, 8], F32)
        epp = sb("epp_t", [128, 4], F32)
        scond = sb("scond", [128, 8, 2], BF16)
        condt = sb("condt", [128, 8, 2], F32)
        tblf = sb("tblf", [128, 3, 480], F32)
        tble = sb("tble", [128, 4, 512], BF16)
        tbln = sb("tbln", [128, NT], F32)
        lgq = sb("lgq", [128, 8], F32)
        ARN = 37888
        arena_t = sb("arena", [128, ARN], BF16)
        AR = Arena(arena_t, ARN, S)
        ps = st.enter_context(nc.psum_tensor("ps", [128, 8, 512], F32))

        ones1024 = cb16[:, 0:128]
        ones128 = cb16[:, 128:256]
        bd64 = cb16[:, 256:384]
        pswap = cb16[:, 384:512]
        cs128 = cb16[:, 512:768]

        XT = TB("xT", 8)
        HT = TB("hT", 8)
        OBR = [TB("oatt", 4), TB("oret", 4), TB("ofou", 4)]
        WB = [Buf("wb%d" % i) for i in range(NWB)]
        PB = [Buf("ps%d" % i) for i in range(8)]
        CONST = Buf("const")
        ABB = Buf("ab")
        LG = Buf("lg")
        KDEC = Buf("kdec")
        wrot = Rot(range(NWB))

        def load_slab(pieces):
            i = wrot()
            for (dst, src) in pieces(wbuf[i]):
                S.dma("pool", dst, src, reads=[], writes=[WB[i]])
            return wbuf[i], WB[i]

        def load_w(l, idx, n):
            return load_slab(lambda wbt: [(wbt[:, 0:n], wpk[l, idx, :, 0:n])])

        def wcols(w_l, c0, nc_, kc=8):
            return w_l.rearrange("(kc p) n -> p kc n", p=128)[:, :, c0:c0 + nc_]

        def slabview(wb, off, kc, ncols):
            return wb[:, off:off + kc * ncols].rearrange("p (k n) -> p k n", k=kc)

        S.dma("sp", cb16[:], cb16d[:, :], writes=[CONST])
        S.dma("sp", pp[:], ppd[:, :], writes=[CONST])
        S.dma("sp", rope[:], roped[:, :, :], writes=[CONST])
        S.dma("sp", epp[:], eppd[:, :], writes=[CONST])
        S.dma("sp", condt[:], condT[:, :, :], writes=[CONST])

        SC = Buf("scond")
        MODL = [Buf("mod0"), Buf("mod1")]
        S.act(scond[:], condt[:], AF.Silu, [CONST], [SC])
        brot0 = Rot(range(8))

        mod_ext = [None]
        drot = Rot([6, 7])

        def mod_slab(l, sidx, brot):
            if mod_ext[0] is None:
                wb, WBb = load_slab(lambda wbt: [(wbt[:, 0:4096], wadapk[l, sidx])])
            else:
                bufs_, BUFS_, rot_ = mod_ext[0]
                i_ = rot_()
                wb, WBb = bufs_[i_], BUFS_[i_]
                S.dma("pool", wb[:, 0:4096], wadapk[l, sidx], writes=[WBb], arena=True)
                brot = drot
            wv = slabview(wb, 0, 8, 512)
            b = brot()
            for j in range(4):
                S.mm(ps[:, b, 2 * j:2 * j + 2],
                     [(wv[:, kc, j * 128:(j + 1) * 128], scond[:, kc, :]) for kc in range(8)],
                     [WBb, SC], [PB[b]])
            for cnd in range(2):
                S.tt(modT[:, l, cnd, sidx * 4:sidx * 4 + 4],
                     ps[:, b, 0:8].rearrange("p (j c) -> p j c", c=2)[:, :, cnd],
                     pp[:, l * PPL + 16 + sidx * 4: l * PPL + 16 + sidx * 4 + 4], ALU.add,
                     [PB[b], CONST], [MODL[l]])
        RDO = 2 * PPL
        S.act(lgb_t[:], pp[:, RDO:RDO + 24], AF.Exp, [CONST], [LG], scale=-1.0)
        S.act(lgb_t[:], lgb_t[:], AF.Ln, [LG], [LG], bias=1.0, scale=1.0)
        S.ts(lgb_t[:], lgb_t[:], -1.0, None, ALU.mult, None, [LG], [LG])

        def lg_b(l, dr, h):
            return lgb_t[:, l * 8 + dr * 4 + h: l * 8 + dr * 4 + h + 1]

        def lg_p(l, dr, j):
            c = 16 + l * 4 + dr * 2 + j
            return lgb_t[:, c:c + 1]

        import collections
        DCH = Buf("dch")
        DCN = Buf("dcn")
        EB = [Buf("te0"), Buf("te1"), Buf("te2"), Buf("te3")]
        TMDQ = [[[Buf("tmd%d%d%d" % (l, h, q)) for q in range(4)] for h in range(4)] for l in range(2)]
        DQDH = [[Buf("dqd%d_%d" % (l, k_)) for k_ in range(8)] for l in range(2)]
        ering = Rot(range(3))

        def tmask_q(l, h, q, load):
            if load:
                S.dma("sp", tblf[:, :, :], dtabd[:, :, q * 480:(q + 1) * 480].rearrange("a p n -> p a n"),
                      writes=[DCH])
            i = ering()
            e1, e2 = tble[:, i, 0:480], tble[:, 3, 0:480]
            S.act(e1, tblf[:, 0, 0:480], AF.Exp, [DCH, LG], [EB[i]], scale=lg_b(l, 0, h))
            S.act(e2, tblf[:, 1, 0:480], AF.Exp, [DCH, LG], [EB[3]], scale=lg_b(l, 1, h))
            S.tt(e1, e1, e2, ALU.mult, [EB[3]], [EB[i]])
            S.tt(e1, e1, tblf[:, 2, 0:480], ALU.add, [DCH], [EB[i]])
            S.dma("sp", tmask_d[l, h][:, q * 480:(q + 1) * 480], e1, reads=[EB[i]], writes=[TMDQ[l][h][q]])

        S.dma("sp", tbln[:, :], n1tabd[0], writes=[DCN])
        for l_ in range(2):
            for j_ in range(2):
                k_ = l_ * 2 + j_
                S.ts(lgq[:, k_:k_ + 1], lg_p(l_, 1, j_), -1.0, None, ALU.mult, None, [LG], [LG])
                S.ts(lgq[:, 4 + k_:5 + k_], lg_p(l_, 1, j_), 1025.0, None, ALU.mult, None, [LG], [LG])

        def decq_h(l, j, dr, hf):
            i = ering()
            src = tbln[:, hf * 512:(hf + 1) * 512]
            if dr == 0:
                S.act(tble[:, i, :], src, AF.Exp, [DCN, LG], [EB[i]], scale=lg_p(l, 0, j))
            else:
                k_ = l * 2 + j
                S.act(tble[:, i, :], src, AF.Exp, [DCN, LG], [EB[i]], scale=lgq[:, k_:k_ + 1],
                      bias=lgq[:, 4 + k_:5 + k_])
            S.dma("sp", decq_d[l, :, j, dr, hf * 512:(hf + 1) * 512], tble[:, i, :], reads=[EB[i]],
                  writes=[DQDH[l][j * 4 + dr * 2 + hf]])

        def tmask_steps(l):
            return [(lambda l=l, h=h, q=q: tmask_q(l, h, q, h == 0)) for q in range(4) for h in range(4)]

        def decq_steps(l):
            return [(lambda l=l, j=j, dr=dr, hf=hf: decq_h(l, j, dr, hf))
                    for j in range(2) for dr in range(2) for hf in range(2)]
        for l in range(2):
            for i in range(2):
                for dr in range(2):
                    S.act(kdec[:, l, i, dr * 4:dr * 4 + 4], lgb_t[:, l * 8 + dr * 4:l * 8 + dr * 4 + 4], AF.Exp,
                          [LG, CONST], [KDEC], scale=epp[:, i * 2 + dr:i * 2 + dr + 1])
        for sidx in range(4):
            mod_slab(0, sidx, brot0)
        Q = collections.defaultdict(collections.deque)
        cur_brot = [brot0]
        cur_pass = [0]
        for sidx in range(4, 12):
            Q[(0, "att")].append(lambda sidx=sidx: mod_slab(0, sidx, cur_brot[0]))
        t0s = tmask_steps(0)
        Q[(0, "four")].extend(t0s[:8])
        Q[(0, "ret")].extend(t0s[8:])
        ml1 = [(lambda sidx=sidx: mod_slab(1, sidx, cur_brot[0])) for sidx in range(6)]
        t1s = tmask_steps(1)
        while ml1 or t1s:
            for lst in (ml1, t1s, t1s):
                if lst:
                    Q[(0, "merge")].append(lst.pop(0))
        for sidx in range(6, 12):
            Q[(1, "att")].append(lambda sidx=sidx: mod_slab(1, sidx, cur_brot[0]))
        Q[(1, "merge")].extend(decq_steps(0) + decq_steps(1))

        def drain(name, n=1):
            q_ = Q[(cur_pass[0], name)]
            while q_ and n != 0:
                q_.popleft()()
                n -= 1

        def norm_mod(A, Bv, brot_, MOD):
            AR.reset()
            sq = [AR.alloc([8, 512], BF16) for _ in range(2)]
            tmp = [AR.alloc([512], F32) for _ in range(4)]
            SQ = [Buf("sq0"), Buf("sq1")]
            TMP = [Buf("tmp%d" % i) for i in range(4)]
            bks = []
            for tg in range(2):
                t0, t1 = tg * 512, tg * 512 + 512
                S.act(sq[tg], xT[:, :, t0:t1], AF.Square, XT.s(range(8), t0, t1), [SQ[tg]])
            for tg in range(2):
                b = brot_()
                bks.append(b)
                S.mm(ps[:, b, :], [(ones1024, sq[tg][:, c, :]) for c in range(8)], [SQ[tg], CONST], [PB[b]])
                S.act(ps[:, b, :], ps[:, b, :], AF.Ln, [], [PB[b]], bias=EPS, scale=1.0)
                S.act(ps[:, b, :], ps[:, b, :], AF.Exp, [], [PB[b]], scale=-0.5)
            k = 0
            for tg in range(2):
                t0, t1 = tg * 512, tg * 512 + 512
                b = bks[tg]
                for c in range(8):
                    i = k % 4
                    k += 1
                    S.stt(tmp[i], xT[:, c, t0:t1], A[:, c:c + 1], ps[:, b, :], ALU.mult, ALU.mult,
                          XT.s(c, t0, t1) + [PB[b], ABB, MOD], [TMP[i]])
                    S.act(hT[:, c, t0:t1], tmp[i], AF.Identity, [TMP[i], MOD], HT.s(c, t0, t1),
                          bias=Bv[:, c:c + 1], scale=1.0)

        def rstd_from(psb, PBb, out, OUTB):
            S.act(out, psb, AF.Ln, [PBb], [OUTB], bias=EPS, scale=1.0)
            S.act(out, out, AF.Exp, [OUTB], [OUTB], scale=-0.5)

        def layer_pass(half, l):
            smp = (half == 1)
            nseq, L = (1, 1024) if smp else (4, 256)
            ntl = L // 128
            NQ = 512 if smp else 256
            P0 = l * PPL
            md = modT[:, l, half, :]
            MOD = MODL[l]
            defer_mod = (half == 0 and l == 0)
            cur_pass[0] = half * 2 + l
            brot_ = Rot(range(8))
            cur_brot[0] = brot_

            S.stt(abT[:, 0, :], md[:, 8:16], 1.0, pp[:, P0:P0 + 8], ALU.add, ALU.mult, [MOD, CONST], [ABB])

            norm_mod(abT[:, 0, :], md[:, 0:8], brot_, MOD)

            def ph_att():
                AR.reset()
                qT = AR.alloc([4, NT], BF16)
                nkeys = 1536 if smp else 1024
                KT = AR.alloc([2, 2, nkeys], BF16)
                nvt = 12 if smp else 8
                VA = AR.alloc([nvt, 2, 128], BF16)
                etb = [AR.alloc([512], BF16) for _ in range(4)]
                sqb = [AR.alloc([512], BF16) for _ in range(4)]
                rsb = [AR.alloc([512], F32) for _ in range(4)]
                if smp:
                    qnb = [AR.alloc([512], BF16) for _ in range(4)]
                    t1b = [AR.alloc([512], F32) for _ in range(3)]
                recb = [AR.alloc([512], F32) for _ in range(2)]
                if not smp:
                    kst = AR.alloc([2, NT], F32)
                    vst = AR.alloc([8, 128], F32)
                    KST, VST = Buf("kst"), Buf("vst")
                if Q[(cur_pass[0], "att")]:
                    mod_ext[0] = ([AR.alloc([4096], BF16) for _ in range(2)], [Buf("mx%d" % k_) for k_ in range(2)], Rot(range(2)))
                QT = TB("qT", 4)
                KTB = TB("KT", 2, 6)
                VAB = [Buf("va%d" % i) for i in range(nvt)]
                ETB = [Buf("et%d" % i) for i in range(4)]
                SQB = [Buf("sqb%d" % i) for i in range(4)]
                RSB = [Buf("rsb%d" % i) for i in range(4)]
                QNB = [Buf("qnb%d" % i) for i in range(4)]
                T1B = [Buf("t1b%d" % i) for i in range(4)]
                T2B = [Buf("t2b%d" % i) for i in range(4)]
                RECB = [Buf("rec%d" % i) for i in range(2)]
                rr = Rot(range(2))

                S.memset(VA[:, :, :, 64:128], 1.0, VAB)
                S.memset(KT[:, :, :, 0:1024], 0.0, KTB.s([0, 1], 0, 1024))

                wbq, WBq = load_w(l, 0, 4096)
                wq = slabview(wbq, 0, 8, 512)

                wbk, WBk = load_w(l, 1, 3072)
                wk = slabview(wbk, 0, 8, 384)
                if smp:
                    for kv in range(2):
                        for z in range(2):
                            S.dma("pool", KT[:, kv, z, 1024:1536], ckT[l, kv, z], writes=KTB.s(kv, 1024, 1536), arena=True)
                    for kv in range(2):
                        S.dma("pool", VA[:, 8:12, kv, 0:64],
                              cvd[l].rearrange("(tt p) f -> p tt f", p=128)[:, :, kv * 64:(kv + 1) * 64],
                              writes=VAB[8:12], arena=True)

                units = []
                for tg in range(2):
                    t0 = tg * 512
                    for j in range(4):
                        units.append(dict(w=wq, WBw=WBq, c0=j * 128, g=P0 + 64, out=qT[:, j, t0:t0 + 512],
                                          OUTB=QT.s(j, t0, t0 + 512), t0=t0, kst=None))
                    for kv in range(2):
                        units.append(dict(w=wk, WBw=WBk, c0=kv * 128, g=P0 + 65, out=None, kv=kv,
                                          OUTB=KTB.s(kv, t0, t0 + 512), t0=t0,
                                          kst=(None if smp else kst[:, kv, t0:t0 + 512])))
                for ui, u in enumerate(units):
                    u["i"] = ui % 4
                    u["i3"] = ui % 3
                    u["split"] = ui in (0, 1, 6, 7)

                rotA, rotB, rotC = Rot([0, 1, 2, 3]), Rot([4, 5]), Rot([6, 7])

                def stA(u):
                    t0, c0, w = u["t0"], u["c0"], u["w"]
                    b = rotA()
                    u["b"] = b
                    if u["split"]:
                        S.mm_split(ps[:, b, :], [(w[:, kc, c0:c0 + 128], hT[:, kc, t0:t0 + 512]) for kc in range(8)],
                                   [HT.s(kc, t0, t0 + 512) for kc in range(8)], [u["WBw"]], [PB[b]])
                    else:
                        S.mm(ps[:, b, :], [(w[:, kc, c0:c0 + 128], hT[:, kc, t0:t0 + 512]) for kc in range(8)],
                             [u["WBw"]] + HT.s(range(8), t0, t0 + 512), [PB[b]])
                    S.act(sqb[u["i"]], ps[:, b, :], AF.Square, [PB[b]], [SQB[u["i"]]])

                def stB(u):
                    i, b, g = u["i"], u["b"], u["g"]
                    b2 = rotB()
                    S.mm(ps[:, b2, :], [(bd64, sqb[i])], [SQB[i], CONST], [PB[b2]])
                    rstd_from(ps[:, b2, :], PB[b2], rsb[i], RSB[i])
                    if smp:
                        S.stt(ps[:, b, :], ps[:, b, :], pp[:, g:g + 1], rsb[i], ALU.mult, ALU.mult,
                              [RSB[i], CONST], [PB[b]])
                        S.copy(qnb[i], ps[:, b, :], [PB[b]], [QNB[i]], eng="act")
                    elif u["kst"] is not None:
                        S.stt(u["kst"], ps[:, b, :], pp[:, g:g + 1], rsb[i], ALU.mult, ALU.mult,
                              [PB[b], RSB[i], CONST], [KST])
                        for z in range(2):
                            S.copy(KT[z * 64:(z + 1) * 64, u["kv"], z, u["t0"]:u["t0"] + 512], u["kst"][z * 64:(z + 1) * 64, :],
                                   [KST], u["OUTB"], eng=("act" if z == 0 else "dve"))
                    else:
                        S.stt(u["out"], ps[:, b, :], pp[:, g:g + 1], rsb[i], ALU.mult, ALU.mult,
                              [PB[b], RSB[i], CONST], u["OUTB"])

                def stC(u):
                    if not smp:
                        return
                    i, t0 = u["i"], u["t0"]
                    i3 = u["i3"]
                    b = u["b"]
                    S.tt(t1b[i3], ps[:, b, :], rope[:, 0, t0:t0 + 512], ALU.mult, [PB[b], CONST], [T1B[i3]])
                    b3 = rotC()
                    S.mm(ps[:, b3, :], [(pswap, qnb[i])], [QNB[i], CONST], [PB[b3]])
                    S.tt(ps[:, b3, :], ps[:, b3, :], rope[:, 1, t0:t0 + 512], ALU.mult, [CONST], [PB[b3]])
                    if u["out"] is not None:
                        S.tt(u["out"], t1b[i3], ps[:, b3, :], ALU.add, [T1B[i3], PB[b3]], u["OUTB"])
                    else:
                        for z in range(2):
                            S.tt(KT[z * 64:(z + 1) * 64, u["kv"], z, t0:t0 + 512], t1b[i3][z * 64:(z + 1) * 64, :],
                                 ps[z * 64:(z + 1) * 64, b3, :], ALU.add, [T1B[i3], PB[b3]], u["OUTB"])
                stages = [stA, stB, stC]
                for step in range(len(units) + len(stages) - 1):
                    for si, stg in enumerate(stages):
                        ui = step - si
                        if 0 <= ui < len(units):
                            stg(units[ui])
                    if step >= 4:
                        drain("att", 1)
                if not smp:
                    for kv in range(2):
                        out_toks.append(S.dma("sp", ck_out[l, kv * 64:(kv + 1) * 64, :], kst[0:64, kv, :], reads=[KST]))
                for tt in range(8):
                    b = brot_()
                    S.mm(ps[:, b, 0:128], [(hT[:, kc, tt * 128:(tt + 1) * 128], wk[:, kc, 256:384]) for kc in range(8)],
                         [WBk] + HT.s(range(8), tt * 128, tt * 128 + 128), [PB[b]])
                    S.copy(VA[:, tt, :, 0:64], ps[:, b, 0:128].rearrange("p (k d) -> p k d", k=2), [PB[b]], [VAB[tt]])
                    if not smp:
                        S.copy(vst[:, tt, :], ps[:, b, 0:128], [PB[b]], [VST], eng="act")
                if not smp:
                    out_toks.append(S.dma("sp", cv_out[l].rearrange("(tt p) f -> p tt f", p=128), vst, reads=[VST]))

                orot = Rot([6, 7])
                srot = Rot([0, 1, 2, 3, 4, 5])
                erot = Rot(range(4))
                scale = 64 ** -0.5
                if smp:
                    sbanks = [Rot([0, 1]), Rot([2, 3])]
                    obanks = Rot([4, 6])
                    e4 = Rot(range(4))
                    for h in range(8):
                        kv, z, ch = h // 4, h % 2, h // 2
                        base = z * 64
                        ob0 = obanks()
                        obs = [ob0, ob0 + 1]

                        def s_exp(st, kt, kv=kv, z=z, ch=ch):
                            sbk = sbanks[st]()
                            q0 = st * 512
                            S.mm(ps[:, sbk, :], [(KT[:, kv, z, kt * 128:(kt + 1) * 128], qT[:, ch, q0:q0 + 512])],
                                 KTB.s(kv, kt * 128, kt * 128 + 128) + QT.s(ch, q0, q0 + 512), [PB[sbk]])
                            ei = e4()
                            S.act(etb[ei], ps[:, sbk, :], AF.Exp, [PB[sbk]], [ETB[ei]], scale=scale)
                            return ei

                        def pv(st, kt, ei, kv=kv, obs=obs):
                            ob = obs[st]
                            S.op("pe", lambda e: e.matmul(ps[:, ob, :], lhsT=VA[:, kt, kv, :], rhs=etb[ei],
                                                          start=(kt == 0), stop=(kt == 11)),
                                 [VAB[kt], ETB[ei]], [PB[ob]])
                        cur = [s_exp(0, 0), s_exp(1, 0)]
                        for kt in range(12):
                            for st in range(2):
                                nxt = s_exp(st, kt + 1) if kt + 1 < 12 else None
                                pv(st, kt, cur[st])
                                cur[st] = nxt
                        for st in range(2):
                            ob, q0 = obs[st], st * 512
                            ri = rr()
                            S.op("dve", lambda e, ri=ri, ob=ob: e.reciprocal(out=recb[ri][64:128, :], in_=ps[64:128, ob, :]),
                                 [PB[ob]], [RECB[ri]])
                            S.tt(obr[0][base:base + 64, ch, q0:q0 + 512], ps[0:64, ob, :], recb[ri][64:128, :],
                                 ALU.mult, [PB[ob], RECB[ri]], OBR[0].s(ch, q0, q0 + 512))
                else:
                    aunits = [(sq_, p_) for sq_ in range(4) for p_ in range(4)]

                    def atA(u):
                        sq_, p_ = u
                        kv, ch, q0 = p_ // 2, p_, sq_ * 256
                        eis = []
                        for kt in range(2):
                            k0 = sq_ * 256 + kt * 128
                            sbk = srot()

                            def smm(e, sbk=sbk, kv=kv, ch=ch, q0=q0, k0=k0):
                                e.matmul(ps[:, sbk, 0:256], lhsT=KT[:, kv, 0, k0:k0 + 128], rhs=qT[:, ch, q0:q0 + 256],
                                         start=True, stop=True)
                                return e.matmul(ps[:, sbk, 256:512], lhsT=KT[:, kv, 1, k0:k0 + 128],
                                                rhs=qT[:, ch, q0:q0 + 256], start=True, stop=True)
                            S.op("pe", smm, KTB.s(kv, k0, k0 + 128) + QT.s(ch, q0, q0 + 256), [PB[sbk]])
                            ei = erot()
                            S.act(etb[ei], ps[:, sbk, :], AF.Exp, [PB[sbk]], [ETB[ei]], scale=scale)
                            eis.append(ei)
                        return eis

                    def atB(u, eis):
                        sq_, p_ = u
                        kv, ch, q0 = p_ // 2, p_, sq_ * 256
                        ob = orot()

                        def pvm(e, ob=ob, kv=kv, sq_=sq_, eis=eis):
                            ins = None
                            for z in range(2):
                                for kt in range(2):
                                    ins = e.matmul(ps[:, ob, z * 256:(z + 1) * 256], lhsT=VA[:, 2 * sq_ + kt, kv, :],
                                                   rhs=etb[eis[kt]][:, z * 256:(z + 1) * 256], start=(kt == 0), stop=(kt == 1))
                            return ins
                        S.op("pe", pvm, [VAB[2 * sq_], VAB[2 * sq_ + 1], ETB[eis[0]], ETB[eis[1]]], [PB[ob]])
                        ri = rr()
                        S.act(recb[ri][64:128, :], ps[64:128, ob, :], AF.Ln, [PB[ob]], [RECB[ri]])
                        S.act(recb[ri][64:128, :], recb[ri][64:128, :], AF.Exp, [RECB[ri]], [RECB[ri]], scale=-1.0)
                        for z in range(2):
                            S.tt(obr[0][z * 64:(z + 1) * 64, ch, q0:q0 + 256], ps[0:64, ob, z * 256:(z + 1) * 256],
                                 recb[ri][64:128, z * 256:(z + 1) * 256], ALU.mult, [PB[ob], RECB[ri]],
                                 OBR[0].s(ch, q0, q0 + 256))
                    prev = None
                    for u in aunits:
                        drain("att", 1)
                        eis = atA(u)
                        if prev is not None:
                            atB(*prev)
                        prev = (u, eis)
                    atB(*prev)
                drain("att", -1)

            def ph_ret():
                AR.reset()
                qr = AR.alloc([2, NT], BF16)
                kr = AR.alloc([2, 2, NT], BF16)
                vr = AR.alloc([8, 512], BF16)
                sg = AR.alloc([4, NT], BF16)
                atb = [AR.alloc([512], BF16) for _ in range(4)]
                tmk = [AR.alloc([TW], BF16) for _ in range(2)]
                sqb = [AR.alloc([512], BF16) for _ in range(2)]
                rsb = [AR.alloc([512], F32) for _ in range(2)]
                t1b = [AR.alloc([512], F32) for _ in range(2)]
                QR, KR = TB("qr", 2), TB("kr", 2)
                VRB = [Buf("vr%d" % i) for i in range(8)]
                SG = TB("sg", 4)
                ATB = [Buf("at%d" % i) for i in range(4)]
                TMK = [Buf("tmk%d" % i) for i in range(2)]
                wb1, WB1 = load_w(l, 2, 4096)
                wb2, WB2 = load_w(l, 3, 4096)
                wb3, WB3 = load_w(l, 4, 4096)
                if smp:
                    qd = AR.alloc([4, NT], BF16)
                    dqt = AR.alloc([2, 2, NT], BF16)
                    s0t = AR.alloc([4, 128], BF16)
                    QD, DQT, S0B = TB("qd", 4), Buf("dqt"), Buf("s0")
                    S.dma("sp", dqt, decq_d[l], reads=DQDH[l], writes=[DQT])
                    S.dma("pool", s0t, s0d[l], writes=[S0B], arena=True)
                else:
                    ktm = AR.alloc([8, 4, 128], BF16)
                    sst = [AR.alloc([512], F32) for _ in range(2)]
                    KTM = [Buf("ktm%d" % i) for i in range(8)]
                    SST = [Buf("sst%d" % i) for i in range(2)]
                S.memset(kr, 0.0, KR.s([0, 1]))

                w1 = slabview(wb1, 0, 8, 512)
                for tg in range(2):
                    for j in range(2):
                        t0 = tg * 512
                        b = brot_()
                        S.mm(ps[:, b, :], [(w1[:, kc, j * 128:(j + 1) * 128], hT[:, kc, t0:t0 + 512]) for kc in range(8)],
                             [WB1] + HT.s(range(8), t0, t0 + 512), [PB[b]])
                        S.copy(qr[:, j, t0:t0 + 512], ps[:, b, :], [PB[b]], QR.s(j, t0, t0 + 512), eng="act")
                        b = brot_()
                        S.mm(ps[:, b, :], [(w1[:, kc, 256 + j * 128:256 + (j + 1) * 128], hT[:, kc, t0:t0 + 512])
                                           for kc in range(8)],
                             [WB1] + HT.s(range(8), t0, t0 + 512), [PB[b]])
                        for z in range(2):
                            S.ts(kr[z * 64:(z + 1) * 64, j, z, t0:t0 + 512], ps[z * 64:(z + 1) * 64, b, :], 0.125, None,
                                 ALU.mult, None, [PB[b]], KR.s(j, t0, t0 + 512))
                if smp:
                    for h in range(4):
                        j, bs = h // 2, (h % 2) * 64
                        for dr in range(2):
                            S.tt(qd[dr * 64:(dr + 1) * 64, h, :], qr[bs:bs + 64, j, :], dqt[bs:bs + 64, j, dr, :], ALU.mult,
                                 QR.s(j) + [DQT], QD.s(h))
                if not smp:
                    for tt in range(8):
                        b = brot_()
                        S.mm(ps[:, b, 0:256], [(hT[:, kc, tt * 128:(tt + 1) * 128], w1[:, kc, 256:512]) for kc in range(8)],
                             [WB1] + HT.s(range(8), tt * 128, tt * 128 + 128), [PB[b]])
                        for dr in range(2):
                            for h in range(4):
                                S.ts(ktm[:, tt, h, dr * 64:(dr + 1) * 64], ps[:, b, h * 64:(h + 1) * 64],
                                     kdec[:, l, tt % 2, dr * 4 + h:dr * 4 + h + 1], 0.125, ALU.mult, ALU.mult,
                                     [PB[b], KDEC], [KTM[tt]])
                w2 = slabview(wb2, 0, 8, 512)
                for tt in range(8):
                    b = brot_()
                    S.mm(ps[:, b, :], [(hT[:, kc, tt * 128:(tt + 1) * 128], w2[:, kc, :]) for kc in range(8)],
                         [WB2] + HT.s(range(8), tt * 128, tt * 128 + 128), [PB[b]])
                    S.copy(vr[:, tt, :], ps[:, b, :], [PB[b]], [VRB[tt]], eng=("act" if tt % 2 else "dve"))
                    drain("ret", 1)
                w3 = slabview(wb3, 0, 8, 512)
                for h in range(4):
                    for tg in range(2):
                        t0 = tg * 512
                        b = brot_()
                        S.mm(ps[:, b, :], [(w3[:, kc, h * 128:(h + 1) * 128], hT[:, kc, t0:t0 + 512]) for kc in range(8)],
                             [WB3] + HT.s(range(8), t0, t0 + 512), [PB[b]])
                        S.act(sg[:, h, t0:t0 + 512], ps[:, b, :], AF.Silu, [PB[b]], SG.s(h, t0, t0 + 512))
                        drain("ret", 1)

                drain("ret", -1)
                if smp:
                    obanks = Rot([4, 6])
                    rsb_ = [Rot([0, 1]), Rot([2, 3])]
                    mrot = Rot([0, 2, 1, 3])
                else:
                    orot = Rot([6, 7])
                    srot = Rot([0, 1, 2, 3])
                    mrot = Rot([4, 5])
                arot = Rot(range(4))
                trot = Rot(range(2))
                pend_ret = [None]
                rr = Rot(range(2))

                def post_norm(ob, h, n0, W, b2=None):
                    i = rr()
                    S.act(sqb[i][:, 0:W], ps[:, ob, 0:W], AF.Square, [PB[ob]], [SQB[i]])
                    if b2 is None:
                        b2 = mrot()
                    S.mm(ps[:, b2, 0:W], [(ones128, sqb[i][:, 0:W])], [SQB[i], CONST], [PB[b2]])
                    rstd_from(ps[:, b2, 0:W], PB[b2], rsb[i][:, 0:W], RSB[i])
                    S.stt(t1b[i][:, 0:W], ps[:, ob, 0:W], pp[:, P0 + 66:P0 + 67], rsb[i][:, 0:W],
                          ALU.mult, ALU.mult, [PB[ob], RSB[i], CONST], [T1B[i]])
                    S.tt(obr[1][:, h, n0:n0 + W], t1b[i][:, 0:W], sg[:, h, n0:n0 + W], ALU.mult,
                         [T1B[i]] + SG.s(h, n0, n0 + W), OBR[1].s(h, n0, n0 + W))

                for h in range(4):
                    z, j = h % 2, h // 2
                    ti = trot()
                    S.dma("sp", tmk[ti], tmask_d[l, h], reads=TMDQ[l][h], writes=[TMK[ti]])
                    if smp:
                        ob0 = obanks()
                        obs = [ob0, ob0 + 1]
                        for st in range(2):
                            S.op("pe", lambda e, ob=obs[st], h=h, n0=st * 512: e.matmul(
                                ps[:, ob, :], lhsT=s0t[:, h, :], rhs=qd[:, h, n0:n0 + 512], start=True, stop=False),
                                 [S0B] + QD.s(h, st * 512, st * 512 + 512), [PB[obs[st]]])

                        def sc_mask(st, mt, j=j, z=z, ti=ti):
                            n0, m0 = st * 512, mt * 128
                            sbk = rsb_[st]()
                            S.mm(ps[:, sbk, :], [(kr[:, j, z, m0:m0 + 128], qr[:, j, n0:n0 + 512])],
                                 KR.s(j, m0, m0 + 128) + QR.s(j, n0, n0 + 512), [PB[sbk]])
                            ai = arot()
                            off = TOFF + n0 - m0
                            S.tt(atb[ai], ps[:, sbk, :], tmk[ti][:, off:off + 512], ALU.mult,
                                 [PB[sbk], TMK[ti]], [ATB[ai]])
                            return ai
                        cur = [sc_mask(0, 0), sc_mask(1, 0)]
                        if pend_ret[0] is not None:
                            pend_ret[0]()
                            pend_ret[0] = None
                        for mt in range(8):
                            for st in range(2):
                                ai = cur[st]
                                S.op("pe", lambda e, ob=obs[st], mt=mt, ai=ai, h=h: e.matmul(
                                    ps[:, ob, :], lhsT=vr[:, mt, h * 128:(h + 1) * 128], rhs=atb[ai],
                                    start=False, stop=(mt == 7)), [VRB[mt], ATB[ai]], [PB[obs[st]]])
                                cur[st] = sc_mask(st, mt + 1) if mt + 1 < 8 else None
                        pend_ret[0] = (lambda obs=obs, h=h: [post_norm(obs[st_], h, st_ * 512, 512, b2=rsb_[st_]())
                                                            for st_ in range(2)])
                    else:
                        def rtA(sp, j=j, z=z, ti=ti):
                            ais = []
                            for mt in range(2):
                                sbk = srot()

                                def smm(e, sbk=sbk, mt=mt, sp=sp, j=j, z=z):
                                    ins = None
                                    for a_ in range(2):
                                        q0 = sp * 512 + a_ * 256
                                        ins = e.matmul(ps[:, sbk, a_ * 256:(a_ + 1) * 256],
                                                       lhsT=kr[:, j, z, q0 + mt * 128:q0 + mt * 128 + 128],
                                                       rhs=qr[:, j, q0:q0 + 256], start=True, stop=True)
                                    return ins
                                S.op("pe", smm, KR.s(j, sp * 512, sp * 512 + 512) + QR.s(j, sp * 512, sp * 512 + 512), [PB[sbk]])
                                ai = arot()
                                off = TOFF - mt * 128
                                S.tt(atb[ai].rearrange("p (a n) -> p a n", a=2), ps[:, sbk, :].rearrange("p (a n) -> p a n", a=2),
                                     tmk[ti][:, off:off + 256].unsqueeze(1).to_broadcast([128, 2, 256]), ALU.mult,
                                     [PB[sbk], TMK[ti]], [ATB[ai]])
                                ais.append(ai)
                            return ais

                        def rtB(sp, ais, h=h):
                            ob = orot()

                            def pvm(e, ob=ob, sp=sp, ais=ais, h=h):
                                ins = None
                                for a_ in range(2):
                                    for mt in range(2):
                                        ins = e.matmul(ps[:, ob, a_ * 256:(a_ + 1) * 256],
                                                       lhsT=vr[:, sp * 4 + a_ * 2 + mt, h * 128:(h + 1) * 128],
                                                       rhs=atb[ais[mt]][:, a_ * 256:(a_ + 1) * 256],
                                                       start=(mt == 0), stop=(mt == 1))
                                return ins
                            S.op("pe", pvm, [VRB[sp * 4 + k_] for k_ in range(4)] + [ATB[ais[0]], ATB[ais[1]]], [PB[ob]])
                            post_norm(ob, h, sp * 512, 512)
                        for sp_ in range(2):
                            ais_ = rtA(sp_)
                            if pend_ret[0] is not None:
                                pend_ret[0]()
                            pend_ret[0] = (lambda sp_=sp_, ais_=ais_, rtB=rtB: rtB(sp_, ais_))
                if pend_ret[0] is not None:
                    pend_ret[0]()
                    pend_ret[0] = None
                if not smp:
                    for s in range(4):
                        b = mrot()
                        for h in range(4):
                            S.mm(ps[:, b, h * 128:(h + 1) * 128],
                                 [(ktm[:, 2 * s + i2, h, :], vr[:, 2 * s + i2, h * 128:(h + 1) * 128]) for i2 in range(2)],
                                 [KTM[2 * s], KTM[2 * s + 1], VRB[2 * s], VRB[2 * s + 1]], [PB[b]])
                        si = s % 2
                        S.copy(sst[si], ps[:, b, :], [PB[b]], [SST[si]])
                        out_toks.append(S.dma("sp", sf_out[l, s].rearrange("h k v -> k h v"),
                                              sst[si][0:64, :].rearrange("p (h v) -> p h v", h=4), reads=[SST[si]]))
                        out_toks.append(S.dma("sp", sb_out[l, s].rearrange("h k v -> k h v"),
                                              sst[si][64:128, :].rearrange("p (h v) -> p h v", h=4), reads=[SST[si]]))

            def ph_four():
                AR.reset()
                uT = AR.alloc([4, NT], BF16)
                Y = AR.alloc([8, 4, 256], BF16)
                UT = TB("uT", 4)
                YB = [Buf("y%d" % i) for i in range(8)]
                wb4, WB4 = load_w(l, 5, 4096)
                w4 = slabview(wb4, 0, 8, 512)
                for g in range(4):
                    for tg in range(2):
                        t0 = tg * 512
                        b = brot_()
                        S.mm(ps[:, b, :], [(w4[:, kc, g * 128:(g + 1) * 128], hT[:, kc, t0:t0 + 512]) for kc in range(8)],
                             [WB4] + HT.s(range(8), t0, t0 + 512), [PB[b]])
                        S.copy(uT[:, g, t0:t0 + 512], ps[:, b, :], [PB[b]], UT.s(g, t0, t0 + 512),
                               eng=("act" if tg else "dve"))
                        drain("four", 1)
                for tt in range(8):
                    for gp in range(2):
                        b = brot_()
                        for g in (2 * gp, 2 * gp + 1):
                            S.mm(ps[:, b, (g % 2) * 256:(g % 2) * 256 + 256],
                                 [(uT[:, g, tt * 128:(tt + 1) * 128], cs128)], UT.s(g, tt * 128, tt * 128 + 128) + [CONST],
                                 [PB[b]])
                        S.copy(Y[:, tt, 2 * gp:2 * gp + 2, :], ps[:, b, :].rearrange("p (g n) -> p g n", g=2),
                               [PB[b]], [YB[tt]], eng=("act" if gp else "dve"))
                        drain("four", 1)
                if smp:
                    for hf in range(2):
                        ct, CTB = load_slab(lambda wbt, hf=hf: [(wbt[:, 0:4096], dft1024d[2 * hf].rearrange("p a b -> p (a b)"))])
                        stt_, STB = load_slab(lambda wbt, hf=hf: [(wbt[:, 0:4096], dft1024d[2 * hf + 1].rearrange("p a b -> p (a b)"))])
                        cv_, sv_ = slabview(ct, 0, 8, 512), slabview(stt_, 0, 8, 512)
                        for g in range(4):
                            b = brot_()
                            pairs = []
                            for nt in range(8):
                                pairs.append((Y[:, nt, g, 0:128], cv_[:, nt, :]))
                                pairs.append((Y[:, nt, g, 128:256], sv_[:, nt, :]))
                            S.mm(ps[:, b, :], pairs, YB + [CTB, STB], [PB[b]])
                            S.copy(obr[2][:, g, hf * 512:(hf + 1) * 512], ps[:, b, :], [PB[b]],
                                   OBR[2].s(g, hf * 512, hf * 512 + 512), eng=("act" if g % 2 else "dve"))
                else:
                    dt_, DTB = load_slab(lambda wbt: [(wbt[:, 0:1024], dft256d.rearrange("p a b c -> p (a b c)"))])
                    dv = dt_[:, 0:1024].rearrange("p (a b c) -> p a b c", a=2, b=2)
                    for s in range(4):
                        for g in range(4):
                            b = brot_()
                            pairs = []
                            for nt in range(2):
                                pairs.append((Y[:, 2 * s + nt, g, 0:128], dv[:, 0, nt, :]))
                                pairs.append((Y[:, 2 * s + nt, g, 128:256], dv[:, 1, nt, :]))
                            S.mm(ps[:, b, 0:256], pairs, [YB[2 * s], YB[2 * s + 1], DTB], [PB[b]])
                            S.copy(obr[2][:, g, s * 256:(s + 1) * 256], ps[:, b, 0:256], [PB[b]],
                                   OBR[2].s(g, s * 256, s * 256 + 256), eng=("act" if g % 2 else "dve"))
                            drain("four", 1)
                drain("four", -1)

            SQB = [Buf("sqb%d" % i) for i in range(4)]
            RSB = [Buf("rsb%d" % i) for i in range(4)]
            T1B = [Buf("t1b%d" % i) for i in range(4)]
            if defer_mod:
                ph_att()
                ph_four()
                ph_ret()
            else:
                ph_att()
                ph_ret()
                ph_four()

            AR.reset()
            mg = AR.alloc([8, NT], BF16)
            sgt = [AR.alloc([512], F32) for _ in range(3)]
            mac = [AR.alloc([512], F32) for _ in range(2)]
            mtp = [AR.alloc([512], F32) for _ in range(2)]
            if cur_pass[0] == 0:
                mod_ext[0] = ([AR.alloc([4096], BF16) for _ in range(4)], [Buf("my%d" % k_) for k_ in range(4)], Rot(range(4)))
            MG = TB("mg", 8)
            SGT = [Buf("sgt%d" % i) for i in range(3)]
            MAC = [Buf("mac%d" % i) for i in range(2)]
            MTP = [Buf("mtp%d" % i) for i in range(2)]
            grot = Rot(range(3))
            for j in range(8):
                wbm, WBm = load_w(l, 6 + j, 4608)
                gv = wbm[:, 0:3072].rearrange("p (k b n) -> p k b n", k=8, b=3)
                bv = wbm[:, 3072:4608].rearrange("p (k b n) -> p k b n", k=4, b=3)
                for tg in range(2):
                    t0 = tg * 512
                    mi = tg
                    for br in range(3):
                        bP = brot_()
                        S.mm(ps[:, bP, :], [(bv[:, kc, br, :], obr[br][:, kc, t0:t0 + 512]) for kc in range(4)],
                             [WBm] + OBR[br].s(range(4), t0, t0 + 512), [PB[bP]])
                        bG = brot_()
                        S.mm(ps[:, bG, :], [(gv[:, kc, br, :], hT[:, kc, t0:t0 + 512]) for kc in range(8)],
                             [WBm] + HT.s(range(8), t0, t0 + 512), [PB[bG]])
                        gi = grot()
                        S.act(sgt[gi], ps[:, bG, :], AF.Sigmoid, [PB[bG]], [SGT[gi]])
                        if br == 0:
                            S.tt(mac[mi], sgt[gi], ps[:, bP, :], ALU.mult, [SGT[gi], PB[bP]], [MAC[mi]])
                        else:
                            S.tt(mtp[mi], sgt[gi], ps[:, bP, :], ALU.mult, [SGT[gi], PB[bP]], [MTP[mi]])
                            if br == 1:
                                S.tt(mac[mi], mac[mi], mtp[mi], ALU.add, [MTP[mi]], [MAC[mi]])
                            else:
                                S.tt(mg[:, j, t0:t0 + 512], mac[mi], mtp[mi], ALU.add, [MTP[mi], MAC[mi]],
                                     MG.s(j, t0, t0 + 512))
                    if j >= 1:
                        drain("merge", 2)
            wos = []
            for io in range(2):
                wbo, WBo = load_w(l, 14 + io, 4096)
                wos.append((slabview(wbo, 0, 8, 512), WBo))
            for tg in range(2):
                t0 = tg * 512
                for i in range(8):
                    wo, WBo = wos[i // 4]
                    ii = i % 4
                    b = brot_()
                    S.mm(ps[:, b, :], [(wo[:, kc, ii * 128:(ii + 1) * 128], mg[:, kc, t0:t0 + 512]) for kc in range(8)],
                         [WBo] + MG.s(range(8), t0, t0 + 512), [PB[b]])
                    S.stt(xT[:, i, t0:t0 + 512], ps[:, b, :], md[:, 16 + i:17 + i], xT[:, i, t0:t0 + 512],
                          ALU.mult, ALU.add, [PB[b], MOD], XT.s(i, t0, t0 + 512))
                    drain("merge", 2)

            drain("merge", -1)

            S.stt(abT[:, 1, :], md[:, 32:40], 1.0, pp[:, P0 + 8:P0 + 16], ALU.add, ALU.mult, [MOD, CONST], [ABB])
            norm_mod(abT[:, 1, :], md[:, 24:32], brot_, MOD)
            AR.reset()
            aT = AR.alloc([NFF, NT], BF16)
            acc = [AR.alloc([NT], F32) for _ in range(2)]
            gl = [AR.alloc([NT], F32) for _ in range(2)]
            ACT_ = TB("aT", NFF)
            ACC = [Buf("acc%d" % i) for i in range(2)]
            GL = [Buf("gl%d" % i) for i in range(2)]
            CW = P0 + 67
            CBc = P0 + 67 + 66

            def segs(ap2d, lo, hi):
                return ap2d.rearrange("p (s l) -> p s l", s=nseq)[:, :, lo:hi]
            for cp in range(11):
                wbu, WBu = load_w(l, 16 + cp, 4096)
                wa, wv = slabview(wbu, 0, 8, 256), slabview(wbu, 2048, 8, 256)
                for cc in range(2):
                    c = 2 * cp + cc
                    a0, v0 = (0, 4) if c % 2 == 0 else (2, 6)
                    for tg in range(2):
                        t0 = tg * 512
                        if cp == 0 and cc == 0:
                            S.mm_split(ps[:, a0 + tg, :], [(wa[:, kc, cc * 128:(cc + 1) * 128], hT[:, kc, t0:t0 + 512]) for kc in range(8)],
                                       [HT.s(kc, t0, t0 + 512) for kc in range(8)], [WBu], [PB[a0 + tg]])
                        else:
                            S.mm(ps[:, a0 + tg, :], [(wa[:, kc, cc * 128:(cc + 1) * 128], hT[:, kc, t0:t0 + 512]) for kc in range(8)],
                                 [WBu] + HT.s(range(8), t0, t0 + 512), [PB[a0 + tg]])
                        S.mm(ps[:, v0 + tg, :], [(wv[:, kc, cc * 128:(cc + 1) * 128], hT[:, kc, t0:t0 + 512]) for kc in range(8)],
                             [WBu] + HT.s(range(8), t0, t0 + 512), [PB[v0 + tg]])
                    pa = ps[:, a0:a0 + 2, :].rearrange("p b n -> p (b n)")
                    pv_ = ps[:, v0:v0 + 2, :].rearrange("p b n -> p (b n)")
                    PA = [PB[a0], PB[a0 + 1]]
                    PV = [PB[v0], PB[v0 + 1]]
                    i = c % 2
                    S.act(acc[i], pa, AF.Identity, PA + [CONST], [ACC[i]],
                          bias=pp[:, CBc + c:CBc + c + 1], scale=pp[:, CW + 22 + c:CW + 22 + c + 1])
                    S.stt(segs(acc[i], 1, L), segs(pa, 0, L - 1), pp[:, CW + c:CW + c + 1], segs(acc[i], 1, L),
                          ALU.mult, ALU.add, PA + [CONST], [ACC[i]])
                    S.stt(segs(acc[i], 0, L - 1), segs(pa, 1, L), pp[:, CW + 44 + c:CW + 44 + c + 1],
                          segs(acc[i], 0, L - 1), ALU.mult, ALU.add, PA + [CONST], [ACC[i]])
                    S.act(gl[i], acc[i], AF.Gelu_apprx_tanh, [ACC[i]], [GL[i]])
                    S.tt(aT[:, c, :], gl[i], pv_, ALU.mult, [GL[i]] + PV, ACT_.s(c))
            for i in range(8):
                wbd, WBd = load_w(l, 27 + i, 2816)
                wd = slabview(wbd, 0, NFF, 128)
                for tg in range(2):
                    t0 = tg * 512
                    b = brot_()
                    S.mm(ps[:, b, :], [(wd[:, kc, :], aT[:, kc, t0:t0 + 512]) for kc in range(NFF)],
                         [WBd] + ACT_.s(range(NFF), t0, t0 + 512), [PB[b]])
                    S.stt(xT[:, i, t0:t0 + 512], ps[:, b, :], md[:, 40 + i:41 + i], xT[:, i, t0:t0 + 512],
                          ALU.mult, ALU.add, [PB[b], MOD], XT.s(i, t0, t0 + 512))

        for half in range(2):
            for c in range(8):
                S.dma("sp", xT[:, c, :], xin[half, :, c, :], writes=XT.s(c))
            for l in range(2):
                layer_pass(half, l)
            for c in range(8):
                out_toks.append(S.dma("sp", yout[half, :, c, :], xT[:, c, :], reads=XT.s(c)))
        S.wait_all("sp", out_toks)
        with nc.Block() as block:
            S.run(block)
    return nc


def _consts():
    bf = ml_dtypes.bfloat16
    cb = np.zeros((128, 768), np.float32)
    cb[:, 0:128] = 1.0 / 1024
    cb[:, 128:256] = 1.0 / 128
    for blk in range(2):
        cb[blk * 64:(blk + 1) * 64, 256 + blk * 64:256 + (blk + 1) * 64] = 1.0 / 64
    p = np.arange(128)
    cb[p ^ 16, 384 + p] = 1.0
    cidx = np.arange(128)[:, None] * np.arange(128)[None, :]
    cb[:, 512:640] = np.cos(2 * np.pi * cidx / 128) / np.sqrt(128)
    cb[:, 640:768] = np.sin(2 * np.pi * cidx / 128) / np.sqrt(128)
    d = p % 64
    axis, half, f = d // 32, (d % 32) // 16, d % 16
    inv = (10000.0 ** (-np.arange(16, dtype=np.float32) / 16)).astype(np.float32)
    tok = np.arange(NT)
    row_id, col_id = (tok // 64).astype(np.float32), (tok % 64).astype(np.float32)
    pos = np.where(axis[:, None] == 0, row_id[None, :], col_id[None, :]).astype(np.float32)
    ang = (pos * inv[f][:, None]).astype(np.float32)
    rope = np.zeros((128, 2, NT), np.float32)
    rope[:, 0, :] = np.cos(ang)
    rope[:, 1, :] = np.sin(ang) * np.where(half == 0, -1.0, 1.0)[:, None]

    def dft(Ln):
        n = np.arange(Ln)
        a = 2 * np.pi * ((n[:, None] * n[None, :]) % Ln) / Ln
        return np.cos(a) / np.sqrt(Ln), -np.sin(a) / np.sqrt(Ln)
    c256, s256 = dft(256)
    dft256 = np.zeros((128, 2, 2, 256), np.float32)
    for nt in range(2):
        dft256[:, 0, nt, :] = c256[nt * 128:(nt + 1) * 128, :]
        dft256[:, 1, nt, :] = s256[nt * 128:(nt + 1) * 128, :]
    c1k, s1k = dft(1024)
    dft1024 = np.zeros((4, 128, 8, 512), np.float32)
    for hf in range(2):
        for nt in range(8):
            dft1024[2 * hf, :, nt, :] = c1k[nt * 128:(nt + 1) * 128, hf * 512:(hf + 1) * 512]
            dft1024[2 * hf + 1, :, nt, :] = s1k[nt * 128:(nt + 1) * 128, hf * 512:(hf + 1) * 512]
    dd = (np.arange(TW)[None, :] - TOFF - np.arange(128)[:, None]).astype(np.float32)
    dtab = np.stack([np.maximum(dd, 0), np.maximum(-dd, 0), (dd == 0).astype(np.float32)]).astype(np.float32)
    n1 = np.stack([np.broadcast_to(tok + 1.0, (128, NT)), np.broadcast_to(1024.0 - tok, (128, NT))]).astype(np.float32)
    epp = np.zeros((128, 4), np.float32)
    for i in range(2):
        epp[:, i * 2 + 0] = 255 - (i * 128 + p)
        epp[:, i * 2 + 1] = i * 128 + p
    return dict(cb16=cb.astype(bf), rope=rope, dft256=dft256.astype(bf), dft1024=dft1024.astype(bf),
                dtab=dtab, n1tab=n1, epp=epp)


def _kcv(w, c0, n):
    kc = w.shape[0] // 128
    return w[:, c0:c0 + n].reshape(kc, 128, n).transpose(1, 0, 2)


def _pack_weights(w_ada, w_in, w_bra, w_brr, w_brf, w_out, w_up, w_down):
    wpk = np.zeros((2, NSLAB, 128, SLAB), np.float32)
    wad = np.zeros((2, 12, 128, 4096), np.float32)
    for l in range(2):
        wi = w_in[l]
        put = lambda idx, arr: wpk[l, idx, :, :arr.reshape(128, -1).shape[1]].__setitem__(slice(None), arr.reshape(128, -1))
        put(0, _kcv(wi, 0, 512))
        kvp = np.concatenate([_kcv(wi, 512, 64), _kcv(wi, 512, 64), _kcv(wi, 576, 64), _kcv(wi, 576, 64),
                              _kcv(wi, 640, 128)], axis=2)
        put(1, kvp)
        put(2, _kcv(wi, 768, 512))
        put(3, _kcv(wi, 1280, 512))
        put(4, _kcv(wi, 1792, 512))
        put(5, _kcv(wi, 2304, 512))
        brs = [w_bra[l], w_brr[l], w_brf[l]]
        for j in range(8):
            g = np.stack([_kcv(wi, 2816 + br * 1024 + j * 128, 128) for br in range(3)], axis=2)
            b = np.stack([_kcv(brs[br], j * 128, 128) for br in range(3)], axis=2)
            wpk[l, 6 + j, :, 0:3072] = g.reshape(128, -1)
            wpk[l, 6 + j, :, 3072:4608] = b.reshape(128, -1)
        for io in range(2):
            put(14 + io, _kcv(w_out[l], io * 512, 512))
        for cp in range(11):
            wpk[l, 16 + cp, :, 0:2048] = _kcv(w_up[l], cp * 256, 256).reshape(128, -1)
            wpk[l, 16 + cp, :, 2048:4096] = _kcv(w_up[l], D_FF + cp * 256, 256).reshape(128, -1)
        for i in range(8):
            put(27 + i, _kcv(w_down[l], i * 128, 128))
        for sidx in range(12):
            wad[l, sidx] = _kcv(w_ada[l], sidx * 512, 512).reshape(128, -1)
    return dict(wpk=wpk, wadapk=wad)


def _fm(v):
    return np.ascontiguousarray(v.reshape(-1, 128).T)


_CACHE = {}


def kernel(x_prompt, x_sample, cache_k, cache_v, state_ret_fwd, state_ret_bwd, c, c_ctx,
           w_ada, b_ada, norm1, w_in, q_norm, k_norm, ret_decay_f, ret_decay_b, ret_norm,
           w_br_att, w_br_ret, w_br_four, w_out, norm2, w_up, conv_w, conv_b, w_down):
    f32 = np.float32
    A = lambda a: np.ascontiguousarray(np.asarray(a, dtype=f32))
    x_prompt, x_sample, cache_k, cache_v = A(x_prompt), A(x_sample), A(cache_k), A(cache_v)
    state_ret_fwd, state_ret_bwd, c, c_ctx = A(state_ret_fwd), A(state_ret_bwd), A(c), A(c_ctx)
    if "nc" not in _CACHE:
        _CACHE["nc"] = build_program()
        _CACHE["consts"] = _consts()
    nc = _CACHE["nc"]
    consts = _CACHE["consts"]
    shared = _pack_weights(A(w_ada), A(w_in), A(w_br_att), A(w_br_ret), A(w_br_four), A(w_out), A(w_up), A(w_down))
    shared.update(consts)
    pp = np.zeros((128, 2 * PPL + 24), f32)
    rdf, rdb = A(ret_decay_f), A(ret_decay_b)
    for l in range(2):
        o = l * PPL
        pp[:, o:o + 8] = _fm(A(norm1)[l])
        pp[:, o + 8:o + 16] = _fm(A(norm2)[l])
        pp[:, o + 16:o + 64] = _fm(A(b_ada)[l])
        pp[:, o + 64] = np.tile(A(q_norm)[l], 2)
        pp[:, o + 65] = np.tile(A(k_norm)[l], 2)
        pp[:, o + 66] = A(ret_norm)[l]
        cw = A(conv_w)[l]
        for k in range(3):
            pp[:, o + 67 + k * 22:o + 67 + (k + 1) * 22] = _fm(cw[k])
        pp[:, o + 67 + 66:o + 67 + 88] = _fm(A(conv_b)[l])
        for dr, rd in enumerate((rdf, rdb)):
            for h in range(4):
                pp[:, 2 * PPL + l * 8 + dr * 4 + h] = rd[l, h]
            for j in range(2):
                pp[0:64, 2 * PPL + 16 + l * 4 + dr * 2 + j] = rd[l, 2 * j]
                pp[64:128, 2 * PPL + 16 + l * 4 + dr * 2 + j] = rd[l, 2 * j + 1]
    in_maps = []
    for i in range(8):
        m = dict(shared)
        xp = x_prompt[4 * i:4 * i + 4].reshape(NT, D)
        xs = x_sample[i]
        xin = np.stack([xp.reshape(NT, 8, 128).transpose(2, 1, 0), xs.reshape(NT, 8, 128).transpose(2, 1, 0)])
        m["xin"] = np.ascontiguousarray(xin)
        m["condT"] = np.ascontiguousarray(np.stack([_fm(c_ctx), _fm(c[i])], axis=-1))
        m["pp"] = pp
        ck = cache_k[i]
        ckt = ck.transpose(0, 2, 3, 1)
        ckz = np.zeros((2, 2, 2, 128, 512), f32)
        ckz[:, :, 0, 0:64, :] = ckt
        ckz[:, :, 1, 64:128, :] = ckt
        m["ckT"] = ckz
        m["cv"] = np.ascontiguousarray(cache_v[i].reshape(2, 512, 128))
        s0 = np.stack([state_ret_fwd[i], state_ret_bwd[i]], axis=0)
        m["s0"] = np.ascontiguousarray(s0.transpose(1, 0, 3, 2, 4).reshape(2, 128, 4, 128))
        in_maps.append(m)
    res = run_bass_kernel_spmd(nc, in_maps, core_ids=list(range(8)))
    R = res.results
    y_prompt = np.zeros((32, 256, D), f32)
    y_sample = np.zeros((8, 1024, D), f32)
    nck = np.zeros((32, 2, 256, 2, 64), f32)
    ncv = np.zeros((32, 2, 256, 2, 64), f32)
    nsf = np.zeros((32, 2, 4, 64, 128), f32)
    nsb = np.zeros((32, 2, 4, 64, 128), f32)
    for i in range(8):
        r = R[i]
        yo = np.asarray(r["yout"], f32)
        yt = yo.transpose(0, 3, 2, 1).reshape(2, NT, D)
        y_prompt[4 * i:4 * i + 4] = yt[0].reshape(4, 256, D)
        y_sample[i] = yt[1]
        cko = np.asarray(r["ck_out"], f32).reshape(2, 2, 64, 4, 256)
        nck[4 * i:4 * i + 4] = cko.transpose(3, 0, 4, 1, 2)
        cvo = np.asarray(r["cv_out"], f32).reshape(2, 4, 256, 2, 64)
        ncv[4 * i:4 * i + 4] = cvo.transpose(1, 0, 2, 3, 4)
        nsf[4 * i:4 * i + 4] = np.asarray(r["sf_out"], f32).transpose(1, 0, 2, 3, 4)
        nsb[4 * i:4 * i + 4] = np.asarray(r["sb_out"], f32).transpose(1, 0, 2, 3, 4)
    return (y_prompt, y_sample, nck, ncv, nsf, nsb)
```

```python
import contextlib
import math
import numpy as np
import ml_dtypes
import concourse.bass as bass
import concourse.mybir as mybir
from concourse.bass_utils import run_bass_kernel_spmd

F32 = mybir.dt.float32
BF16 = mybir.dt.bfloat16
AF = mybir.ActivationFunctionType
ALU = mybir.AluOpType

D = 1024
DEPTH = 2
NT = 1024
D_IN = 5888
D_FF = 2816
NFF = 22
EPS = 1e-6
TW = 1920
TOFF = 896
PPL = 155
SLAB = 4608
NWB = 4
NSLAB = 35
SERIALIZE = False


class Buf:
    __slots__ = ("name", "w", "r")

    def __init__(self, name):
        self.name = name
        self.w = None
        self.r = []


class TB:
    def __init__(self, name, nch, ntb=4):
        self.b = [[Buf("%s_%d_%d" % (name, c, t)) for t in range(ntb)] for c in range(nch)]

    def s(self, chunks, t0=0, t1=NT):
        if isinstance(chunks, int):
            chunks = [chunks]
        return [self.b[c][t] for c in chunks for t in range(t0 // 256, (t1 + 255) // 256)]


class Sched:
    ENGS = ("pe", "act", "dve", "pool", "sp")

    def __init__(self, nc, stack, serialize=False, ring=8):
        self.nc = nc
        self.sem = {e: stack.enter_context(nc.semaphore("s_" + e)) for e in self.ENGS}
        self.cnt = {e: 0 for e in self.ENGS}
        self.waited = {e: {} for e in self.ENGS}
        self.prog = {e: [] for e in self.ENGS}
        self.rings = {}
        for q in ("sp", "pool"):
            self.rings[q] = [[stack.enter_context(nc.semaphore("r_%s%d" % (q, i))), 0] for i in range(ring)]
        self.ring_pos = {q: 0 for q in self.rings}
        self.serialize = serialize
        self.last_tok = None
        self.pool_bar = None
        self.skip_self = False

    def _wait(self, eng, tok):
        if tok is None:
            return
        sem, val = tok
        if self.skip_self and sem is self.sem[eng]:
            return
        k = id(sem)
        if self.waited[eng].get(k, 0) >= val:
            return
        self.waited[eng][k] = val
        self.prog[eng].append(("wait", sem, val))

    def _deps(self, eng, reads, writes):
        for b in reads:
            self._wait(eng, b.w)
        for b in writes:
            self._wait(eng, b.w)
            for t in b.r:
                self._wait(eng, t)
        if self.serialize:
            self._wait(eng, self.last_tok)

    def _commit(self, tok, reads, writes):
        for b in reads:
            b.r.append(tok)
            if len(b.r) > 48:
                best = {}
                for s, v in b.r:
                    if best.get(id(s), (None, 0))[1] < v:
                        best[id(s)] = (s, v)
                b.r = list(best.values())
        for b in writes:
            b.w = tok
            b.r = []
        self.last_tok = tok

    def op(self, eng, fn, reads=(), writes=()):
        self._deps(eng, reads, writes)
        self.cnt[eng] += 1
        tok = (self.sem[eng], self.cnt[eng])
        self.prog[eng].append(("op", fn, self.sem[eng]))
        self._commit(tok, reads, writes)
        return tok

    def dma(self, q, out, in_, reads=(), writes=(), arena=False):
        if arena and q == "pool" and self.pool_bar:
            for t in self.pool_bar:
                self._wait(q, t)
            self.pool_bar = None
        self._deps(q, reads, writes)
        ring = self.rings[q]
        i = self.ring_pos[q]
        self.ring_pos[q] = (i + 1) % len(ring)
        sem, n = ring[i]
        if n > 0:
            self._wait(q, (sem, 16 * n))
        ring[i][1] = n + 1
        tok = (sem, 16 * (n + 1))
        self.prog[q].append(("dma", out, in_, sem))
        self._commit(tok, reads, writes)
        return tok

    def mm(self, out, pairs, reads, writes):
        pairs = list(pairs)

        def fn(e):
            n = len(pairs)
            ins = None
            for i, (l, r) in enumerate(pairs):
                ins = e.matmul(out, lhsT=l, rhs=r, start=(i == 0), stop=(i == n - 1))
            return ins
        return self.op("pe", fn, reads, writes)

    def mm_split(self, out, pairs, reads_list, common_reads, writes):
        pairs = list(pairs)
        n = len(pairs)
        for i, (l_, r_) in enumerate(pairs):
            self.skip_self = (i > 0)
            self.op("pe", lambda e, i=i, l_=l_, r_=r_: e.matmul(out, lhsT=l_, rhs=r_, start=(i == 0), stop=(i == n - 1)),
                    list(reads_list[i]) + (list(common_reads) if i == 0 else []), writes)
        self.skip_self = False

    def act(self, out, in_, func, reads, writes, bias=None, scale=None):
        kw = {}
        if bias is not None:
            kw["bias"] = bias
        if scale is not None:
            kw["scale"] = scale
        return self.op("act", lambda e: e.activation(out=out, in_=in_, func=func, **kw), reads, writes)

    def tt(self, out, in0, in1, op, reads, writes, eng="dve"):
        return self.op(eng, lambda e: e.tensor_tensor(out=out, in0=in0, in1=in1, op=op), reads, writes)

    def ts(self, out, in0, s1, s2, op0, op1, reads, writes, eng="dve"):
        if op1 is None:
            return self.op(eng, lambda e: e.tensor_scalar(out=out, in0=in0, scalar1=s1, scalar2=None, op0=op0),
                           reads, writes)
        return self.op(eng, lambda e: e.tensor_scalar(out=out, in0=in0, scalar1=s1, scalar2=s2, op0=op0, op1=op1),
                       reads, writes)

    def stt(self, out, in0, scalar, in1, op0, op1, reads, writes):
        return self.op("dve", lambda e: e.scalar_tensor_tensor(out=out, in0=in0, scalar=scalar, in1=in1,
                                                                op0=op0, op1=op1), reads, writes)

    def copy(self, out, in_, reads, writes, eng="dve"):
        if eng == "act":
            return self.op(eng, lambda e: e.activation(out=out, in_=in_, func=AF.Copy), reads, writes)
        return self.op(eng, lambda e: e.tensor_copy(out=out, in_=in_), reads, writes)

    def barrier(self):
        toks = [(self.sem[e], self.cnt[e]) for e in self.ENGS if self.cnt[e] > 0]
        for q in self.rings:
            for sem, n in self.rings[q]:
                if n > 0:
                    toks.append((sem, 16 * n))
        for e in ("act", "dve", "sp"):
            for t in toks:
                self._wait(e, t)
        self.pool_bar = toks

    def memset(self, ap, val, writes, eng="dve"):
        return self.op(eng, lambda e: e.memset(ap, val), (), writes)

    def wait_all(self, eng, toks):
        for t in toks:
            self._wait(eng, t)

    def run(self, block):
        def mk(eng):
            def body(e):
                for it in self.prog[eng]:
                    if it[0] == "wait":
                        e.wait_ge(it[1], it[2])
                    elif it[0] == "op":
                        it[1](e).then_inc(it[2], 1)
                    else:
                        e.dma_start(out=it[1], in_=it[2]).then_inc(it[3], 16)
            return body
        block.tensor(mk("pe"))
        block.scalar(mk("act"))
        block.vector(mk("dve"))
        block.gpsimd(mk("pool"))
        block.sync(mk("sp"))


class Rot:
    def __init__(self, items):
        self.items = list(items)
        self.i = 0

    def __call__(self):
        v = self.items[self.i % len(self.items)]
        self.i += 1
        return v


class Arena:
    def __init__(self, t, nelem16, sched=None):
        self.sched = sched
        self.t = t
        self.n = nelem16
        self.off = 0

    def reset(self, off=0):
        self.off = off
        if self.sched is not None:
            self.sched.barrier()

    def alloc(self, free, dt):
        n = int(np.prod(free))
        w = n * 2 if dt == F32 else n
        off = (self.off + 31) // 32 * 32
        assert off + w <= self.n, ("arena overflow", off, w, self.n)
        ap = self.t[:, off:off + w]
        if dt == F32:
            ap = ap.bitcast(F32)
        if len(free) == 2:
            ap = ap.rearrange("p (a b) -> p a b", a=free[0])
        elif len(free) == 3:
            ap = ap.rearrange("p (a b c) -> p a b c", a=free[0], b=free[1])
        elif len(free) == 4:
            ap = ap.rearrange("p (a b c d) -> p a b c d", a=free[0], b=free[1], c=free[2])
        self.off = off + w
        return ap


def build_program():
    nc = bass.Bass("TRN2", target_bir_lowering=False)

    def din(name, shape, dt=F32):
        return nc.dram_tensor(name, list(shape), dt, kind="ExternalInput").ap()

    def dout(name, shape, dt=F32):
        return nc.dram_tensor(name, list(shape), dt, kind="ExternalOutput").ap()

    xin = din("xin", [2, 128, 8, NT])
    condT = din("condT", [128, 8, 2])
    ppd = din("pp", [128, 2 * PPL + 24])
    wpk = din("wpk", [2, NSLAB, 128, SLAB])
    wadapk = din("wadapk", [2, 12, 128, 4096])
    ckT = din("ckT", [2, 2, 2, 128, 512])
    cvd = din("cv", [2, 512, 128])
    s0d = din("s0", [2, 128, 4, 128])
    cb16d = din("cb16", [128, 768], BF16)
    roped = din("rope", [128, 2, NT])
    dft256d = din("dft256", [128, 2, 2, 256], BF16)
    dft1024d = din("dft1024", [4, 128, 8, 512], BF16)
    dtabd = din("dtab", [3, 128, TW])
    n1tabd = din("n1tab", [2, 128, NT])
    eppd = din("epp", [128, 4])

    yout = dout("yout", [2, 128, 8, NT])
    ck_out = dout("ck_out", [2, 128, NT])
    cv_out = dout("cv_out", [2, NT, 128])
    sf_out = dout("sf_out", [2, 4, 4, 64, 128])
    sb_out = dout("sb_out", [2, 4, 4, 64, 128])

    tmask_d = nc.dram_tensor("tmask_d", [2, 4, 128, TW], BF16, kind="Internal").ap()
    decq_d = nc.dram_tensor("decq_d", [2, 128, 2, 2, NT], BF16, kind="Internal").ap()

    out_toks = []
    with contextlib.ExitStack() as st:
        S = Sched(nc, st, serialize=SERIALIZE)

        def sb(name, shape, dt):
            return st.enter_context(nc.sbuf_tensor(name, list(shape), dt))

        xT = sb("xT", [128, 8, NT], F32)
        hT = sb("hT", [128, 8, NT], BF16)
        obr = [sb("oatt", [128, 4, NT], BF16), sb("oret", [128, 4, NT], BF16), sb("ofou", [128, 4, NT], BF16)]
        wbuf = [sb("wb%d" % i, [128, SLAB], BF16) for i in range(NWB)]
        rope = sb("rope_t", [128, 2, NT], F32)
        cb16 = sb("cb16_t", [128, 768], BF16)
        pp = sb("pp_t", [128, 2 * PPL + 24], F32)
        modT = sb("modT", [128, 2, 2, 48], F32)
        abT = sb("abT", [128, 2, 8], F32)
        lgb_t = sb("lgb_t", [128, 24], F32)
        kdec = sb("kdec", [128, 2, 2, 8], F32)
        epp = sb("epp_t", [128, 4], F32)
        scond = sb("scond", [128, 8, 2], BF16)
        condt = sb("condt", [128, 8, 2], F32)
        tblf = sb("tblf", [128, 3, 480], F32)
        tble = sb("tble", [128, 4, 512], BF16)
        tbln = sb("tbln", [128, NT], F32)
        lgq = sb("lgq", [128, 8], F32)
        ARN = 37888
        arena_t = sb("arena", [128, ARN], BF16)
        AR = Arena(arena_t, ARN, S)
        ps = st.enter_context(nc.psum_tensor("ps", [128, 8, 512], F32))

        ones1024 = cb16[:, 0:128]
        ones128 = cb16[:, 128:256]
        bd64 = cb16[:, 256:384]
        pswap = cb16[:, 384:512]
        cs128 = cb16[:, 512:768]

        XT = TB("xT", 8)
        HT = TB("hT", 8)
        OBR = [TB("oatt", 4), TB("oret", 4), TB("ofou", 4)]
        WB = [Buf("wb%d" % i) for i in range(NWB)]
        PB = [Buf("ps%d" % i) for i in range(8)]
        CONST = Buf("const")
        ABB = Buf("ab")
        LG = Buf("lg")
        KDEC = Buf("kdec")
        wrot = Rot(range(NWB))

        def load_slab(pieces):
            i = wrot()
            for (dst, src) in pieces(wbuf[i]):
                S.dma("pool", dst, src, reads=[], writes=[WB[i]])
            return wbuf[i], WB[i]

        def load_w(l, idx, n):
            return load_slab(lambda wbt: [(wbt[:, 0:n], wpk[l, idx, :, 0:n])])

        def wcols(w_l, c0, nc_, kc=8):
            return w_l.rearrange("(kc p) n -> p kc n", p=128)[:, :, c0:c0 + nc_]

        def slabview(wb, off, kc, ncols):
            return wb[:, off:off + kc * ncols].rearrange("p (k n) -> p k n", k=kc)

        S.dma("sp", cb16[:], cb16d[:, :], writes=[CONST])
        S.dma("sp", pp[:], ppd[:, :], writes=[CONST])
        S.dma("sp", rope[:], roped[:, :, :], writes=[CONST])
        S.dma("sp", epp[:], eppd[:, :], writes=[CONST])
        S.dma("sp", condt[:], condT[:, :, :], writes=[CONST])

        SC = Buf("scond")
        MODL = [Buf("mod0"), Buf("mod1")]
        S.act(scond[:], condt[:], AF.Silu, [CONST], [SC])
        brot0 = Rot(range(8))

        mod_ext = [None]
        drot = Rot([6, 7])

        def mod_slab(l, sidx, brot):
            if mod_ext[0] is None:
                wb, WBb = load_slab(lambda wbt: [(wbt[:, 0:4096], wadapk[l, sidx])])
            else:
                bufs_, BUFS_, rot_ = mod_ext[0]
                i_ = rot_()
                wb, WBb = bufs_[i_], BUFS_[i_]
                S.dma("pool", wb[:, 0:4096], wadapk[l, sidx], writes=[WBb], arena=True)
                brot = drot
            wv = slabview(wb, 0, 8, 512)
            b = brot()
            for j in range(4):
                S.mm(ps[:, b, 2 * j:2 * j + 2],
                     [(wv[:, kc, j * 128:(j + 1) * 128], scond[:, kc, :]) for kc in range(8)],
                     [WBb, SC], [PB[b]])
            for cnd in range(2):
                S.tt(modT[:, l, cnd, sidx * 4:sidx * 4 + 4],
                     ps[:, b, 0:8].rearrange("p (j c) -> p j c", c=2)[:, :, cnd],
                     pp[:, l * PPL + 16 + sidx * 4: l * PPL + 16 + sidx * 4 + 4], ALU.add,
                     [PB[b], CONST], [MODL[l]])
        RDO = 2 * PPL
        S.act(lgb_t[:], pp[:, RDO:RDO + 24], AF.Exp, [CONST], [LG], scale=-1.0)
        S.act(lgb_t[:], lgb_t[:], AF.Ln, [LG], [LG], bias=1.0, scale=1.0)
        S.ts(lgb_t[:], lgb_t[:], -1.0, None, ALU.mult, None, [LG], [LG])

        def lg_b(l, dr, h):
            return lgb_t[:, l * 8 + dr * 4 + h: l * 8 + dr * 4 + h + 1]

        def lg_p(l, dr, j):
            c = 16 + l * 4 + dr * 2 + j
            return lgb_t[:, c:c + 1]

        import collections
        DCH = Buf("dch")
        DCN = Buf("dcn")
        EB = [Buf("te0"), Buf("te1"), Buf("te2"), Buf("te3")]
        TMDQ = [[[Buf("tmd%d%d%d" % (l, h, q)) for q in range(4)] for h in range(4)] for l in range(2)]
        DQDH = [[Buf("dqd%d_%d" % (l, k_)) for k_ in range(8)] for l in range(2)]
        ering = Rot(range(3))

        def tmask_q(l, h, q, load):
            if load:
                S.dma("sp", tblf[:, :, :], dtabd[:, :, q * 480:(q + 1) * 480].rearrange("a p n -> p a n"),
                      writes=[DCH])
            i = ering()
            e1, e2 = tble[:, i, 0:480], tble[:, 3, 0:480]
            S.act(e1, tblf[:, 0, 0:480], AF.Exp, [DCH, LG], [EB[i]], scale=lg_b(l, 0, h))
            S.act(e2, tblf[:, 1, 0:480], AF.Exp, [DCH, LG], [EB[3]], scale=lg_b(l, 1, h))
            S.tt(e1, e1, e2, ALU.mult, [EB[3]], [EB[i]])
            S.tt(e1, e1, tblf[:, 2, 0:480], ALU.add, [DCH], [EB[i]])
            S.dma("sp", tmask_d[l, h][:, q * 480:(q + 1) * 480], e1, reads=[EB[i]], writes=[TMDQ[l][h][q]])

        S.dma("sp", tbln[:, :], n1tabd[0], writes=[DCN])
        for l_ in range(2):
            for j_ in range(2):
                k_ = l_ * 2 + j_
                S.ts(lgq[:, k_:k_ + 1], lg_p(l_, 1, j_), -1.0, None, ALU.mult, None, [LG], [LG])
                S.ts(lgq[:, 4 + k_:5 + k_], lg_p(l_, 1, j_), 1025.0, None, ALU.mult, None, [LG], [LG])

        def decq_h(l, j, dr, hf):
            i = ering()
            src = tbln[:, hf * 512:(hf + 1) * 512]
            if dr == 0:
                S.act(tble[:, i, :], src, AF.Exp, [DCN, LG], [EB[i]], scale=lg_p(l, 0, j))
            else:
                k_ = l * 2 + j
                S.act(tble[:, i, :], src, AF.Exp, [DCN, LG], [EB[i]], scale=lgq[:, k_:k_ + 1],
                      bias=lgq[:, 4 + k_:5 + k_])
            S.dma("sp", decq_d[l, :, j, dr, hf * 512:(hf + 1) * 512], tble[:, i, :], reads=[EB[i]],
                  writes=[DQDH[l][j * 4 + dr * 2 + hf]])

        def tmask_steps(l):
            return [(lambda l=l, h=h, q=q: tmask_q(l, h, q, h == 0)) for q in range(4) for h in range(4)]

        def decq_steps(l):
            return [(lambda l=l, j=j, dr=dr, hf=hf: decq_h(l, j, dr, hf))
                    for j in range(2) for dr in range(2) for hf in range(2)]
        for l in range(2):
            for i in range(2):
                for dr in range(2):
                    S.act(kdec[:, l, i, dr * 4:dr * 4 + 4], lgb_t[:, l * 8 + dr * 4:l * 8 + dr * 4 + 4], AF.Exp,
                          [LG, CONST], [KDEC], scale=epp[:, i * 2 + dr:i * 2 + dr + 1])
        for sidx in range(4):
            mod_slab(0, sidx, brot0)
        Q = collections.defaultdict(collections.deque)
        cur_brot = [brot0]
        cur_pass = [0]
        for sidx in range(4, 12):
            Q[(0, "att")].append(lambda sidx=sidx: mod_slab(0, sidx, cur_brot[0]))
        t0s = tmask_steps(0)
        Q[(0, "four")].extend(t0s[:8])
        Q[(0, "ret")].extend(t0s[8:])
        ml1 = [(lambda sidx=sidx: mod_slab(1, sidx, cur_brot[0])) for sidx in range(6)]
        t1s = tmask_steps(1)
        while ml1 or t1s:
            for lst in (ml1, t1s, t1s):
                if lst:
                    Q[(0, "merge")].append(lst.pop(0))
        for sidx in range(6, 12):
            Q[(1, "att")].append(lambda sidx=sidx: mod_slab(1, sidx, cur_brot[0]))
        Q[(1, "merge")].extend(decq_steps(0) + decq_steps(1))

        def drain(name, n=1):
            q_ = Q[(cur_pass[0], name)]
            while q_ and n != 0:
                q_.popleft()()
                n -= 1

        def norm_mod(A, Bv, brot_, MOD):
            AR.reset()
            sq = [AR.alloc([8, 512], BF16) for _ in range(2)]
            tmp = [AR.alloc([512], F32) for _ in range(4)]
            SQ = [Buf("sq0"), Buf("sq1")]
            TMP = [Buf("tmp%d" % i) for i in range(4)]
            bks = []
            for tg in range(2):
                t0, t1 = tg * 512, tg * 512 + 512
                S.act(sq[tg], xT[:, :, t0:t1], AF.Square, XT.s(range(8), t0, t1), [SQ[tg]])
            for tg in range(2):
                b = brot_()
                bks.append(b)
                S.mm(ps[:, b, :], [(ones1024, sq[tg][:, c, :]) for c in range(8)], [SQ[tg], CONST], [PB[b]])
                S.act(ps[:, b, :], ps[:, b, :], AF.Ln, [], [PB[b]], bias=EPS, scale=1.0)
                S.act(ps[:, b, :], ps[:, b, :], AF.Exp, [], [PB[b]], scale=-0.5)
            k = 0
            for tg in range(2):
                t0, t1 = tg * 512, tg * 512 + 512
                b = bks[tg]
                for c in range(8):
                    i = k % 4
                    k += 1
                    S.stt(tmp[i], xT[:, c, t0:t1], A[:, c:c + 1], ps[:, b, :], ALU.mult, ALU.mult,
                          XT.s(c, t0, t1) + [PB[b], ABB, MOD], [TMP[i]])
                    S.act(hT[:, c, t0:t1], tmp[i], AF.Identity, [TMP[i], MOD], HT.s(c, t0, t1),
                          bias=Bv[:, c:c + 1], scale=1.0)

        def rstd_from(psb, PBb, out, OUTB):
            S.act(out, psb, AF.Ln, [PBb], [OUTB], bias=EPS, scale=1.0)
            S.act(out, out, AF.Exp, [OUTB], [OUTB], scale=-0.5)

        def layer_pass(half, l):
            smp = (half == 1)
            nseq, L = (1, 1024) if smp else (4, 256)
            ntl = L // 128
            NQ = 512 if smp else 256
            P0 = l * PPL
            md = modT[:, l, half, :]
            MOD = MODL[l]
            defer_mod = (half == 0 and l == 0)
            cur_pass[0] = half * 2 + l
            brot_ = Rot(range(8))
            cur_brot[0] = brot_

            S.stt(abT[:, 0, :], md[:, 8:16], 1.0, pp[:, P0:P0 + 8], ALU.add, ALU.mult, [MOD, CONST], [ABB])

            norm_mod(abT[:, 0, :], md[:, 0:8], brot_, MOD)

            def ph_att():
                AR.reset()
                qT = AR.alloc([4, NT], BF16)
                nkeys = 1536 if smp else 1024
                KT = AR.alloc([2, 2, nkeys], BF16)
                nvt = 12 if smp else 8
                VA = AR.alloc([nvt, 2, 128], BF16)
                etb = [AR.alloc([512], BF16) for _ in range(4)]
                sqb = [AR.alloc([512], BF16) for _ in range(4)]
                rsb = [AR.alloc([512], F32) for _ in range(4)]
                if smp:
                    qnb = [AR.alloc([512], BF16) for _ in range(4)]
                    t1b = [AR.alloc([512], F32) for _ in range(3)]
                recb = [AR.alloc([512], F32) for _ in range(2)]
                if not smp:
                    kst = AR.alloc([2, NT], F32)
                    vst = AR.alloc([8, 128], F32)
                    KST, VST = Buf("kst"), Buf("vst")
                if Q[(cur_pass[0], "att")]:
                    mod_ext[0] = ([AR.alloc([4096], BF16) for _ in range(2)], [Buf("mx%d" % k_) for k_ in range(2)], Rot(range(2)))
                QT = TB("qT", 4)
                KTB = TB("KT", 2, 6)
                VAB = [Buf("va%d" % i) for i in range(nvt)]
                ETB = [Buf("et%d" % i) for i in range(4)]
                SQB = [Buf("sqb%d" % i) for i in range(4)]
                RSB = [Buf("rsb%d" % i) for i in range(4)]
                QNB = [Buf("qnb%d" % i) for i in range(4)]
                T1B = [Buf("t1b%d" % i) for i in range(4)]
                T2B = [Buf("t2b%d" % i) for i in range(4)]
                RECB = [Buf("rec%d" % i) for i in range(2)]
                rr = Rot(range(2))

                S.memset(VA[:, :, :, 64:128], 1.0, VAB)
                S.memset(KT[:, :, :, 0:1024], 0.0, KTB.s([0, 1], 0, 1024))

                wbq, WBq = load_w(l, 0, 4096)
                wq = slabview(wbq, 0, 8, 512)

                wbk, WBk = load_w(l, 1, 3072)
                wk = slabview(wbk, 0, 8, 384)
                if smp:
                    for kv in range(2):
                        for z in range(2):
                            S.dma("pool", KT[:, kv, z, 1024:1536], ckT[l, kv, z], writes=KTB.s(kv, 1024, 1536), arena=True)
                    for kv in range(2):
                        S.dma("pool", VA[:, 8:12, kv, 0:64],
                              cvd[l].rearrange("(tt p) f -> p tt f", p=128)[:, :, kv * 64:(kv + 1) * 64],
                              writes=VAB[8:12], arena=True)

                units = []
                for tg in range(2):
                    t0 = tg * 512
                    for j in range(4):
                        units.append(dict(w=wq, WBw=WBq, c0=j * 128, g=P0 + 64, out=qT[:, j, t0:t0 + 512],
                                          OUTB=QT.s(j, t0, t0 + 512), t0=t0, kst=None))
                    for kv in range(2):
                        units.append(dict(w=wk, WBw=WBk, c0=kv * 128, g=P0 + 65, out=None, kv=kv,
                                          OUTB=KTB.s(kv, t0, t0 + 512), t0=t0,
                                          kst=(None if smp else kst[:, kv, t0:t0 + 512])))
                for ui, u in enumerate(units):
                    u["i"] = ui % 4
                    u["i3"] = ui % 3
                    u["split"] = ui in (0, 1, 6, 7)

                rotA, rotB, rotC = Rot([0, 1, 2, 3]), Rot([4, 5]), Rot([6, 7])

                def stA(u):
                    t0, c0, w = u["t0"], u["c0"], u["w"]
                    b = rotA()
                    u["b"] = b
                    if u["split"]:
                        S.mm_split(ps[:, b, :], [(w[:, kc, c0:c0 + 128], hT[:, kc, t0:t0 + 512]) for kc in range(8)],
                                   [HT.s(kc, t0, t0 + 512) for kc in range(8)], [u["WBw"]], [PB[b]])
                    else:
                        S.mm(ps[:, b, :], [(w[:, kc, c0:c0 + 128], hT[:, kc, t0:t0 + 512]) for kc in range(8)],
                             [u["WBw"]] + HT.s(range(8), t0, t0 + 512), [PB[b]])
                    S.act(sqb[u["i"]], ps[:, b, :], AF.Square, [PB[b]], [SQB[u["i"]]])

                def stB(u):
                    i, b, g = u["i"], u["b"], u["g"]
                    b2 = rotB()
                    S.mm(ps[:, b2, :], [(bd64, sqb[i])], [SQB[i], CONST], [PB[b2]])
                    rstd_from(ps[:, b2, :], PB[b2], rsb[i], RSB[i])
                    if smp:
                        S.stt(ps[:, b, :], ps[:, b, :], pp[:, g:g + 1], rsb[i], ALU.mult, ALU.mult,
                              [RSB[i], CONST], [PB[b]])
                        S.copy(qnb[i], ps[:, b, :], [PB[b]], [QNB[i]], eng="act")
                    elif u["kst"] is not None:
                        S.stt(u["kst"], ps[:, b, :], pp[:, g:g + 1], rsb[i], ALU.mult, ALU.mult,
                              [PB[b], RSB[i], CONST], [KST])
                        for z in range(2):
                            S.copy(KT[z * 64:(z + 1) * 64, u["kv"], z, u["t0"]:u["t0"] + 512], u["kst"][z * 64:(z + 1) * 64, :],
                                   [KST], u["OUTB"], eng=("act" if z == 0 else "dve"))
                    else:
                        S.stt(u["out"], ps[:, b, :], pp[:, g:g + 1], rsb[i], ALU.mult, ALU.mult,
                              [PB[b], RSB[i], CONST], u["OUTB"])

                def stC(u):
                    if not smp:
                        return
                    i, t0 = u["i"], u["t0"]
                    i3 = u["i3"]
                    b = u["b"]
                    S.tt(t1b[i3], ps[:, b, :], rope[:, 0, t0:t0 + 512], ALU.mult, [PB[b], CONST], [T1B[i3]])
                    b3 = rotC()
                    S.mm(ps[:, b3, :], [(pswap, qnb[i])], [QNB[i], CONST], [PB[b3]])
                    S.tt(ps[:, b3, :], ps[:, b3, :], rope[:, 1, t0:t0 + 512], ALU.mult, [CONST], [PB[b3]])
                    if u["out"] is not None:
                        S.tt(u["out"], t1b[i3], ps[:, b3, :], ALU.add, [T1B[i3], PB[b3]], u["OUTB"])
                    else:
                        for z in range(2):
                            S.tt(KT[z * 64:(z + 1) * 64, u["kv"], z, t0:t0 + 512], t1b[i3][z * 64:(z + 1) * 64, :],
                                 ps[z * 64:(z + 1) * 64, b3, :], ALU.add, [T1B[i3], PB[b3]], u["OUTB"])
                stages = [stA, stB, stC]
                for step in range(len(units) + len(stages) - 1):
                    for si, stg in enumerate(stages):
                        ui = step - si
                        if 0 <= ui < len(units):
                            stg(units[ui])
                    if step >= 4:
                        drain("att", 1)
                if not smp:
                    for kv in range(2):
                        out_toks.append(S.dma("sp", ck_out[l, kv * 64:(kv + 1) * 64, :], kst[0:64, kv, :], reads=[KST]))
                for tt in range(8):
                    b = brot_()
                    S.mm(ps[:, b, 0:128], [(hT[:, kc, tt * 128:(tt + 1) * 128], wk[:, kc, 256:384]) for kc in range(8)],
                         [WBk] + HT.s(range(8), tt * 128, tt * 128 + 128), [PB[b]])
                    S.copy(VA[:, tt, :, 0:64], ps[:, b, 0:128].rearrange("p (k d) -> p k d", k=2), [PB[b]], [VAB[tt]])
                    if not smp:
                        S.copy(vst[:, tt, :], ps[:, b, 0:128], [PB[b]], [VST], eng="act")
                if not smp:
                    out_toks.append(S.dma("sp", cv_out[l].rearrange("(tt p) f -> p tt f", p=128), vst, reads=[VST]))

                orot = Rot([6, 7])
                srot = Rot([0, 1, 2, 3, 4, 5])
                erot = Rot(range(4))
                scale = 64 ** -0.5
                if smp:
                    sbanks = [Rot([0, 1]), Rot([2, 3])]
                    obanks = Rot([4, 6])
                    e4 = Rot(range(4))
                    for h in range(8):
                        kv, z, ch = h // 4, h % 2, h // 2
                        base = z * 64
                        ob0 = obanks()
                        obs = [ob0, ob0 + 1]

                        def s_exp(st, kt, kv=kv, z=z, ch=ch):
                            sbk = sbanks[st]()
                            q0 = st * 512
                            S.mm(ps[:, sbk, :], [(KT[:, kv, z, kt * 128:(kt + 1) * 128], qT[:, ch, q0:q0 + 512])],
                                 KTB.s(kv, kt * 128, kt * 128 + 128) + QT.s(ch, q0, q0 + 512), [PB[sbk]])
                            ei = e4()
                            S.act(etb[ei], ps[:, sbk, :], AF.Exp, [PB[sbk]], [ETB[ei]], scale=scale)
                            return ei

                        def pv(st, kt, ei, kv=kv, obs=obs):
                            ob = obs[st]
                            S.op("pe", lambda e: e.matmul(ps[:, ob, :], lhsT=VA[:, kt, kv, :], rhs=etb[ei],
                                                          start=(kt == 0), stop=(kt == 11)),
                                 [VAB[kt], ETB[ei]], [PB[ob]])
                        cur = [s_exp(0, 0), s_exp(1, 0)]
                        for kt in range(12):
                            for st in range(2):
                                nxt = s_exp(st, kt + 1) if kt + 1 < 12 else None
                                pv(st, kt, cur[st])
                                cur[st] = nxt
                        for st in range(2):
                            ob, q0 = obs[st], st * 512
                            ri = rr()
                            S.op("dve", lambda e, ri=ri, ob=ob: e.reciprocal(out=recb[ri][64:128, :], in_=ps[64:128, ob, :]),
                                 [PB[ob]], [RECB[ri]])
                            S.tt(obr[0][base:base + 64, ch, q0:q0 + 512], ps[0:64, ob, :], recb[ri][64:128, :],
                                 ALU.mult, [PB[ob], RECB[ri]], OBR[0].s(ch, q0, q0 + 512))
                else:
                    aunits = [(sq_, p_) for sq_ in range(4) for p_ in range(4)]

                    def atA(u):
                        sq_, p_ = u
                        kv, ch, q0 = p_ // 2, p_, sq_ * 256
                        eis = []
                        for kt in range(2):
                            k0 = sq_ * 256 + kt * 128
                            sbk = srot()

                            def smm(e, sbk=sbk, kv=kv, ch=ch, q0=q0, k0=k0):
                                e.matmul(ps[:, sbk, 0:256], lhsT=KT[:, kv, 0, k0:k0 + 128], rhs=qT[:, ch, q0:q0 + 256],
                                         start=True, stop=True)
                                return e.matmul(ps[:, sbk, 256:512], lhsT=KT[:, kv, 1, k0:k0 + 128],
                                                rhs=qT[:, ch, q0:q0 + 256], start=True, stop=True)
                            S.op("pe", smm, KTB.s(kv, k0, k0 + 128) + QT.s(ch, q0, q0 + 256), [PB[sbk]])
                            ei = erot()
                            S.act(etb[ei], ps[:, sbk, :], AF.Exp, [PB[sbk]], [ETB[ei]], scale=scale)
                            eis.append(ei)
                        return eis

                    def atB(u, eis):
                        sq_, p_ = u
                        kv, ch, q0 = p_ // 2, p_, sq_ * 256
                        ob = orot()

                        def pvm(e, ob=ob, kv=kv, sq_=sq_, eis=eis):
                            ins = None
                            for z in range(2):
                                for kt in range(2):
                                    ins = e.matmul(ps[:, ob, z * 256:(z + 1) * 256], lhsT=VA[:, 2 * sq_ + kt, kv, :],
                                                   rhs=etb[eis[kt]][:, z * 256:(z + 1) * 256], start=(kt == 0), stop=(kt == 1))
                            return ins
                        S.op("pe", pvm, [VAB[2 * sq_], VAB[2 * sq_ + 1], ETB[eis[0]], ETB[eis[1]]], [PB[ob]])
                        ri = rr()
                        S.act(recb[ri][64:128, :], ps[64:128, ob, :], AF.Ln, [PB[ob]], [RECB[ri]])
                        S.act(recb[ri][64:128, :], recb[ri][64:128, :], AF.Exp, [RECB[ri]], [RECB[ri]], scale=-1.0)
                        for z in range(2):
                            S.tt(obr[0][z * 64:(z + 1) * 64, ch, q0:q0 + 256], ps[0:64, ob, z * 256:(z + 1) * 256],
                                 recb[ri][64:128, z * 256:(z + 1) * 256], ALU.mult, [PB[ob], RECB[ri]],
                                 OBR[0].s(ch, q0, q0 + 256))
                    prev = None
                    for u in aunits:
                        drain("att", 1)
                        eis = atA(u)
                        if prev is not None:
                            atB(*prev)
                        prev = (u, eis)
                    atB(*prev)
                drain("att", -1)

            def ph_ret():
                AR.reset()
                qr = AR.alloc([2, NT], BF16)
                kr = AR.alloc([2, 2, NT], BF16)
                vr = AR.alloc([8, 512], BF16)
                sg = AR.alloc([4, NT], BF16)
                atb = [AR.alloc([512], BF16) for _ in range(4)]
                tmk = [AR.alloc([TW], BF16) for _ in range(2)]
                sqb = [AR.alloc([512], BF16) for _ in range(2)]
                rsb = [AR.alloc([512], F32) for _ in range(2)]
                t1b = [AR.alloc([512], F32) for _ in range(2)]
                QR, KR = TB("qr", 2), TB("kr", 2)
                VRB = [Buf("vr%d" % i) for i in range(8)]
                SG = TB("sg", 4)
                ATB = [Buf("at%d" % i) for i in range(4)]
                TMK = [Buf("tmk%d" % i) for i in range(2)]
                wb1, WB1 = load_w(l, 2, 4096)
                wb2, WB2 = load_w(l, 3, 4096)
                wb3, WB3 = load_w(l, 4, 4096)
                if smp:
                    qd = AR.alloc([4, NT], BF16)
                    dqt = AR.alloc([2, 2, NT], BF16)
                    s0t = AR.alloc([4, 128], BF16)
                    QD, DQT, S0B = TB("qd", 4), Buf("dqt"), Buf("s0")
                    S.dma("sp", dqt, decq_d[l], reads=DQDH[l], writes=[DQT])
                    S.dma("pool", s0t, s0d[l], writes=[S0B], arena=True)
                else:
                    ktm = AR.alloc([8, 4, 128], BF16)
                    sst = [AR.alloc([512], F32) for _ in range(2)]
                    KTM = [Buf("ktm%d" % i) for i in range(8)]
                    SST = [Buf("sst%d" % i) for i in range(2)]
                S.memset(kr, 0.0, KR.s([0, 1]))

                w1 = slabview(wb1, 0, 8, 512)
                for tg in range(2):
                    for j in range(2):
                        t0 = tg * 512
                        b = brot_()
                        S.mm(ps[:, b, :], [(w1[:, kc, j * 128:(j + 1) * 128], hT[:, kc, t0:t0 + 512]) for kc in range(8)],
                             [WB1] + HT.s(range(8), t0, t0 + 512), [PB[b]])
                        S.copy(qr[:, j, t0:t0 + 512], ps[:, b, :], [PB[b]], QR.s(j, t0, t0 + 512), eng="act")
                        b = brot_()
                        S.mm(ps[:, b, :], [(w1[:, kc, 256 + j * 128:256 + (j + 1) * 128], hT[:, kc, t0:t0 + 512])
                                           for kc in range(8)],
                             [WB1] + HT.s(range(8), t0, t0 + 512), [PB[b]])
                        for z in range(2):
                            S.ts(kr[z * 64:(z + 1) * 64, j, z, t0:t0 + 512], ps[z * 64:(z + 1) * 64, b, :], 0.125, None,
                                 ALU.mult, None, [PB[b]], KR.s(j, t0, t0 + 512))
                if smp:
                    for h in range(4):
                        j, bs = h // 2, (h % 2) * 64
                        for dr in range(2):
                            S.tt(qd[dr * 64:(dr + 1) * 64, h, :], qr[bs:bs + 64, j, :], dqt[bs:bs + 64, j, dr, :], ALU.mult,
                                 QR.s(j) + [DQT], QD.s(h))
                if not smp:
                    for tt in range(8):
                        b = brot_()
                        S.mm(ps[:, b, 0:256], [(hT[:, kc, tt * 128:(tt + 1) * 128], w1[:, kc, 256:512]) for kc in range(8)],
                             [WB1] + HT.s(range(8), tt * 128, tt * 128 + 128), [PB[b]])
                        for dr in range(2):
                            for h in range(4):
                                S.ts(ktm[:, tt, h, dr * 64:(dr + 1) * 64], ps[:, b, h * 64:(h + 1) * 64],
                                     kdec[:, l, tt % 2, dr * 4 + h:dr * 4 + h + 1], 0.125, ALU.mult, ALU.mult,
                                     [PB[b], KDEC], [KTM[tt]])
                w2 = slabview(wb2, 0, 8, 512)
                for tt in range(8):
                    b = brot_()
                    S.mm(ps[:, b, :], [(hT[:, kc, tt * 128:(tt + 1) * 128], w2[:, kc, :]) for kc in range(8)],
                         [WB2] + HT.s(range(8), tt * 128, tt * 128 + 128), [PB[b]])
                    S.copy(vr[:, tt, :], ps[:, b, :], [PB[b]], [VRB[tt]], eng=("act" if tt % 2 else "dve"))
                    drain("ret", 1)
                w3 = slabview(wb3, 0, 8, 512)
                for h in range(4):
                    for tg in range(2):
                        t0 = tg * 512
                        b = brot_()
                        S.mm(ps[:, b, :], [(w3[:, kc, h * 128:(h + 1) * 128], hT[:, kc, t0:t0 + 512]) for kc in range(8)],
                             [WB3] + HT.s(range(8), t0, t0 + 512), [PB[b]])
                        S.act(sg[:, h, t0:t0 + 512], ps[:, b, :], AF.Silu, [PB[b]], SG.s(h, t0, t0 + 512))
                        drain("ret", 1)

                drain("ret", -1)
                if smp:
                    obanks = Rot([4, 6])
                    rsb_ = [Rot([0, 1]), Rot([2, 3])]
                    mrot = Rot([0, 2, 1, 3])
                else:
                    orot = Rot([6, 7])
                    srot = Rot([0, 1, 2, 3])
                    mrot = Rot([4, 5])
                arot = Rot(range(4))
                trot = Rot(range(2))
                pend_ret = [None]
                rr = Rot(range(2))

                def post_norm(ob, h, n0, W, b2=None):
                    i = rr()
                    S.act(sqb[i][:, 0:W], ps[:, ob, 0:W], AF.Square, [PB[ob]], [SQB[i]])
                    if b2 is None:
                        b2 = mrot()
                    S.mm(ps[:, b2, 0:W], [(ones128, sqb[i][:, 0:W])], [SQB[i], CONST], [PB[b2]])
                    rstd_from(ps[:, b2, 0:W], PB[b2], rsb[i][:, 0:W], RSB[i])
                    S.stt(t1b[i][:, 0:W], ps[:, ob, 0:W], pp[:, P0 + 66:P0 + 67], rsb[i][:, 0:W],
                          ALU.mult, ALU.mult, [PB[ob], RSB[i], CONST], [T1B[i]])
                    S.tt(obr[1][:, h, n0:n0 + W], t1b[i][:, 0:W], sg[:, h, n0:n0 + W], ALU.mult,
                         [T1B[i]] + SG.s(h, n0, n0 + W), OBR[1].s(h, n0, n0 + W))

                for h in range(4):
                    z, j = h % 2, h // 2
                    ti = trot()
                    S.dma("sp", tmk[ti], tmask_d[l, h], reads=TMDQ[l][h], writes=[TMK[ti]])
                    if smp:
                        ob0 = obanks()
                        obs = [ob0, ob0 + 1]
                        for st in range(2):
                            S.op("pe", lambda e, ob=obs[st], h=h, n0=st * 512: e.matmul(
                                ps[:, ob, :], lhsT=s0t[:, h, :], rhs=qd[:, h, n0:n0 + 512], start=True, stop=False),
                                 [S0B] + QD.s(h, st * 512, st * 512 + 512), [PB[obs[st]]])

                        def sc_mask(st, mt, j=j, z=z, ti=ti):
                            n0, m0 = st * 512, mt * 128
                            sbk = rsb_[st]()
                            S.mm(ps[:, sbk, :], [(kr[:, j, z, m0:m0 + 128], qr[:, j, n0:n0 + 512])],
                                 KR.s(j, m0, m0 + 128) + QR.s(j, n0, n0 + 512), [PB[sbk]])
                            ai = arot()
                            off = TOFF + n0 - m0
                            S.tt(atb[ai], ps[:, sbk, :], tmk[ti][:, off:off + 512], ALU.mult,
                                 [PB[sbk], TMK[ti]], [ATB[ai]])
                            return ai
                        cur = [sc_mask(0, 0), sc_mask(1, 0)]
                        if pend_ret[0] is not None:
                            pend_ret[0]()
                            pend_ret[0] = None
                        for mt in range(8):
                            for st in range(2):
                                ai = cur[st]
                                S.op("pe", lambda e, ob=obs[st], mt=mt, ai=ai, h=h: e.matmul(
                                    ps[:, ob, :], lhsT=vr[:, mt, h * 128:(h + 1) * 128], rhs=atb[ai],
                                    start=False, stop=(mt == 7)), [VRB[mt], ATB[ai]], [PB[obs[st]]])
                                cur[st] = sc_mask(st, mt + 1) if mt + 1 < 8 else None
                        pend_ret[0] = (lambda obs=obs, h=h: [post_norm(obs[st_], h, st_ * 512, 512, b2=rsb_[st_]())
                                                            for st_ in range(2)])
                    else:
                        def rtA(sp, j=j, z=z, ti=ti):
                            ais = []
                            for mt in range(2):
                                sbk = srot()

                                def smm(e, sbk=sbk, mt=mt, sp=sp, j=j, z=z):
                                    ins = None
                                    for a_ in range(2):
                                        q0 = sp * 512 + a_ * 256
                                        ins = e.matmul(ps[:, sbk, a_ * 256:(a_ + 1) * 256],
                                                       lhsT=kr[:, j, z, q0 + mt * 128:q0 + mt * 128 + 128],
                                                       rhs=qr[:, j, q0:q0 + 256], start=True, stop=True)
                                    return ins
                                S.op("pe", smm, KR.s(j, sp * 512, sp * 512 + 512) + QR.s(j, sp * 512, sp * 512 + 512), [PB[sbk]])
                                ai = arot()
                                off = TOFF - mt * 128
                                S.tt(atb[ai].rearrange("p (a n) -> p a n", a=2), ps[:, sbk, :].rearrange("p (a n) -> p a n", a=2),
                                     tmk[ti][:, off:off + 256].unsqueeze(1).to_broadcast([128, 2, 256]), ALU.mult,
                                     [PB[sbk], TMK[ti]], [ATB[ai]])
                                ais.append(ai)
                            return ais

                        def rtB(sp, ais, h=h):
                            ob = orot()

                            def pvm(e, ob=ob, sp=sp, ais=ais, h=h):
                                ins = None
                                for a_ in range(2):
                                    for mt in range(2):
                                        ins = e.matmul(ps[:, ob, a_ * 256:(a_ + 1) * 256],
                                                       lhsT=vr[:, sp * 4 + a_ * 2 + mt, h * 128:(h + 1) * 128],
                                                       rhs=atb[ais[mt]][:, a_ * 256:(a_ + 1) * 256],
                                                       start=(mt == 0), stop=(mt == 1))
                                return ins
                            S.op("pe", pvm, [VRB[sp * 4 + k_] for k_ in range(4)] + [ATB[ais[0]], ATB[ais[1]]], [PB[ob]])
                            post_norm(ob, h, sp * 512, 512)
                        for sp_ in range(2):
                            ais_ = rtA(sp_)
                            if pend_ret[0] is not None:
                                pend_ret[0]()
                            pend_ret[0] = (lambda sp_=sp_, ais_=ais_, rtB=rtB: rtB(sp_, ais_))
                if pend_ret[0] is not None:
                    pend_ret[0]()
                    pend_ret[0] = None
                if not smp:
                    for s in range(4):
                        b = mrot()
                        for h in range(4):
                            S.mm(ps[:, b, h * 128:(h + 1) * 128],
                                 [(ktm[:, 2 * s + i2, h, :], vr[:, 2 * s + i2, h * 128:(h + 1) * 128]) for i2 in range(2)],
                                 [KTM[2 * s], KTM[2 * s + 1], VRB[2 * s], VRB[2 * s + 1]], [PB[b]])
                        si = s % 2
                        S.copy(sst[si], ps[:, b, :], [PB[b]], [SST[si]])
                        out_toks.append(S.dma("sp", sf_out[l, s].rearrange("h k v -> k h v"),
                                              sst[si][0:64, :].rearrange("p (h v) -> p h v", h=4), reads=[SST[si]]))
                        out_toks.append(S.dma("sp", sb_out[l, s].rearrange("h k v -> k h v"),
                                              sst[si][64:128, :].rearrange("p (h v) -> p h v", h=4), reads=[SST[si]]))

            def ph_four():
                AR.reset()
                uT = AR.alloc([4, NT], BF16)
                Y = AR.alloc([8, 4, 256], BF16)
                UT = TB("uT", 4)
                YB = [Buf("y%d" % i) for i in range(8)]
                wb4, WB4 = load_w(l, 5, 4096)
                w4 = slabview(wb4, 0, 8, 512)
                for g in range(4):
                    for tg in range(2):
                        t0 = tg * 512
                        b = brot_()
                        S.mm(ps[:, b, :], [(w4[:, kc, g * 128:(g + 1) * 128], hT[:, kc, t0:t0 + 512]) for kc in range(8)],
                             [WB4] + HT.s(range(8), t0, t0 + 512), [PB[b]])
                        S.copy(uT[:, g, t0:t0 + 512], ps[:, b, :], [PB[b]], UT.s(g, t0, t0 + 512),
                               eng=("act" if tg else "dve"))
                        drain("four", 1)
                for tt in range(8):
                    for gp in range(2):
                        b = brot_()
                        for g in (2 * gp, 2 * gp + 1):
                            S.mm(ps[:, b, (g % 2) * 256:(g % 2) * 256 + 256],
                                 [(uT[:, g, tt * 128:(tt + 1) * 128], cs128)], UT.s(g, tt * 128, tt * 128 + 128) + [CONST],
                                 [PB[b]])
                        S.copy(Y[:, tt, 2 * gp:2 * gp + 2, :], ps[:, b, :].rearrange("p (g n) -> p g n", g=2),
                               [PB[b]], [YB[tt]], eng=("act" if gp else "dve"))
                        drain("four", 1)
                if smp:
                    for hf in range(2):
                        ct, CTB = load_slab(lambda wbt, hf=hf: [(wbt[:, 0:4096], dft1024d[2 * hf].rearrange("p a b -> p (a b)"))])
                        stt_, STB = load_slab(lambda wbt, hf=hf: [(wbt[:, 0:4096], dft1024d[2 * hf + 1].rearrange("p a b -> p (a b)"))])
                        cv_, sv_ = slabview(ct, 0, 8, 512), slabview(stt_, 0, 8, 512)
                        for g in range(4):
                            b = brot_()
                            pairs = []
                            for nt in range(8):
                                pairs.append((Y[:, nt, g, 0:128], cv_[:, nt, :]))
                                pairs.append((Y[:, nt, g, 128:256], sv_[:, nt, :]))
                            S.mm(ps[:, b, :], pairs, YB + [CTB, STB], [PB[b]])
                            S.copy(obr[2][:, g, hf * 512:(hf + 1) * 512], ps[:, b, :], [PB[b]],
                                   OBR[2].s(g, hf * 512, hf * 512 + 512), eng=("act" if g % 2 else "dve"))
                else:
                    dt_, DTB = load_slab(lambda wbt: [(wbt[:, 0:1024], dft256d.rearrange("p a b c -> p (a b c)"))])
                    dv = dt_[:, 0:1024].rearrange("p (a b c) -> p a b c", a=2, b=2)
                    for s in range(4):
                        for g in range(4):
                            b = brot_()
                            pairs = []
                            for nt in range(2):
                                pairs.append((Y[:, 2 * s + nt, g, 0:128], dv[:, 0, nt, :]))
                                pairs.append((Y[:, 2 * s + nt, g, 128:256], dv[:, 1, nt, :]))
                            S.mm(ps[:, b, 0:256], pairs, [YB[2 * s], YB[2 * s + 1], DTB], [PB[b]])
                            S.copy(obr[2][:, g, s * 256:(s + 1) * 256], ps[:, b, 0:256], [PB[b]],
                                   OBR[2].s(g, s * 256, s * 256 + 256), eng=("act" if g % 2 else "dve"))
                            drain("four", 1)
                drain("four", -1)

            SQB = [Buf("sqb%d" % i) for i in range(4)]
            RSB = [Buf("rsb%d" % i) for i in range(4)]
            T1B = [Buf("t1b%d" % i) for i in range(4)]
            if defer_mod:
                ph_att()
                ph_four()
                ph_ret()
            else:
                ph_att()
                ph_ret()
                ph_four()

            AR.reset()
            mg = AR.alloc([8, NT], BF16)
            sgt = [AR.alloc([512], F32) for _ in range(3)]
            mac = [AR.alloc([512], F32) for _ in range(2)]
            mtp = [AR.alloc([512], F32) for _ in range(2)]
            if cur_pass[0] == 0:
                mod_ext[0] = ([AR.alloc([4096], BF16) for _ in range(4)], [Buf("my%d" % k_) for k_ in range(4)], Rot(range(4)))
            MG = TB("mg", 8)
            SGT = [Buf("sgt%d" % i) for i in range(3)]
            MAC = [Buf("mac%d" % i) for i in range(2)]
            MTP = [Buf("mtp%d" % i) for i in range(2)]
            grot = Rot(range(3))
            for j in range(8):
                wbm, WBm = load_w(l, 6 + j, 4608)
                gv = wbm[:, 0:3072].rearrange("p (k b n) -> p k b n", k=8, b=3)
                bv = wbm[:, 3072:4608].rearrange("p (k b n) -> p k b n", k=4, b=3)
                for tg in range(2):
                    t0 = tg * 512
                    mi = tg
                    for br in range(3):
                        bP = brot_()
                        S.mm(ps[:, bP, :], [(bv[:, kc, br, :], obr[br][:, kc, t0:t0 + 512]) for kc in range(4)],
                             [WBm] + OBR[br].s(range(4), t0, t0 + 512), [PB[bP]])
                        bG = brot_()
                        S.mm(ps[:, bG, :], [(gv[:, kc, br, :], hT[:, kc, t0:t0 + 512]) for kc in range(8)],
                             [WBm] + HT.s(range(8), t0, t0 + 512), [PB[bG]])
                        gi = grot()
                        S.act(sgt[gi], ps[:, bG, :], AF.Sigmoid, [PB[bG]], [SGT[gi]])
                        if br == 0:
                            S.tt(mac[mi], sgt[gi], ps[:, bP, :], ALU.mult, [SGT[gi], PB[bP]], [MAC[mi]])
                        else:
                            S.tt(mtp[mi], sgt[gi], ps[:, bP, :], ALU.mult, [SGT[gi], PB[bP]], [MTP[mi]])
                            if br == 1:
                                S.tt(mac[mi], mac[mi], mtp[mi], ALU.add, [MTP[mi]], [MAC[mi]])
                            else:
                                S.tt(mg[:, j, t0:t0 + 512], mac[mi], mtp[mi], ALU.add, [MTP[mi], MAC[mi]],
                                     MG.s(j, t0, t0 + 512))
                    if j >= 1:
                        drain("merge", 2)
            wos = []
            for io in range(2):
                wbo, WBo = load_w(l, 14 + io, 4096)
                wos.append((slabview(wbo, 0, 8, 512), WBo))
            for tg in range(2):
                t0 = tg * 512
                for i in range(8):
                    wo, WBo = wos[i // 4]
                    ii = i % 4
                    b = brot_()
                    S.mm(ps[:, b, :], [(wo[:, kc, ii * 128:(ii + 1) * 128], mg[:, kc, t0:t0 + 512]) for kc in range(8)],
                         [WBo] + MG.s(range(8), t0, t0 + 512), [PB[b]])
                    S.stt(xT[:, i, t0:t0 + 512], ps[:, b, :], md[:, 16 + i:17 + i], xT[:, i, t0:t0 + 512],
                          ALU.mult, ALU.add, [PB[b], MOD], XT.s(i, t0, t0 + 512))
                    drain("merge", 2)

            drain("merge", -1)

            S.stt(abT[:, 1, :], md[:, 32:40], 1.0, pp[:, P0 + 8:P0 + 16], ALU.add, ALU.mult, [MOD, CONST], [ABB])
            norm_mod(abT[:, 1, :], md[:, 24:32], brot_, MOD)
            AR.reset()
            aT = AR.alloc([NFF, NT], BF16)
            acc = [AR.alloc([NT], F32) for _ in range(2)]
            gl = [AR.alloc([NT], F32) for _ in range(2)]
            ACT_ = TB("aT", NFF)
            ACC = [Buf("acc%d" % i) for i in range(2)]
            GL = [Buf("gl%d" % i) for i in range(2)]
            CW = P0 + 67
            CBc = P0 + 67 + 66

            def segs(ap2d, lo, hi):
                return ap2d.rearrange("p (s l) -> p s l", s=nseq)[:, :, lo:hi]
            for cp in range(11):
                wbu, WBu = load_w(l, 16 + cp, 4096)
                wa, wv = slabview(wbu, 0, 8, 256), slabview(wbu, 2048, 8, 256)
                for cc in range(2):
                    c = 2 * cp + cc
                    a0, v0 = (0, 4) if c % 2 == 0 else (2, 6)
                    for tg in range(2):
                        t0 = tg * 512
                        if cp == 0 and cc == 0:
                            S.mm_split(ps[:, a0 + tg, :], [(wa[:, kc, cc * 128:(cc + 1) * 128], hT[:, kc, t0:t0 + 512]) for kc in range(8)],
                                       [HT.s(kc, t0, t0 + 512) for kc in range(8)], [WBu], [PB[a0 + tg]])
                        else:
                            S.mm(ps[:, a0 + tg, :], [(wa[:, kc, cc * 128:(cc + 1) * 128], hT[:, kc, t0:t0 + 512]) for kc in range(8)],
                                 [WBu] + HT.s(range(8), t0, t0 + 512), [PB[a0 + tg]])
                        S.mm(ps[:, v0 + tg, :], [(wv[:, kc, cc * 128:(cc + 1) * 128], hT[:, kc, t0:t0 + 512]) for kc in range(8)],
                             [WBu] + HT.s(range(8), t0, t0 + 512), [PB[v0 + tg]])
                    pa = ps[:, a0:a0 + 2, :].rearrange("p b n -> p (b n)")
                    pv_ = ps[:, v0:v0 + 2, :].rearrange("p b n -> p (b n)")
                    PA = [PB[a0], PB[a0 + 1]]
                    PV = [PB[v0], PB[v0 + 1]]
                    i = c % 2
                    S.act(acc[i], pa, AF.Identity, PA + [CONST], [ACC[i]],
                          bias=pp[:, CBc + c:CBc + c + 1], scale=pp[:, CW + 22 + c:CW + 22 + c + 1])
                    S.stt(segs(acc[i], 1, L), segs(pa, 0, L - 1), pp[:, CW + c:CW + c + 1], segs(acc[i], 1, L),
                          ALU.mult, ALU.add, PA + [CONST], [ACC[i]])
                    S.stt(segs(acc[i], 0, L - 1), segs(pa, 1, L), pp[:, CW + 44 + c:CW + 44 + c + 1],
                          segs(acc[i], 0, L - 1), ALU.mult, ALU.add, PA + [CONST], [ACC[i]])
                    S.act(gl[i], acc[i], AF.Gelu_apprx_tanh, [ACC[i]], [GL[i]])
                    S.tt(aT[:, c, :], gl[i], pv_, ALU.mult, [GL[i]] + PV, ACT_.s(c))
            for i in range(8):
                wbd, WBd = load_w(l, 27 + i, 2816)
                wd = slabview(wbd, 0, NFF, 128)
                for tg in range(2):
                    t0 = tg * 512
                    b = brot_()
                    S.mm(ps[:, b, :], [(wd[:, kc, :], aT[:, kc, t0:t0 + 512]) for kc in range(NFF)],
                         [WBd] + ACT_.s(range(NFF), t0, t0 + 512), [PB[b]])
                    S.stt(xT[:, i, t0:t0 + 512], ps[:, b, :], md[:, 40 + i:41 + i], xT[:, i, t0:t0 + 512],
                          ALU.mult, ALU.add, [PB[b], MOD], XT.s(i, t0, t0 + 512))

        for c in range(8):
            S.dma("sp", xT[:, c, :], xin[0, :, c, :], writes=XT.s(c))
        for half in range(2):
            for l in range(2):
                layer_pass(half, l)
            for c in range(8):
                out_toks.append(S.dma("sp", yout[half, :, c, :], xT[:, c, :], reads=XT.s(c)))
                if half == 0:
                    S.dma("sp", xT[:, c, :], xin[1, :, c, :], writes=XT.s(c))
        S.wait_all("sp", out_toks)
        with nc.Block() as block:
            S.run(block)
    return nc


def _consts():
    bf = ml_dtypes.bfloat16
    cb = np.zeros((128, 768), np.float32)
    cb[:, 0:128] = 1.0 / 1024
    cb[:, 128:256] = 1.0 / 128
    for blk in range(2):
        cb[blk * 64:(blk + 1) * 64, 256 + blk * 64:256 + (blk + 1) * 64] = 1.0 / 64
    p = np.arange(128)
    cb[p ^ 16, 384 + p] = 1.0
    cidx = np.arange(128)[:, None] * np.arange(128)[None, :]
    cb[:, 512:640] = np.cos(2 * np.pi * cidx / 128) / np.sqrt(128)
    cb[:, 640:768] = np.sin(2 * np.pi * cidx / 128) / np.sqrt(128)
    d = p % 64
    axis, half, f = d // 32, (d % 32) // 16, d % 16
    inv = (10000.0 ** (-np.arange(16, dtype=np.float32) / 16)).astype(np.float32)
    tok = np.arange(NT)
    row_id, col_id = (tok // 64).astype(np.float32), (tok % 64).astype(np.float32)
    pos = np.where(axis[:, None] == 0, row_id[None, :], col_id[None, :]).astype(np.float32)
    ang = (pos * inv[f][:, None]).astype(np.float32)
    rope = np.zeros((128, 2, NT), np.float32)
    rope[:, 0, :] = np.cos(ang)
    rope[:, 1, :] = np.sin(ang) * np.where(half == 0, -1.0, 1.0)[:, None]

    def dft(Ln):
        n = np.arange(Ln)
        a = 2 * np.pi * ((n[:, None] * n[None, :]) % Ln) / Ln
        return np.cos(a) / np.sqrt(Ln), -np.sin(a) / np.sqrt(Ln)
    c256, s256 = dft(256)
    dft256 = np.zeros((128, 2, 2, 256), np.float32)
    for nt in range(2):
        dft256[:, 0, nt, :] = c256[nt * 128:(nt + 1) * 128, :]
        dft256[:, 1, nt, :] = s256[nt * 128:(nt + 1) * 128, :]
    c1k, s1k = dft(1024)
    dft1024 = np.zeros((4, 128, 8, 512), np.float32)
    for hf in range(2):
        for nt in range(8):
            dft1024[2 * hf, :, nt, :] = c1k[nt * 128:(nt + 1) * 128, hf * 512:(hf + 1) * 512]
            dft1024[2 * hf + 1, :, nt, :] = s1k[nt * 128:(nt + 1) * 128, hf * 512:(hf + 1) * 512]
    dd = (np.arange(TW)[None, :] - TOFF - np.arange(128)[:, None]).astype(np.float32)
    dtab = np.stack([np.maximum(dd, 0), np.maximum(-dd, 0), (dd == 0).astype(np.float32)]).astype(np.float32)
    n1 = np.stack([np.broadcast_to(tok + 1.0, (128, NT)), np.broadcast_to(1024.0 - tok, (128, NT))]).astype(np.float32)
    epp = np.zeros((128, 4), np.float32)
    for i in range(2):
        epp[:, i * 2 + 0] = 255 - (i * 128 + p)
        epp[:, i * 2 + 1] = i * 128 + p
    return dict(cb16=cb.astype(bf), rope=rope, dft256=dft256.astype(bf), dft1024=dft1024.astype(bf),
                dtab=dtab, n1tab=n1, epp=epp)


def _kcv(w, c0, n):
    kc = w.shape[0] // 128
    return w[:, c0:c0 + n].reshape(kc, 128, n).transpose(1, 0, 2)


def _pack_weights(w_ada, w_in, w_bra, w_brr, w_brf, w_out, w_up, w_down):
    wpk = np.zeros((2, NSLAB, 128, SLAB), np.float32)
    wad = np.zeros((2, 12, 128, 4096), np.float32)
    for l in range(2):
        wi = w_in[l]
        put = lambda idx, arr: wpk[l, idx, :, :arr.reshape(128, -1).shape[1]].__setitem__(slice(None), arr.reshape(128, -1))
        put(0, _kcv(wi, 0, 512))
        kvp = np.concatenate([_kcv(wi, 512, 64), _kcv(wi, 512, 64), _kcv(wi, 576, 64), _kcv(wi, 576, 64),
                              _kcv(wi, 640, 128)], axis=2)
        put(1, kvp)
        put(2, _kcv(wi, 768, 512))
        put(3, _kcv(wi, 1280, 512))
        put(4, _kcv(wi, 1792, 512))
        put(5, _kcv(wi, 2304, 512))
        brs = [w_bra[l], w_brr[l], w_brf[l]]
        for j in range(8):
            g = np.stack([_kcv(wi, 2816 + br * 1024 + j * 128, 128) for br in range(3)], axis=2)
            b = np.stack([_kcv(brs[br], j * 128, 128) for br in range(3)], axis=2)
            wpk[l, 6 + j, :, 0:3072] = g.reshape(128, -1)
            wpk[l, 6 + j, :, 3072:4608] = b.reshape(128, -1)
        for io in range(2):
            put(14 + io, _kcv(w_out[l], io * 512, 512))
        for cp in range(11):
            wpk[l, 16 + cp, :, 0:2048] = _kcv(w_up[l], cp * 256, 256).reshape(128, -1)
            wpk[l, 16 + cp, :, 2048:4096] = _kcv(w_up[l], D_FF + cp * 256, 256).reshape(128, -1)
        for i in range(8):
            put(27 + i, _kcv(w_down[l], i * 128, 128))
        for sidx in range(12):
            wad[l, sidx] = _kcv(w_ada[l], sidx * 512, 512).reshape(128, -1)
    return dict(wpk=wpk, wadapk=wad)


def _fm(v):
    return np.ascontiguousarray(v.reshape(-1, 128).T)


_CACHE = {}


def kernel(x_prompt, x_sample, cache_k, cache_v, state_ret_fwd, state_ret_bwd, c, c_ctx,
           w_ada, b_ada, norm1, w_in, q_norm, k_norm, ret_decay_f, ret_decay_b, ret_norm,
           w_br_att, w_br_ret, w_br_four, w_out, norm2, w_up, conv_w, conv_b, w_down):
    f32 = np.float32
    A = lambda a: np.ascontiguousarray(np.asarray(a, dtype=f32))
    x_prompt, x_sample, cache_k, cache_v = A(x_prompt), A(x_sample), A(cache_k), A(cache_v)
    state_ret_fwd, state_ret_bwd, c, c_ctx = A(state_ret_fwd), A(state_ret_bwd), A(c), A(c_ctx)
    if "nc" not in _CACHE:
        _CACHE["nc"] = build_program()
        _CACHE["consts"] = _consts()
    nc = _CACHE["nc"]
    consts = _CACHE["consts"]
    shared = _pack_weights(A(w_ada), A(w_in), A(w_br_att), A(w_br_ret), A(w_br_four), A(w_out), A(w_up), A(w_down))
    shared.update(consts)
    pp = np.zeros((128, 2 * PPL + 24), f32)
    rdf, rdb = A(ret_decay_f), A(ret_decay_b)
    for l in range(2):
        o = l * PPL
        pp[:, o:o + 8] = _fm(A(norm1)[l])
        pp[:, o + 8:o + 16] = _fm(A(norm2)[l])
        pp[:, o + 16:o + 64] = _fm(A(b_ada)[l])
        pp[:, o + 64] = np.tile(A(q_norm)[l], 2)
        pp[:, o + 65] = np.tile(A(k_norm)[l], 2)
        pp[:, o + 66] = A(ret_norm)[l]
        cw = A(conv_w)[l]
        for k in range(3):
            pp[:, o + 67 + k * 22:o + 67 + (k + 1) * 22] = _fm(cw[k])
        pp[:, o + 67 + 66:o + 67 + 88] = _fm(A(conv_b)[l])
        for dr, rd in enumerate((rdf, rdb)):
            for h in range(4):
                pp[:, 2 * PPL + l * 8 + dr * 4 + h] = rd[l, h]
            for j in range(2):
                pp[0:64, 2 * PPL + 16 + l * 4 + dr * 2 + j] = rd[l, 2 * j]
                pp[64:128, 2 * PPL + 16 + l * 4 + dr * 2 + j] = rd[l, 2 * j + 1]
    in_maps = []
    for i in range(8):
        m = dict(shared)
        xp = x_prompt[4 * i:4 * i + 4].reshape(NT, D)
        xs = x_sample[i]
        xin = np.stack([xp.reshape(NT, 8, 128).transpose(2, 1, 0), xs.reshape(NT, 8, 128).transpose(2, 1, 0)])
        m["xin"] = np.ascontiguousarray(xin)
        m["condT"] = np.ascontiguousarray(np.stack([_fm(c_ctx), _fm(c[i])], axis=-1))
        m["pp"] = pp
        ck = cache_k[i]
        ckt = ck.transpose(0, 2, 3, 1)
        ckz = np.zeros((2, 2, 2, 128, 512), f32)
        ckz[:, :, 0, 0:64, :] = ckt
        ckz[:, :, 1, 64:128, :] = ckt
        m["ckT"] = ckz
        m["cv"] = np.ascontiguousarray(cache_v[i].reshape(2, 512, 128))
        s0 = np.stack([state_ret_fwd[i], state_ret_bwd[i]], axis=0)
        m["s0"] = np.ascontiguousarray(s0.transpose(1, 0, 3, 2, 4).reshape(2, 128, 4, 128))
        in_maps.append(m)
    res = run_bass_kernel_spmd(nc, in_maps, core_ids=list(range(8)))
    R = res.results
    y_prompt = np.zeros((32, 256, D), f32)
    y_sample = np.zeros((8, 1024, D), f32)
    nck = np.zeros((32, 2, 256, 2, 64), f32)
    ncv = np.zeros((32, 2, 256, 2, 64), f32)
    nsf = np.zeros((32, 2, 4, 64, 128), f32)
    nsb = np.zeros((32, 2, 4, 64, 128), f32)
    for i in range(8):
        r = R[i]
        yo = np.asarray(r["yout"], f32)
        yt = yo.transpose(0, 3, 2, 1).reshape(2, NT, D)
        y_prompt[4 * i:4 * i + 4] = yt[0].reshape(4, 256, D)
        y_sample[i] = yt[1]
        cko = np.asarray(r["ck_out"], f32).reshape(2, 2, 64, 4, 256)
        nck[4 * i:4 * i + 4] = cko.transpose(3, 0, 4, 1, 2)
        cvo = np.asarray(r["cv_out"], f32).reshape(2, 4, 256, 2, 64)
        ncv[4 * i:4 * i + 4] = cvo.transpose(1, 0, 2, 3, 4)
        nsf[4 * i:4 * i + 4] = np.asarray(r["sf_out"], f32).transpose(1, 0, 2, 3, 4)
        nsb[4 * i:4 * i + 4] = np.asarray(r["sb_out"], f32).transpose(1, 0, 2, 3, 4)
    return (y_prompt, y_sample, nck, ncv, nsf, nsb)
```
